# Optimizing a Trainium2 kernel written in Bass

```python
import jax, jax.numpy as jnp
from jax import lax
import numpy as np

D_MODEL = 1024
BATCH = 8
SEQ = 2048
DEPTH = 4

POOL_WINDOWS = (2, 4, 8, 16)
N_POOL_GROUPS = len(POOL_WINDOWS)
POOL_WIDTH = D_MODEL
POOL_GROUP = POOL_WIDTH // N_POOL_GROUPS
SSD_EXPAND = 2
SSD_INNER = SSD_EXPAND * D_MODEL
SSD_HEAD_DIM = 64
SSD_HEADS = SSD_INNER // SSD_HEAD_DIM
SSD_GROUPS = 4
SSD_HEADS_PER_GROUP = SSD_HEADS // SSD_GROUPS
SSD_STATE = 128
SSD_CONV = 5
SSD_CHUNK = 128
SSD_CONV_CH = SSD_INNER + 2 * SSD_GROUPS * SSD_STATE
N_BRANCHES = 2
D_FF = 2816
FFN_CONV = 3
ALPHA = (2 * DEPTH) ** 0.25
BETA = (8 * DEPTH) ** -0.25
LN_EPS = 1e-5
RMS_EPS = 1e-5
IN_SPLITS = (POOL_WIDTH, POOL_WIDTH + SSD_INNER, POOL_WIDTH + SSD_INNER + SSD_CONV_CH,
             POOL_WIDTH + SSD_INNER + SSD_CONV_CH + 2 * SSD_HEADS)
IN_COLS = IN_SPLITS[-1] + N_BRANCHES * D_MODEL

kernel_name = 'hybrid_pool_ssd_convffn_encoder'


def layer_norm(x, g, b):
    xf = x.astype(jnp.float32)
    mu = jnp.mean(xf, axis=-1, keepdims=True)
    var = jnp.mean(jnp.square(xf - mu), axis=-1, keepdims=True)
    return ((xf - mu) * lax.rsqrt(var + LN_EPS) * g + b).astype(x.dtype)


def dwconv_centred(x, w, b):
    pad = w.shape[0] // 2
    y = lax.conv_general_dilated(x, w[:, None, :], window_strides=(1,), padding=[(pad, pad)],
                                 dimension_numbers=('NWC', 'WIO', 'NWC'),
                                 feature_group_count=x.shape[-1])
    return y + b


def multiscale_pool(u, w_map, scale):
    bsz, s, _ = u.shape
    ug = u.astype(jnp.float32).reshape(bsz, s, N_POOL_GROUPS, POOL_GROUP)
    t = np.arange(s)
    groups = []
    for gi, w in enumerate(POOL_WINDOWS):
        half = w // 2
        v = ug[:, :, gi]
        cs = jnp.cumsum(jnp.pad(v, ((0, 0), (half + 1, half), (0, 0))), axis=1)
        cnt = np.minimum(t + half - 1, s - 1) - np.maximum(t - half, 0) + 1
        mean = (cs[:, w:w + s] - cs[:, :s]) / jnp.asarray(cnt, jnp.float32)[:, None]
        groups.append(mean - v)
    pooled = jnp.stack(groups, axis=2).astype(u.dtype)
    out = jnp.einsum('bsgc,gcd->bsgd', pooled, w_map).reshape(bsz, s, POOL_WIDTH)
    return out * scale


def ssd_scan(xh, dt, a, bmat, cmat):
    f32 = jnp.float32
    bsz, s = xh.shape[0], xh.shape[1]
    nc = s // SSD_CHUNK
    shp = (bsz, nc, SSD_CHUNK, SSD_GROUPS)
    x = (xh.astype(f32) * dt[..., None]).reshape(*shp, SSD_HEADS_PER_GROUP, SSD_HEAD_DIM)
    bc = bmat.astype(f32).reshape(*shp, SSD_STATE)
    cc = cmat.astype(f32).reshape(*shp, SSD_STATE)
    adt = (dt * a).reshape(*shp, SSD_HEADS_PER_GROUP)
    a_cum = jnp.cumsum(jnp.moveaxis(adt, 2, -1), axis=-1)
    seg = a_cum[..., :, None] - a_cum[..., None, :]
    lower = np.tril(np.ones((SSD_CHUNK, SSD_CHUNK), dtype=bool))
    decay = jnp.exp(jnp.where(lower, seg, -jnp.inf))
    scores = jnp.einsum('bclgn,bcsgn->bcgls', cc, bc)
    y_diag = jnp.einsum('bcghls,bcsghp->bclghp', scores[:, :, :, None] * decay, x)

    def step(h, inp):
        b_c, c_c, x_c, ac_c = inp
        y_off = jnp.einsum('blgn,bghpn,bghl->blghp', c_c, h, jnp.exp(ac_c))
        decay_end = jnp.exp(ac_c[..., -1:] - ac_c)
        new_state = jnp.einsum('blgn,bghl,blghp->bghpn', b_c, decay_end, x_c)
        h = h * jnp.exp(ac_c[..., -1])[..., None, None] + new_state
        return h, y_off

    h0 = jnp.zeros((bsz, SSD_GROUPS, SSD_HEADS_PER_GROUP, SSD_HEAD_DIM, SSD_STATE), f32)
    xs = (jnp.moveaxis(bc, 1, 0), jnp.moveaxis(cc, 1, 0), jnp.moveaxis(x, 1, 0),
          jnp.moveaxis(a_cum, 1, 0))
    _, y_off = lax.scan(step, h0, xs)
    y = y_diag + jnp.moveaxis(y_off, 0, 1)
    return y.reshape(bsz, s, SSD_HEADS, SSD_HEAD_DIM)


def bidirectional_ssd(xh, dt_raw, a_log, dt_bias, bmat, cmat, d_skip):
    bsz, s = xh.shape[0], xh.shape[1]
    dt = jax.nn.softplus(dt_raw.astype(jnp.float32).reshape(bsz, s, 2, SSD_HEADS)
                         + dt_bias.astype(jnp.float32))
    a = -jnp.exp(a_log.astype(jnp.float32))
    y_f = ssd_scan(xh, dt[:, :, 0], a[0], bmat, cmat)
    fl = lambda t: jnp.flip(t, axis=1)
    y_b = fl(ssd_scan(fl(xh), fl(dt[:, :, 1]), a[1], fl(bmat), fl(cmat)))
    y = y_f + y_b + d_skip.astype(jnp.float32)[:, None] * xh.astype(jnp.float32)
    return y.astype(xh.dtype)


def gated_rmsnorm(y, z, g):
    bsz, s, _ = y.shape
    v = (y.astype(jnp.float32) * jax.nn.silu(z.astype(jnp.float32))).reshape(bsz, s, SSD_GROUPS, -1)
    v = v * lax.rsqrt(jnp.mean(v * v, axis=-1, keepdims=True) + RMS_EPS)
    return (v.reshape(bsz, s, SSD_INNER) * g).astype(y.dtype)


def mixer_sublayer(x, w_in, pool_w, pool_scale, ssd_conv_w, ssd_conv_b, a_log, dt_bias,
                   d_skip, ssd_norm_g, w_ssd_proj, w_out):
    bsz, s, _ = x.shape
    proj = x @ w_in
    u_pool, z, xbc, dt_raw, gate_logits = jnp.split(proj, list(IN_SPLITS), axis=-1)
    pool_out = multiscale_pool(u_pool, pool_w, pool_scale)
    xbc = jax.nn.silu(dwconv_centred(xbc, ssd_conv_w, ssd_conv_b))
    gn = SSD_GROUPS * SSD_STATE
    xs, bm, cm = jnp.split(xbc, [SSD_INNER, SSD_INNER + gn], axis=-1)
    xh = xs.reshape(bsz, s, SSD_HEADS, SSD_HEAD_DIM)
    bm = bm.reshape(bsz, s, SSD_GROUPS, SSD_STATE)
    cm = cm.reshape(bsz, s, SSD_GROUPS, SSD_STATE)
    y = bidirectional_ssd(xh, dt_raw, a_log, dt_bias, bm, cm, d_skip)
    y = gated_rmsnorm(y.reshape(bsz, s, SSD_INNER), z, ssd_norm_g)
    ssd_out = y @ w_ssd_proj
    gates = jax.nn.sigmoid(gate_logits.astype(jnp.float32)).astype(x.dtype)
    gates = gates.reshape(bsz, s, N_BRANCHES, D_MODEL)
    merged = gates[:, :, 0] * pool_out + gates[:, :, 1] * ssd_out
    return merged @ w_out


def conv_ffn(x, w_up, ffn_conv_w, ffn_conv_b, w_down):
    h = dwconv_centred(x @ w_up, ffn_conv_w, ffn_conv_b)
    gate, val = jnp.split(h, 2, axis=-1)
    return (jax.nn.gelu(gate, approximate=False) * val) @ w_down


def setup_inputs(seed: int = 0) -> dict:
    key = jax.random.key(seed)
    ks = jax.random.split(key, 21)
    nrm = lambda k, shp, sc: jax.random.normal(k, shp, jnp.float32) * sc
    L = DEPTH
    dt0 = jnp.exp(jax.random.uniform(ks[8], (L, 2, SSD_HEADS), jnp.float32,
                                     np.log(1e-3), np.log(1e-1)))
    return {
        'x': nrm(ks[0], (BATCH, SEQ, D_MODEL), 1.0),
        'w_in': nrm(ks[1], (L, D_MODEL, IN_COLS), D_MODEL ** -0.5),
        'pool_w': nrm(ks[2], (L, N_POOL_GROUPS, POOL_GROUP, POOL_GROUP), POOL_GROUP ** -0.5),
        'pool_scale': 1.0 + nrm(ks[3], (L, POOL_WIDTH), 0.02),
        'ssd_conv_w': nrm(ks[4], (L, SSD_CONV, SSD_CONV_CH), SSD_CONV ** -0.5),
        'ssd_conv_b': nrm(ks[5], (L, SSD_CONV_CH), 0.02),
        'a_log': jnp.log(jax.random.uniform(ks[6], (L, 2, SSD_HEADS), jnp.float32, 1.0, 16.0)),
        'dt_bias': dt0 + jnp.log(-jnp.expm1(-dt0)),
        'd_skip': 1.0 + nrm(ks[7], (L, SSD_HEADS), 0.02),
        'ssd_norm_g': 1.0 + nrm(ks[9], (L, SSD_INNER), 0.02),
        'w_ssd_proj': nrm(ks[10], (L, SSD_INNER, D_MODEL), SSD_INNER ** -0.5),
        'w_out': nrm(ks[11], (L, D_MODEL, D_MODEL), BETA * D_MODEL ** -0.5),
        'ln1_g': 1.0 + nrm(ks[12], (L, D_MODEL), 0.02),
        'ln1_b': nrm(ks[13], (L, D_MODEL), 0.02),
        'w_up': nrm(ks[14], (L, D_MODEL, 2 * D_FF), D_MODEL ** -0.5),
        'ffn_conv_w': nrm(ks[15], (L, FFN_CONV, 2 * D_FF), FFN_CONV ** -0.5),
        'ffn_conv_b': nrm(ks[16], (L, 2 * D_FF), 0.02),
        'w_down': nrm(ks[17], (L, D_FF, D_MODEL), BETA * D_FF ** -0.5),
        'ln2_g': 1.0 + nrm(ks[18], (L, D_MODEL), 0.02),
        'ln2_b': nrm(ks[19], (L, D_MODEL), 0.02),
    }


def reference(x, w_in, pool_w, pool_scale, ssd_conv_w, ssd_conv_b, a_log, dt_bias, d_skip,
              ssd_norm_g, w_ssd_proj, w_out, ln1_g, ln1_b, w_up, ffn_conv_w, ffn_conv_b,
              w_down, ln2_g, ln2_b):
    for i in range(DEPTH):
        mix = mixer_sublayer(x, w_in[i], pool_w[i], pool_scale[i], ssd_conv_w[i], ssd_conv_b[i],
                             a_log[i], dt_bias[i], d_skip[i], ssd_norm_g[i], w_ssd_proj[i],
                             w_out[i])
        x = layer_norm(ALPHA * x + mix, ln1_g[i], ln1_b[i])
        ffn = conv_ffn(x, w_up[i], ffn_conv_w[i], ffn_conv_b[i], w_down[i])
        x = layer_norm(ALPHA * x + ffn, ln2_g[i], ln2_b[i])
    return x
```

```python
import numpy as np
import concourse.bass as bass
import concourse.mybir as mybir
from concourse.bass_utils import run_bass_kernel_spmd
from contextlib import ExitStack

F32 = mybir.dt.float32
BF16 = mybir.dt.bfloat16
F32R = mybir.dt.float32r
AF = mybir.ActivationFunctionType
ALU = mybir.AluOpType

D = 1024
KC = 8
S = 2048
NT = 4
DEPTH = 4
NCH = 16
DFF = 2816
FK = 22
U0, Z0, XBC0, DT0, G0 = 0, 1024, 3072, 6144, 6208
ALPHA = (2 * DEPTH) ** 0.25
LN_EPS = 1e-5
RMS_EPS = 1e-5
POOL_WINDOWS = (2, 4, 8, 16)

PP_CW = 0
PP_CB = PP_CW + 120
PP_FW = PP_CB + 24
PP_FB = PP_FW + 132
PP_PS = PP_FB + 44
PP_L1G = PP_PS + 8
PP_L1B = PP_L1G + 8
PP_L2G = PP_L1B + 8
PP_L2B = PP_L2G + 8
NPP = PP_L2B + 8
BC_ALOG = 0
BC_DTB = 64
BC_DSK = 128
BC_NG = 160
NBC = BC_NG + 2048
NBC_SB = BC_NG + 512

SEM_GEN = 30000
PIPELINE_B = True
USE_REGIONS = True
SOFT_BARRIERS = False
SCHED_EPS = 3.0
LAT_TAIL = 0.7
WIN_REGION = 1500
NONCE = ""
DB_A = True
DB_F = True
JK_BF = True
ARENA_WORDS = 32256


class Buf:
    __slots__ = ("name", "w", "r", "dsem", "dcount", "excl")

    def __init__(self, name, excl=False):
        self.name = name
        self.w = None
        self.r = []
        self.dsem = None
        self.dcount = 0
        self.excl = excl


class Eng:
    def __init__(self, name):
        self.name = name
        self.count = 0
        self.pending = False
        self.ops = []
        self.waited = {}

    def semkey(self, cnt):
        return ("E", self.name, (cnt - 1) // SEM_GEN)


class _Rec:
    def __init__(self):
        self.call = None

    def __getattr__(self, name):
        def f(*a, **k):
            self.call = (name, a, k)
            return None
        return f


class Sched:
    def __init__(self, nc, dry=False):
        self.nc = nc
        self.dry = dry
        self.eng = {n: Eng(n) for n in ("tensor", "vector", "scalar", "gpsimd", "sync")}
        self.semkeys = {}
        self.nbuf = 0
        self.dma_out = []
        self.fence_toks = []
        self.fence_dma = {}
        self.dsem_count = {}
        self.region = None
        self._pend = {}

    def buf(self, name=None, excl=False):
        self.nbuf += 1
        b = Buf(name or f"b{self.nbuf}", excl)
        b.r = list(self.fence_toks)
        return b

    def bufs(self, name, n, excl=False):
        return [self.buf(f"{name}{i}", excl) for i in range(n)]

    @staticmethod
    def _tok_local(tok):
        key, val = tok
        if key[0] == "E":
            return key, val - key[2] * SEM_GEN
        return key, val

    def _need(self, e, toks):
        best = {}
        for t in toks:
            if t is None:
                continue
            if t[0][0] == "E" and t[0][1] == e.name and t[1] > e.count:
                continue
            key, val = self._tok_local(t)
            if e.waited.get(key, 0) >= val:
                continue
            if best.get(key, 0) < val:
                best[key] = val
        for k, v in best.items():
            e.waited[k] = v
            self.semkeys[k] = None
        return list(best.items())

    def op(self, engname, fn, reads=(), writes=(), inc=True):
        if self.dry:
            return None
        rec = _Rec()
        fn(rec)
        if self.region is not None:
            pend = self._pend.setdefault(engname, [])
            pend.append((rec.call, list(reads), list(writes)))
            if inc:
                self.region.append(("op", engname, pend))
                self._pend[engname] = []
            return None
        return self._op_core(engname, rec.call, reads, writes, inc)

    def _op_core(self, engname, call, reads, writes, inc):
        e = self.eng[engname]
        xr = [b for b in reads if b.excl]
        if xr:
            reads = [b for b in reads if not b.excl]
            writes = list(writes) + [b for b in xr if b not in writes]
        deps = []
        for b in reads:
            deps.append(b.w)
        for b in writes:
            deps.append(b.w)
            deps.extend(b.r)
        waits = self._need(e, deps)
        if inc:
            e.count += 1
            e.pending = False
            tokval = e.count
        else:
            e.pending = True
            tokval = e.count + 1
        key = e.semkey(tokval)
        tok = (key, tokval)
        self.semkeys[key] = None
        for b in reads:
            b.r.append(tok)
        for b in writes:
            b.w = tok
            b.r = []
        e.ops.append((waits, call, key if inc else None, None))
        return tok

    def dma(self, engname, out_ap, in_ap, reads=(), writes=(), sembuf=None, track=True):
        if self.dry:
            return None
        if self.region is not None:
            self.region.append(("dma", engname, (out_ap, in_ap, list(reads), list(writes), sembuf, track)))
            return None
        return self._dma_core(engname, out_ap, in_ap, reads, writes, sembuf, track)

    def _dma_core(self, engname, out_ap, in_ap, reads, writes, sembuf, track):
        e = self.eng[engname]
        sb = sembuf or writes[0]
        if sb.dsem is None:
            sb.dsem = ("D", sb.name)
        deps = []
        for b in reads:
            deps.append(b.w)
        for b in writes:
            deps.append(b.w)
            deps.extend(b.r)
        waits = self._need(e, deps)
        self.dsem_count[sb.dsem] = self.dsem_count.get(sb.dsem, 0) + 1
        tok = (sb.dsem, 16 * self.dsem_count[sb.dsem])
        self.semkeys[sb.dsem] = None
        for b in reads:
            b.r.append(tok)
        for b in writes:
            b.w = tok
            b.r = []

        e.ops.append((waits, ("dma_start", (), {"out": out_ap, "in_": in_ap}), None, sb.dsem))
        if track:
            self.dma_out.append(tok)
        return tok

    def begin_region(self):
        if self.dry or not USE_REGIONS:
            return
        assert self.region is None
        self.region = []
        self._pend = {}

    @staticmethod
    def _free(ap):
        n = 1
        for d in ap.shape[1:]:
            n *= d
        return n

    def _est(self, rec):
        kind, engname, body = rec
        if kind == "dma":
            return 0.15
        t = 0.0
        for call, _, _ in body:
            name, a, k = call
            out = a[0] if a else k.get("out")
            try:
                n = self._free(out)
            except Exception:
                n = 512
            if engname == "tensor":
                if name == "matmul":
                    d = max(n, 64) / 2400.0 + 0.03
                    if a[1].dtype == F32:
                        d *= 4.0
                    t += d
                else:
                    t += 0.1
            elif engname == "vector":
                t += (n + 150) / 960.0
            elif engname == "scalar":
                t += (n + 224) / 1200.0
            elif engname == "gpsimd":
                t += (2 * n + 150) / 960.0
            else:
                t += 0.1
        return t

    def end_region(self):
        if self.dry or not USE_REGIONS:
            return
        recs = self.region
        self.region = None
        for en, p in self._pend.items():
            assert not p, f"pending un-inc'd ops on {en} at region end"
        n = len(recs)
        last_w = {}
        readers = {}
        deps = [set() for _ in range(n)]
        for i, (kind, engname, body) in enumerate(recs):
            if kind == "dma":
                rr, ww = body[2], body[3]
            else:
                rr = [b for c in body for b in c[1]]
                ww = [b for c in body for b in c[2]]
            R = [b for b in rr if not b.excl]
            Wr = list(ww) + [b for b in rr if b.excl]
            for b in R:
                if id(b) in last_w:
                    deps[i].add(last_w[id(b)])
            for b in Wr:
                if id(b) in last_w:
                    deps[i].add(last_w[id(b)])
                deps[i].update(readers.get(id(b), ()))
            for b in R:
                readers.setdefault(id(b), []).append(i)
            for b in Wr:
                last_w[id(b)] = i
                readers[id(b)] = []
            deps[i].discard(i)
        dur = [self._est(r) for r in recs]
        succ = [[] for _ in range(n)]
        indeg = [0] * n
        for i in range(n):
            for d in deps[i]:
                succ[d].append(i)
            indeg[i] = len(deps[i])
        tail = [0.0] * n
        for i in range(n - 1, -1, -1):
            m = 0.0
            for j in succ[i]:
                if tail[j] > m:
                    m = tail[j]
            tail[i] = dur[i] + (m + LAT_TAIL if succ[i] else 0.0)
        ready_t = [0.0] * n
        ready = [i for i in range(n) if indeg[i] == 0]
        eng_free = {}
        order = []
        LAT = 1.2
        WIN = 48
        done = [False] * n
        lo = 0
        while ready:
            while lo < n and done[lo]:
                lo += 1
            cands = []
            emin = None
            for i in ready:
                if i > lo + WIN_REGION:
                    continue
                est = max(eng_free.get(recs[i][1], 0.0), ready_t[i])
                cands.append((est, i))
                if emin is None or est < emin:
                    emin = est
            if not cands:
                i = min(ready)
                est = max(eng_free.get(recs[i][1], 0.0), ready_t[i])
            else:
                best = None
                for est_i, i_ in cands:
                    if est_i <= emin + SCHED_EPS:
                        key = (-tail[i_], est_i, i_)
                        if best is None or key < best:
                            best = key
                i = best[2]
                est = best[1]
            ready.remove(i)
            done[i] = True
            fin = est + dur[i]
            eng_free[recs[i][1]] = fin
            if recs[i][0] == "dma":
                fin += 2.0
            order.append(i)
            for j in succ[i]:
                indeg[j] -= 1
                ready_t[j] = max(ready_t[j], fin + LAT)
                if indeg[j] == 0:
                    ready.append(j)
        assert len(order) == n
        for i in order:
            kind, engname, body = recs[i]
            if kind == "dma":
                self._dma_core(engname, *body)
            else:
                for k, (call, rr, ww) in enumerate(body):
                    self._op_core(engname, call, rr, ww, k == len(body) - 1)

    def soft_barrier(self):
        if self.dry:
            return
        if not SOFT_BARRIERS:
            return self.barrier()
        assert self.region is None
        for t in self.dma_out:
            if self.fence_dma.get(t[0], 0) < t[1]:
                self.fence_dma[t[0]] = t[1]
        self.dma_out = []
        toks = list(self.fence_dma.items())
        for e in self.eng.values():
            if e.pending:
                raise RuntimeError(f"engine {e.name} pending at soft barrier")
            if e.count > 0:
                toks.append((e.semkey(e.count), e.count))
        self.fence_toks = toks

    def barrier(self):
        if self.dry:
            return
        assert self.region is None
        toks = list(self.dma_out)
        self.dma_out = []
        for e in self.eng.values():
            if e.pending:
                raise RuntimeError(f"engine {e.name} pending at barrier")
            if e.count > 0:
                toks.append((e.semkey(e.count), e.count))
        for e in self.eng.values():
            waits = self._need(e, toks)
            if waits:
                e.ops.append((waits, None, None, None))

    def emit(self, stack):
        nc = self.nc
        sems = {}
        print(f"[sched] {len(self.semkeys)} semaphores", flush=True)
        for i, k in enumerate(self.semkeys):
            sems[k] = stack.enter_context(nc.semaphore(f"s{i}"))
        for e in self.eng.values():
            if e.pending:
                raise RuntimeError(f"engine {e.name} ends pending")
        block = stack.enter_context(nc.Block())

        def runner(e):
            def body(eng):
                for waits, fn, inckey, dsem in e.ops:
                    for k, v in waits:
                        eng.wait_ge(sems[k], v)
                    if fn is None:
                        continue
                    ins = getattr(eng, fn[0])(*fn[1], **fn[2])
                    if inckey is not None:
                        ins.then_inc(sems[inckey], 1)
                    elif dsem is not None:
                        ins.then_inc(sems[dsem], 16)
            return body

        block.tensor(runner(self.eng["tensor"]))
        block.vector(runner(self.eng["vector"]))
        block.scalar(runner(self.eng["scalar"]))
        block.gpsimd(runner(self.eng["gpsimd"]))
        block.sync(runner(self.eng["sync"]))


class T:
    __slots__ = ("ap", "b")

    def __init__(self, ap, b):
        self.ap = ap
        self.b = b


class Arena:
    def __init__(self, S_, arena_ap):
        self.S = S_
        self.arena = arena_ap
        self.off = 0

    def reset(self, off=0):
        self.off = off

    def f32(self, name, n):
        a = self.arena[:, self.off:self.off + n]
        self.off += n
        assert self.off <= ARENA_WORDS, (name, self.off)
        return T(a, self.S.buf(name))

    def bf(self, name, n):
        w = (n + 1) // 2
        a = self.arena[:, self.off:self.off + w].bitcast(BF16)
        self.off += w
        assert self.off <= ARENA_WORDS, (name, self.off)
        return T(a, self.S.buf(name))


DEBUG_TAPS = {}
DEBUG_ON = set()


def build_program(depth=DEPTH):
    nc = bass.Bass("TRN2", target_bir_lowering=False)
    dr = {}
    dr["xT"] = nc.dram_tensor("xT", [KC, 128, S], F32, kind="ExternalInput").ap()
    dr["w_in"] = nc.dram_tensor("w_in", [depth, D, 8256], F32, kind="ExternalInput").ap()
    dr["pool_w"] = nc.dram_tensor("pool_w", [depth, 4, 256, 256], F32, kind="ExternalInput").ap()
    dr["w_ssd_proj"] = nc.dram_tensor("w_ssd_proj", [depth, 2048, D], F32, kind="ExternalInput").ap()
    dr["w_out"] = nc.dram_tensor("w_out", [depth, D, D], F32, kind="ExternalInput").ap()
    dr["w_up"] = nc.dram_tensor("w_up", [depth, D, 2 * DFF], F32, kind="ExternalInput").ap()
    dr["w_down"] = nc.dram_tensor("w_down", [depth, DFF, D], F32, kind="ExternalInput").ap()
    dr["pp"] = nc.dram_tensor("pp", [128, depth * NPP], F32, kind="ExternalInput").ap()
    dr["bc"] = nc.dram_tensor("bc", [depth, NBC], F32, kind="ExternalInput").ap()
    dr["cmask"] = nc.dram_tensor("cmask", [128, 5 * 128], F32, kind="ExternalInput").ap()
    dr["cid"] = nc.dram_tensor("cid", [128, 128], F32, kind="ExternalInput").ap()
    dr["ctm"] = nc.dram_tensor("ctm", [128, 16 * 128], F32, kind="ExternalInput").ap()
    dr["crc"] = nc.dram_tensor("crc", [1, 64], F32, kind="ExternalInput").ap()
    dr["cid8"] = nc.dram_tensor("cid8", [128, 1024], F32, kind="ExternalInput").ap()
    dr["out"] = nc.dram_tensor("out", [KC, 128, S], F32, kind="ExternalOutput").ap()
    dr["xres"] = nc.dram_tensor("xres", [KC, 128, S], F32, kind="Internal").ap()
    dr["yn"] = nc.dram_tensor("yn", [16, 128, S], BF16, kind="Internal").ap()
    dr["wob"] = nc.dram_tensor("wob", [D, D], BF16, kind="Internal").ap()
    dr["wdb"] = nc.dram_tensor("wdb", [DFF, D], BF16, kind="Internal").ap()

    with ExitStack() as st:
        plan = []
        _emit_all(nc, st, dr, depth, dry=True, plan=plan, alloc=None)
        alloc = {}
        S_ = _emit_all(nc, st, dr, depth, dry=False, plan=plan, alloc=alloc)
        S_.emit(st)
    return nc


class WMgr:
    def __init__(self, S_, slots, plan, dry):
        self.S = S_
        self.slots = slots
        self.plan = plan
        self.dry = dry
        self.i = 0
        self.issued = 0

    def _issue(self, idx):
        spec = self.plan[idx]
        slot = self.slots[idx % len(self.slots)]
        for item in spec:
            o0, shape3, src = item[0], item[1], item[2]
            eng = item[3] if len(item) > 3 else "gpsimd"
            n = shape3[0] * shape3[1]
            dst = slot.ap[:, o0:o0 + n].rearrange("p (a b) -> p a b", a=shape3[0])
            self.S.dma(eng, dst, src, writes=[slot.b], track=False)

    def load(self, spec):
        if self.dry:
            self.plan.append(spec)
            self.i += 1
            return self.slots[(self.i - 1) % len(self.slots)]
        idx = self.i
        self.i += 1
        while self.issued < min(idx + 2, len(self.plan)):
            self._issue(self.issued)
            self.issued += 1
        return self.slots[idx % len(self.slots)]


def _emit_all(nc, st, dr, depth, dry, plan, alloc):
    S_ = Sched(nc, dry=dry)
    if dry:
        class _Fake:
            def __getitem__(self, k):
                return self

            def rearrange(self, *a, **k):
                return self

            def bitcast(self, *a):
                return self

            def unsqueeze(self, *a):
                return self

            def to_broadcast(self, *a):
                return self
        fake = _Fake()

        def sbt(name, shape, dt):
            return fake

        def pst(name, shape, dt):
            return fake
    else:
        def sbt(name, shape, dt):
            return st.enter_context(nc.sbuf_tensor(name, shape, dt))

        def pst(name, shape, dt):
            return st.enter_context(nc.psum_tensor(name, shape, dt))

    op = S_.op
    dma = S_.dma

    def tap(name, ap, bufs, shape, dt):
        if name not in DEBUG_ON or dry:
            return
        d_ = nc.dram_tensor("dbg_" + name, list(shape), dt, kind="ExternalOutput").ap()
        DEBUG_TAPS[name] = d_
        dma("sync", d_, ap, reads=list(bufs), writes=[S_.buf("dbg_" + name)])

    CM = sbt("CM" + NONCE, [128, 5 * 128], F32)
    bCM = S_.buf("CM")
    LE, GE, GT_, LT_, ONES = (CM[:, i * 128:(i + 1) * 128] for i in range(5))
    IDF = sbt("IDF", [128, 128], F32)
    bIDF = S_.buf("IDF")
    IDB = sbt("IDB", [128, 128], BF16)
    bIDB = S_.buf("IDB")
    RB = sbt("RB", [128, 2048], F32R)
    bRB = S_.buf("RB")
    RB3 = RB[:, :].rearrange("p (h l) -> p h l", h=16)
    GLR = sbt("GLR", [128, 256], F32R)
    bGLR = S_.buf("GLR")
    RCN = sbt("RCN", [128, 64], F32)
    bRCN = S_.buf("RCN")
    PP = sbt("PP", [128, depth * NPP], F32)
    bPP = S_.buf("PP")
    BC = sbt("BC", [128, NBC_SB], F32)
    bNG = S_.buf("BCng")
    bBC = S_.buf("BC")
    XB = sbt("XB", [128, KC * S], BF16)
    XB3 = XB[:, :].rearrange("p (k t) -> p k t", k=KC)
    bXB = S_.bufs("XB", NT)
    slots = [T(sbt(f"WS{i}", [128, 4096], BF16)[:, :], S_.buf(f"WS{i}")) for i in range(3)]
    ARENA = sbt("ARENA", [128, ARENA_WORDS], F32)
    AR = Arena(S_, ARENA)
    PS = pst("PS", [128, 8 * 512], F32)
    bPS = S_.bufs("PSB", 8, excl=True)
    W = WMgr(S_, slots, plan, dry)

    def bank(i, n=1):
        return PS[:, i * 512:(i + n) * 512]

    dma("sync", CM[:, :], dr["cmask"], writes=[bCM])
    dma("sync", IDF[:, :], dr["cid"], writes=[bIDF])
    dma("gpsimd", IDB[:, :], dr["cid"], writes=[bIDB])
    dma("sync", RCN[:, :], dr["crc"].to_broadcast([128, 64]), writes=[bRCN])
    dma("sync", PP[:, :], dr["pp"], writes=[bPP])
    dma("gpsimd", GLR[:, :], dr["cmask"][:, 256:512], writes=[bGLR])
    for nt in range(NT):
        dma("gpsimd", XB3[:, :, nt * 512:(nt + 1) * 512],
            dr["xT"].rearrange("k p t -> p k t")[:, :, nt * 512:(nt + 1) * 512], writes=[bXB[nt]])

    def pcol(l, off):
        return PP[:, l * NPP + off: l * NPP + off + 1]

    def win_src(l, c0, ncols):
        return dr["w_in"][l].rearrange("(k p) n -> p k n", p=128)[:, :, c0:c0 + ncols]

    def proj_fm(slot, so, ncols_in_slot, cidx, psbanks, pbufs):
        w3 = slot.ap[:, so:so + KC * ncols_in_slot].rearrange("p (k n) -> p k n", k=KC)
        for nt in range(NT):
            for kc in range(KC):
                last = (nt == NT - 1 and kc == KC - 1)
                op("tensor",
                   lambda e, nt=nt, kc=kc: e.matmul(bank(psbanks + nt), w3[:, kc, cidx * 128:(cidx + 1) * 128],
                                                    XB3[:, kc, nt * 512:(nt + 1) * 512],
                                                    start=(kc == 0), stop=(kc == KC - 1)),
                   reads=[slot.b] + bXB, writes=pbufs, inc=last)

    xres_src = dr["xT"]
    xres_bufs = [S_.bufs(f"xT{i}_", 2) for i in range(NT)]
    bXRES = [S_.bufs(f"xres{i}_", 2) for i in range(NT)]
    bYN = S_.bufs("yn", 2)
    bWOB = S_.buf("wob")
    bWDB = S_.buf("wdb")
    bOUT = S_.bufs("out", 2)

    for l in range(depth):
        S_.soft_barrier()
        AR.reset()
        SCR1 = AR.f32("SCR1", 2052)
        PAD2 = AR.f32("PAD2", 2052)
        SCR2 = AR.f32("SCR2", 2048)
        SCR3 = AR.bf("SCR3", 2048)
        MB2 = AR.bf("MB2", 2048)
        XTOK = AR.bf("XTOK", 16 * 512)
        bXTOK = S_.bufs("XTOKc", 16)
        XTOK3 = XTOK.ap.rearrange("p (c n) -> p c n", c=16)
        BTOK = AR.bf("BTOK", 16 * 128)
        bBTOK = S_.bufs("BTOKc", 16)
        BTOK3 = BTOK.ap.rearrange("p (c n) -> p c n", c=16)
        BT = AR.bf("BT", 2048)
        CT = AR.bf("CT", 2048)
        HPREV = AR.bf("HPREV", 16 * 512)
        bHPREV = S_.bufs("HPREVc", 16)
        HPREV3 = HPREV.ap.rearrange("p (c n) -> p c n", c=16)
        XDTFs = [AR.bf(f"XDTF{i}", 512) for i in range(2)]
        XDTBs = [AR.bf(f"XDTB{i}", 512) for i in range(2)]
        XDD = AR.bf("XDD", 512)
        HS = AR.f32("HS", 512)
        HBb = AR.bf("HBb", 512)
        DTt = AR.f32("DT", 1024)
        ADT = AR.f32("ADT", 1024)
        ACUM = T(SCR2.ap[:, 0:1024], SCR2.b)
        EXPA = AR.f32("EXPA", 1024)
        DTDE = AR.f32("DTDE", 1024)
        DECC = AR.f32("DECC", 1024)
        EA = AR.f32("EA", 64)
        SFB = AR.f32("SFB", 256)
        acc2_off = AR.off
        T1 = AR.f32("T1", 512)
        T2 = AR.f32("T2", 512)
        YT = AR.f32("YT", 512)
        SZ = AR.f32("SZ", 512)
        ACC2ap = ARENA[:, acc2_off:acc2_off + 2048]
        ACC2b = [T1.b, T2.b, YT.b, SZ.b]
        VV = AR.f32("V", 512)
        V2 = AR.f32("V2", 512)
        JK = AR.bf("JK", 512) if JK_BF else AR.f32("JK", 512)
        SS = AR.f32("SS", 4)
        VN = AR.bf("VN", 512)
        YNT = [AR.bf(f"YNT{i}", 512) for i in range(2)]
        ID8 = AR.bf("ID8", 1024)
        DIg = AR.bf("DIg", 1024)
        DI3 = DIg.ap.rearrange("p (h l) -> p h l", h=8)

        def v3(ap, c):
            return ap[:, c * 64:(c + 1) * 64]

        def hd_bc(t, c, h0):
            return t.ap[:, c * 64 + h0: c * 64 + h0 + 8].unsqueeze(2).to_broadcast([128, 8, 64])

        dma("sync", BC[:, 0:BC_NG], dr["bc"][l:l + 1, 0:BC_NG].to_broadcast([128, BC_NG]), writes=[bBC])
        dma("gpsimd", ID8.ap, dr["cid8"], writes=[ID8.b])
        wdt = W.load([(0, (KC, 64), win_src(l, DT0, 64))])
        wdt3 = wdt.ap[:, 0:KC * 64].rearrange("p (k n) -> p k n", k=KC)
        DTP = PS[:, 4 * 512: 4 * 512 + 1024]
        ACP = PS[:, 6 * 512: 6 * 512 + 1024]
        ATP = PS[:, 2 * 512: 2 * 512 + 1024]
        for i in range(NCH):
            for kc in range(KC):
                op("tensor", lambda e, i=i, kc=kc: e.matmul(DTP[:, i * 64:(i + 1) * 64],
                                                          XB3[:, kc, i * 128:(i + 1) * 128], wdt3[:, kc, :],
                                                          start=(kc == 0), stop=(kc == KC - 1)),
                   reads=[wdt.b] + bXB, writes=[bPS[4], bPS[5]], inc=(kc == KC - 1 and i == NCH - 1))
        op("vector", lambda e: e.tensor_tensor(DTt.ap.rearrange("p (c h) -> p c h", c=16),
                                               DTP.rearrange("p (c h) -> p c h", c=16),
                                               BC[:, BC_DTB:BC_DTB + 64].unsqueeze(1).to_broadcast([128, 16, 64]),
                                               ALU.add),
           reads=[bPS[4], bPS[5], bBC], writes=[DTt.b])
        op("scalar", lambda e: e.activation(DTt.ap, DTt.ap, AF.Exp), reads=[DTt.b], writes=[DTt.b])
        op("scalar", lambda e: e.activation(DTt.ap, DTt.ap, AF.Ln, bias=1.0), reads=[DTt.b], writes=[DTt.b])
        op("scalar", lambda e: e.activation(EA.ap, BC[:, BC_ALOG:BC_ALOG + 64], AF.Exp), reads=[bBC], writes=[EA.b])
        op("vector", lambda e: e.scalar_tensor_tensor(ADT.ap.rearrange("p (c h) -> p c h", c=16),
                                                      DTt.ap.rearrange("p (c h) -> p c h", c=16), -1.0,
                                                      EA.ap.unsqueeze(1).to_broadcast([128, 16, 64]),
                                                      ALU.mult, ALU.mult),
           reads=[DTt.b, EA.b], writes=[ADT.b])
        for c in range(NCH):
            op("tensor", lambda e, c=c: e.matmul(ACP[:, c * 64: c * 64 + 32], LE, ADT.ap[:, c * 64: c * 64 + 32],
                                                 start=True, stop=True),
               reads=[ADT.b, bCM], writes=[bPS[6], bPS[7]], inc=False)
            op("tensor", lambda e, c=c: e.matmul(ACP[:, c * 64 + 32: c * 64 + 64], GE,
                                                 ADT.ap[:, c * 64 + 32: c * 64 + 64], start=True, stop=True),
               reads=[ADT.b, bCM], writes=[bPS[6], bPS[7]], inc=False)
            op("tensor", lambda e, c=c: e.matmul(ATP[:, c * 64: c * 64 + 64], ONES, ADT.ap[:, c * 64: c * 64 + 64],
                                                 start=True, stop=True),
               reads=[ADT.b, bCM], writes=[bPS[2], bPS[3]], inc=(c == NCH - 1))
        op("scalar", lambda e: e.copy(ACUM.ap, ACP), reads=[bPS[6], bPS[7]], writes=[ACUM.b])
        op("scalar", lambda e: e.activation(EXPA.ap, ACUM.ap, AF.Exp), reads=[ACUM.b], writes=[EXPA.b])
        op("scalar", lambda e: e.activation(DECC.ap, ATP, AF.Exp), reads=[bPS[2], bPS[3]], writes=[DECC.b])
        op("vector", lambda e: e.tensor_tensor(DTDE.ap, ATP, ACUM.ap, ALU.subtract),
           reads=[bPS[2], bPS[3], ACUM.b], writes=[DTDE.b])
        op("scalar", lambda e: e.activation(DTDE.ap, DTDE.ap, AF.Exp), reads=[DTDE.b], writes=[DTDE.b])
        op("vector", lambda e: e.tensor_tensor(DTDE.ap, DTDE.ap, DTt.ap, ALU.mult),
           reads=[DTDE.b, DTt.b], writes=[DTDE.b])
        op("gpsimd", lambda e: e.memset(SCR1.ap[:, 0:2], 0.0), writes=[SCR1.b])
        op("gpsimd", lambda e: e.memset(SCR1.ap[:, 2050:2052], 0.0), writes=[SCR1.b])
        op("gpsimd", lambda e: e.memset(PAD2.ap[:, 0:2], 0.0), writes=[PAD2.b])
        op("gpsimd", lambda e: e.memset(PAD2.ap[:, 2050:2052], 0.0), writes=[PAD2.b])

        for g in range(4):
            S_.begin_region()
            wx = W.load([(0, (KC, 512), win_src(l, XBC0 + 512 * g, 512))])
            wbc = W.load([(0, (KC, 128), win_src(l, XBC0 + 2048 + 128 * g, 128)),
                          (KC * 128, (KC, 128), win_src(l, XBC0 + 2560 + 128 * g, 128))])
            if l == 0 and g == 0:
                tap("wx", wx.ap, [wx.b], [128, 4096], BF16)
            if g > 0:
                op("gpsimd", lambda e: e.memset(PAD2.ap[:, 0:2], 0.0), writes=[PAD2.b])
            tpi = 0

            def projA(cc):
                if cc < 4:
                    proj_fm(wx, 0, 512, cc, 0, bPS[0:4])
                elif cc == 4:
                    proj_fm(wbc, 0, 128, 0, 0, bPS[0:4])
                else:
                    proj_fm(wbc, KC * 128, 128, 0, 0, bPS[0:4])

            projA(0)
            for cc in range(6):
                cch = (4 * g + cc) if cc < 4 else ((16 + g) if cc == 4 else (20 + g))
                pad = SCR1 if (cc % 2 == 0 or not DB_A) else PAD2
                acc_ap = SCR2.ap if (cc % 2 == 0 or not DB_A) else ACC2ap
                acc_b = [SCR2.b] if (cc % 2 == 0 or not DB_A) else ACC2b
                op("scalar", lambda e, pad=pad: e.copy(pad.ap[:, 2:2050], bank(0, 4)), reads=bPS[0:4], writes=[pad.b])
                if cc < 5:
                    projA(cc + 1)
                op("scalar", lambda e, cch=cch, pad=pad, acc_ap=acc_ap: e.activation(
                    acc_ap, pad.ap[:, 0:2048], AF.Identity, bias=pcol(l, PP_CB + cch), scale=pcol(l, PP_CW + cch * 5)),
                   reads=[pad.b, bPP], writes=acc_b)
                for k in range(1, 5):
                    op("vector", lambda e, k=k, cch=cch, pad=pad, acc_ap=acc_ap: e.scalar_tensor_tensor(
                        acc_ap, pad.ap[:, k:k + 2048], pcol(l, PP_CW + cch * 5 + k), acc_ap, ALU.mult, ALU.add),
                       reads=[pad.b, bPP] + acc_b, writes=acc_b)
                dst = SCR3 if cc < 4 else (BT if cc == 4 else CT)
                op("scalar", lambda e, dst=dst, acc_ap=acc_ap: e.activation(dst.ap, acc_ap, AF.Silu),
                   reads=acc_b, writes=[dst.b])
                if l == 0 and g == 0 and cc == 0:
                    tap("xs0", SCR3.ap, [SCR3.b], [128, 2048], BF16)
                    tap("pad0", SCR1.ap, [SCR1.b], [128, 2052], F32)
                if cc < 5:
                    for q4 in range(4):
                        pb = 4 + (tpi % 2)
                        tpi += 1
                        TPv = bank(pb)[:, 0:256].bitcast(BF16)
                        for q in range(4):
                            i = q4 * 4 + q
                            op("tensor", lambda e, dst=dst, i=i, q=q, TPv=TPv: e.transpose(
                                TPv[:, q * 128:(q + 1) * 128], dst.ap[:, i * 128:(i + 1) * 128], IDB[:, :]),
                               reads=[dst.b, bIDB], writes=[bPS[pb]], inc=(q == 3))
                        if cc < 4:
                            op("scalar", lambda e, q4=q4, cc=cc, TPv=TPv: e.copy(
                                XTOK3[:, q4 * 4:q4 * 4 + 4, cc * 128:(cc + 1) * 128],
                                TPv.rearrange("p (a b) -> p a b", a=4)),
                               reads=[bPS[pb]], writes=bXTOK[q4 * 4:q4 * 4 + 4])
                        else:
                            op("scalar", lambda e, q4=q4, TPv=TPv: e.copy(
                                BTOK3[:, q4 * 4:q4 * 4 + 4, :], TPv.rearrange("p (a b) -> p a b", a=4)),
                               reads=[bPS[pb]], writes=bBTOK[q4 * 4:q4 * 4 + 4])

            wz = W.load([(0, (KC, 512), win_src(l, Z0 + 512 * g, 512))])
            wz3 = wz.ap[:, 0:KC * 512].rearrange("p (k n) -> p k n", k=KC)
            ko0, ko1 = 2 * g, 2 * g + 2
            dma("gpsimd", dr["wob"].rearrange("(k p) n -> p k n", p=128)[:, ko0:ko1, :],
                dr["w_out"][l].rearrange("(k p) n -> p k n", p=128)[:, ko0:ko1, :], writes=[bWOB])
            kd0, kd1 = 6 * g, min(6 * g + 6, FK)
            dma("gpsimd", dr["wdb"].rearrange("(k p) n -> p k n", p=128)[:, kd0:kd1, :],
                dr["w_down"][l].rearrange("(k p) n -> p k n", p=128)[:, kd0:kd1, :], writes=[bWDB])
            hf, hb = 8 * g, 32 + 8 * g
            R3 = SCR1.ap[:, 0:2048].rearrange("p (h l) -> p h l", h=16)
            Es = [SCR2, T(PAD2.ap[:, 0:2048], PAD2.b)]

            def state_update(c, h0):
                op("gpsimd", lambda e: e.tensor_tensor(XDD.ap.rearrange("p (h d) -> p h d", h=8),
                                                       XTOK3[:, c, :].rearrange("p (h d) -> p h d", h=8),
                                                       hd_bc(DTDE, c, h0), ALU.mult),
                   reads=[bXTOK[c], DTDE.b], writes=[XDD.b])
                op("tensor", lambda e: e.matmul(bank(5), BTOK3[:, c, :], XDD.ap, start=True, stop=True),
                   reads=[bBTOK[c], XDD.b], writes=[bPS[5]])
                op("vector", lambda e: e.tensor_tensor(HS.ap.rearrange("p (h d) -> p h d", h=8),
                                                       HS.ap.rearrange("p (h d) -> p h d", h=8),
                                                       hd_bc(DECC, c, h0), ALU.mult),
                   reads=[HS.b, DECC.b], writes=[HS.b])
                op("vector", lambda e: e.tensor_tensor(HS.ap, HS.ap, bank(5), ALU.add),
                   reads=[HS.b, bPS[5]], writes=[HS.b])

            dma("sync", BC[:, BC_NG:BC_NG + 512],
                dr["bc"][l:l + 1, BC_NG + 512 * g:BC_NG + 512 * (g + 1)].to_broadcast([128, 512]), writes=[bNG])
            op("gpsimd", lambda e: e.tensor_tensor(
                DI3, ID8.ap.rearrange("p (h l) -> p h l", h=8),
                BC[:, BC_DSK + hf: BC_DSK + hf + 8].unsqueeze(2).to_broadcast([128, 8, 128]), ALU.mult),
               reads=[ID8.b, bBC], writes=[DIg.b])
            op("gpsimd", lambda e: e.memset(HS.ap, 0.0), writes=[HS.b])
            for c in range(NCH):
                op("scalar", lambda e, c=c: e.copy(HPREV3[:, c, :], HS.ap), reads=[HS.b], writes=[bHPREV[c]])
                if c < NCH - 1:
                    state_update(c, hf)
            MBs = [SCR3, MB2]
            bSC = bPS[7]
            bTPY = bPS[7]

            def stage1(c, par):
                Mt = MBs[par]
                M3 = Mt.ap.rearrange("p (h l) -> p h l", h=16)
                xf, xb_ = XDTFs[par], XDTBs[par]
                E = Es[par]
                SC = bank(7)[:, 0:128]
                op("tensor", lambda e: e.matmul(SC, BT.ap[:, c * 128:(c + 1) * 128], CT.ap[:, c * 128:(c + 1) * 128],
                                                start=True, stop=True),
                   reads=[BT.b, CT.b], writes=[bSC])
                op("vector", lambda e: e.tensor_tensor(SFB.ap.rearrange("p (a l) -> p a l", a=2),
                                                       SC.unsqueeze(1).to_broadcast([128, 2, 128]),
                                                       CM[:, 0:256].rearrange("p (a l) -> p a l", a=2), ALU.mult),
                   reads=[bSC, bCM], writes=[SFB.b])
                op("gpsimd", lambda e: e.tensor_tensor(
                    RB3[:, 0:8, :], LE.unsqueeze(1).to_broadcast([128, 8, 128]),
                    ADT.ap[:, c * 64 + hf: c * 64 + hf + 8].unsqueeze(2).to_broadcast([128, 8, 128]), ALU.mult),
                   reads=[bCM, ADT.b], writes=[bRB])
                op("gpsimd", lambda e: e.tensor_tensor(
                    RB3[:, 8:16, :], GE.unsqueeze(1).to_broadcast([128, 8, 128]),
                    ADT.ap[:, c * 64 + hb: c * 64 + hb + 8].unsqueeze(2).to_broadcast([128, 8, 128]), ALU.mult),
                   reads=[bCM, ADT.b], writes=[bRB])
                op("gpsimd", lambda e: e.tensor_tensor(xf.ap.rearrange("p (h d) -> p h d", h=8),
                                                       XTOK3[:, c, :].rearrange("p (h d) -> p h d", h=8),
                                                       hd_bc(DTt, c, hf), ALU.mult),
                   reads=[bXTOK[c], DTt.b], writes=[xf.b])
                op("gpsimd", lambda e: e.tensor_tensor(xb_.ap.rearrange("p (h d) -> p h d", h=8),
                                                       XTOK3[:, c, :].rearrange("p (h d) -> p h d", h=8),
                                                       hd_bc(DTt, c, hb), ALU.mult),
                   reads=[bXTOK[c], DTt.b], writes=[xb_.b])
                for d_ in range(2):
                    lhs = GLR[:, 0:128] if d_ == 0 else GLR[:, 128:256]
                    for q in range(2):
                        op("tensor", lambda e, d_=d_, q=q, lhs=lhs: e.matmul(
                            bank(q), lhs, RB[:, d_ * 1024 + q * 512: d_ * 1024 + (q + 1) * 512],
                            start=True, stop=True),
                           reads=[bRB, bGLR], writes=[bPS[q]], inc=(q == 1))
                    op("scalar", lambda e, d_=d_: e.activation(E.ap[:, d_ * 1024:(d_ + 1) * 1024], bank(0, 2), AF.Exp),
                       reads=bPS[0:2], writes=[E.b])
                    op("vector", lambda e, d_=d_: e.tensor_tensor(
                        M3[:, d_ * 8:(d_ + 1) * 8, :],
                        E.ap[:, d_ * 1024:(d_ + 1) * 1024].rearrange("p (h l) -> p h l", h=8),
                        SFB.ap[:, d_ * 128:(d_ + 1) * 128].unsqueeze(1).to_broadcast([128, 8, 128]), ALU.mult),
                       reads=[E.b, SFB.b], writes=[Mt.b])

            def stage2(c, par, ci):
                Mt = MBs[par]
                M3 = Mt.ap.rearrange("p (h l) -> p h l", h=16)
                xf, xb_ = XDTFs[par], XDTBs[par]
                op("scalar", lambda e: e.copy(HBb.ap, HS.ap), reads=[HS.b], writes=[HBb.b])
                if c > 0:
                    state_update(c, hb)
                for h in range(8):
                    op("tensor", lambda e, h=h: e.matmul(bank(2)[:, h * 64:(h + 1) * 64], M3[:, h, :],
                                                         xf.ap[:, h * 64:(h + 1) * 64], start=True, stop=False),
                       reads=[Mt.b, xf.b], writes=[bPS[2]], inc=False)
                    op("tensor", lambda e, h=h: e.matmul(bank(2)[:, h * 64:(h + 1) * 64], M3[:, 8 + h, :],
                                                         xb_.ap[:, h * 64:(h + 1) * 64], start=False, stop=False),
                       reads=[Mt.b, xb_.b], writes=[bPS[2]], inc=False)
                    op("tensor", lambda e, h=h: e.matmul(bank(2)[:, h * 64:(h + 1) * 64], DI3[:, h, :],
                                                         XTOK3[:, c, h * 64:(h + 1) * 64], start=False, stop=True),
                       reads=[DIg.b, bXTOK[c]], writes=[bPS[2]], inc=(h == 7))
                op("tensor", lambda e: e.matmul(bank(3), CT.ap[:, c * 128:(c + 1) * 128], HPREV3[:, c, :],
                                                start=True, stop=True),
                   reads=[CT.b, bHPREV[c]], writes=[bPS[3]])
                op("tensor", lambda e: e.matmul(bank(4), CT.ap[:, c * 128:(c + 1) * 128], HBb.ap,
                                                start=True, stop=True),
                   reads=[CT.b, HBb.b], writes=[bPS[4]])
                for kc in range(KC):
                    op("tensor", lambda e, kc=kc: e.matmul(bank(6), XB3[:, kc, c * 128:(c + 1) * 128], wz3[:, kc, :],
                                                           start=(kc == 0), stop=(kc == KC - 1)),
                       reads=[wz.b] + bXB, writes=[bPS[6]], inc=(kc == KC - 1))
                op("vector", lambda e: e.tensor_tensor(T1.ap.rearrange("p (h d) -> p h d", h=8),
                                                       bank(3).rearrange("p (h d) -> p h d", h=8),
                                                       hd_bc(EXPA, c, hf), ALU.mult),
                   reads=[bPS[3], EXPA.b], writes=[T1.b])
                op("vector", lambda e: e.tensor_tensor(T2.ap.rearrange("p (h d) -> p h d", h=8),
                                                       bank(4).rearrange("p (h d) -> p h d", h=8),
                                                       hd_bc(EXPA, c, hb), ALU.mult),
                   reads=[bPS[4], EXPA.b], writes=[T2.b])
                op("gpsimd", lambda e: e.tensor_tensor(T1.ap, T1.ap, T2.ap, ALU.add), reads=[T1.b, T2.b], writes=[T1.b])
                op("vector", lambda e: e.tensor_tensor(YT.ap, T1.ap, bank(2), ALU.add), reads=[T1.b, bPS[2]], writes=[YT.b])
                op("scalar", lambda e: e.activation(SZ.ap, bank(6), AF.Silu), reads=[bPS[6]], writes=[SZ.b])
                op("vector", lambda e: e.tensor_tensor(VV.ap, YT.ap, SZ.ap, ALU.mult), reads=[YT.b, SZ.b], writes=[VV.b])
                op("gpsimd", lambda e: e.memset(SS.ap[:, 0:1], 0.0), writes=[SS.b])
                op("scalar", lambda e: e.activation(JK.ap, VV.ap, AF.Square, accum_out=SS.ap[:, 0:1]),
                   reads=[VV.b], writes=[JK.b, SS.b])
                op("scalar", lambda e: e.activation(SS.ap[:, 1:2], SS.ap[:, 0:1], AF.Ln, bias=RMS_EPS, scale=1.0 / 512),
                   reads=[SS.b], writes=[SS.b])
                op("scalar", lambda e: e.activation(SS.ap[:, 2:3], SS.ap[:, 1:2], AF.Exp, scale=-0.5),
                   reads=[SS.b], writes=[SS.b])
                op("scalar", lambda e: e.activation(V2.ap, VV.ap, AF.Copy, scale=SS.ap[:, 2:3]),
                   reads=[VV.b, SS.b], writes=[V2.b])
                op("vector", lambda e: e.tensor_tensor(VN.ap, V2.ap, BC[:, BC_NG: BC_NG + 512], ALU.mult),
                   reads=[V2.b, bNG], writes=[VN.b])
                TPY = bank(7)[:, 256:512].bitcast(BF16)
                for q in range(4):
                    op("tensor", lambda e, q=q: e.transpose(TPY[:, q * 128:(q + 1) * 128], VN.ap[:, q * 128:(q + 1) * 128],
                                                            IDB[:, :]),
                       reads=[VN.b, bIDB], writes=[bTPY], inc=(q == 3))
                ynt = YNT[ci % 2]
                op("scalar", lambda e: e.copy(ynt.ap, TPY), reads=[bTPY], writes=[ynt.b])
                dma("sync", dr["yn"][4 * g:4 * g + 4, :, c * 128:(c + 1) * 128].rearrange("q p t -> p q t"),
                    ynt.ap.rearrange("p (q t) -> p q t", q=4), reads=[ynt.b], writes=[bYN[ci % 2]], sembuf=ynt.b)

            op("gpsimd", lambda e: e.memset(HS.ap, 0.0), writes=[HS.b])
            if PIPELINE_B:
                stage1(NCH - 1, 0)
            for ci, c in enumerate(range(NCH - 1, -1, -1)):
                if PIPELINE_B:
                    if c > 0:
                        stage1(c - 1, (ci + 1) % 2)
                else:
                    stage1(c, ci % 2)
                stage2(c, ci % 2, ci)
            S_.end_region()

        S_.soft_barrier()
        AR.reset()
        MRG = AR.bf("MRG", KC * S)
        MRG3 = MRG.ap.rearrange("p (k t) -> p k t", k=KC)
        bMRG = S_.bufs("MRGj", KC)
        mrg_end = AR.off
        YN = AR.bf("YN", 16 * S)
        YN3 = YN.ap.rearrange("p (k t) -> p k t", k=16)
        G = AR.f32("G", 2048)
        S_.begin_region()
        bYNl = S_.bufs("YNl", 4)
        for k4 in range(4):
            dma("sync", YN3[:, 4 * k4:4 * k4 + 4, :], dr["yn"][4 * k4:4 * k4 + 4].rearrange("k p t -> p k t"),
                reads=bYN, writes=[bYNl[k4]])
        for j in range(KC):
            wp = W.load([(0, (16, 128), dr["w_ssd_proj"][l].rearrange("(k p) n -> p k n", p=128)[:, :, j * 128:(j + 1) * 128])])
            wp3 = wp.ap[:, 0:2048].rearrange("p (k n) -> p k n", k=16)
            wg = W.load([(0, (KC, 128), win_src(l, G0 + 1024 + j * 128, 128))])
            for nt in range(NT):
                for kc in range(16):
                    op("tensor", lambda e, nt=nt, kc=kc: e.matmul(bank(nt), wp3[:, kc, :], YN3[:, kc, nt * 512:(nt + 1) * 512],
                                                                  start=(kc == 0), stop=(kc == 15)),
                       reads=[wp.b, bYNl[kc // 4]], writes=bPS[0:4], inc=(nt == NT - 1 and kc == 15))
            proj_fm(wg, 0, 128, 0, 4, bPS[4:8])
            op("scalar", lambda e: e.activation(G.ap, bank(4, 4), AF.Sigmoid), reads=bPS[4:8], writes=[G.b])
            op("vector", lambda e, j=j: e.tensor_tensor(MRG3[:, j, :], G.ap, bank(0, 4), ALU.mult),
               reads=[G.b] + bPS[0:4], writes=[bMRG[j]])
        S_.end_region()

        if l == 0:
            tap("mrgC", MRG.ap, bMRG, [128, KC * S], BF16)
            tap("yn", YN.ap, bYNl, [128, 16 * S], BF16)
        S_.soft_barrier()
        AR.reset(mrg_end)
        P0 = AR.f32("P0", 2064)
        Q1 = AR.f32("Q1", 2064)
        Q2 = AR.f32("Q2", 2064)
        PLD = [AR.bf(f"PLD{i}", 2048) for i in range(2)]
        G = AR.f32("G", 2048)
        TMP = AR.f32("TMP", 2048)
        TE = AR.f32("TE", 16)
        S_.begin_region()
        op("gpsimd", lambda e: e.memset(P0.ap[:, 0:8], 0.0), writes=[P0.b])
        op("gpsimd", lambda e: e.memset(P0.ap[:, 2056:2064], 0.0), writes=[P0.b])
        for gi, w_ in enumerate(POOL_WINDOWS):
            half = w_ // 2
            wu = W.load([(0, (KC, 256), win_src(l, U0 + 256 * gi, 256))])
            for k2 in range(2):
                pb0 = 4 * (k2 % 2)
                proj_fm(wu, 0, 256, k2, pb0, bPS[pb0:pb0 + 4])
                op("scalar", lambda e, pb0=pb0: e.copy(P0.ap[:, 8:2056], bank(pb0, 4)), reads=bPS[pb0:pb0 + 4], writes=[P0.b])
                src, dst = P0, Q1
                sh = 1
                while sh < w_:
                    op("vector", lambda e, src=src, dst=dst, sh=sh: e.tensor_tensor(
                        dst.ap[:, sh:2064], src.ap[:, sh:2064], src.ap[:, 0:2064 - sh], ALU.add),
                       reads=[src.b], writes=[dst.b])
                    src = dst
                    dst = Q2 if dst is Q1 else Q1
                    sh *= 2
                o = 8 + half - 1
                pl = PLD[k2]
                op("vector", lambda e, src=src, o=o, pl=pl, w_=w_: e.scalar_tensor_tensor(
                    pl.ap, src.ap[:, o:o + 2048], 1.0 / w_, P0.ap[:, 8:2056], ALU.mult, ALU.subtract),
                   reads=[src.b, P0.b], writes=[pl.b])
                nl, nr = half, half - 1
                op("vector", lambda e, src=src, o=o, gi=gi, nl=nl: e.tensor_tensor(
                    TE.ap[:, 0:nl], src.ap[:, o:o + nl], RCN[:, gi * 16: gi * 16 + nl], ALU.mult),
                   reads=[src.b, bRCN], writes=[TE.b])
                op("vector", lambda e, pl=pl, nl=nl: e.tensor_tensor(pl.ap[:, 0:nl], TE.ap[:, 0:nl], P0.ap[:, 8:8 + nl],
                                                                      ALU.subtract),
                   reads=[TE.b, P0.b], writes=[pl.b])
                if nr > 0:
                    op("vector", lambda e, src=src, o=o, gi=gi, nr=nr: e.tensor_tensor(
                        TE.ap[:, 8:8 + nr], src.ap[:, o + 2048 - nr:o + 2048],
                        RCN[:, gi * 16 + 8: gi * 16 + 8 + nr], ALU.mult),
                       reads=[src.b, bRCN], writes=[TE.b])
                    op("vector", lambda e, pl=pl, nr=nr: e.tensor_tensor(
                        pl.ap[:, 2048 - nr:2048], TE.ap[:, 8:8 + nr], P0.ap[:, 8 + 2048 - nr:8 + 2048], ALU.subtract),
                       reads=[TE.b, P0.b], writes=[pl.b])
            wm = W.load([(0, (2, 256), dr["pool_w"][l, gi].rearrange("(k p) n -> p k n", p=128))])
            wm3 = wm.ap[:, 0:512].rearrange("p (k n) -> p k n", k=2)
            for jj in range(2):
                j = 2 * gi + jj
                for nt in range(NT):
                    for k2 in range(2):
                        op("tensor", lambda e, nt=nt, k2=k2, jj=jj: e.matmul(
                            bank(nt), wm3[:, k2, jj * 128:(jj + 1) * 128], PLD[k2].ap[:, nt * 512:(nt + 1) * 512],
                            start=(k2 == 0), stop=(k2 == 1)),
                           reads=[wm.b, PLD[0].b, PLD[1].b], writes=bPS[0:4], inc=(nt == NT - 1 and k2 == 1))
                wg = W.load([(0, (KC, 128), win_src(l, G0 + j * 128, 128))])
                proj_fm(wg, 0, 128, 0, 4, bPS[4:8])
                op("scalar", lambda e: e.activation(G.ap, bank(4, 4), AF.Sigmoid), reads=bPS[4:8], writes=[G.b])
                op("vector", lambda e, j=j: e.scalar_tensor_tensor(TMP.ap, bank(0, 4), pcol(l, PP_PS + j), G.ap,
                                                                   ALU.mult, ALU.mult),
                   reads=bPS[0:4] + [G.b, bPP], writes=[TMP.b])
                op("gpsimd", lambda e, j=j: e.tensor_tensor(MRG3[:, j, :], TMP.ap, MRG3[:, j, :], ALU.add),
                   reads=[TMP.b, bMRG[j]], writes=[bMRG[j]])
        S_.end_region()

        def outproj_ln(rhs3, rhs_bufs, nK, wsrc, goff, boff, final):
            nonlocal xres_src, xres_bufs
            SUMt = AR.f32("SUMt", KC * 512)
            SUM3 = SUMt.ap.rearrange("p (k t) -> p k t", k=KC)
            XR = [AR.f32(f"XR{i}", 512) for i in range(2)]
            SQ = [AR.f32(f"SQ{i}", 512) for i in range(2)]
            MEAN = AR.f32("MEAN", 512)
            M2 = AR.f32("M2", 512)
            RSTD = AR.f32("RSTD", 512)
            TA = [AR.f32(f"TA{i}", 512) for i in range(2)]
            XN = [AR.f32(f"XN{i}", 512) for i in range(2)]
            while len(XR) < 8 and ARENA_WORDS - AR.off >= 512:
                XR.append(AR.f32(f"XR{len(XR)}", 512))
            nxr = len(XR)
            kgroups = [(k0, min(4, nK - k0)) for k0 in range(0, nK, 4)]
            out_dst = dr["out"] if final else dr["xres"]
            S_.begin_region()
            for nt in range(NT):
                for (k0, nk) in kgroups:
                    ws = W.load([(0, (nk, 1024), wsrc.rearrange("(k p) n -> p k n", p=128)[:, k0:k0 + nk, :], "sync")])
                    ws3 = ws.ap[:, 0:nk * 1024].rearrange("p (k n) -> p k n", k=nk)
                    for j in range(KC):
                        for kk in range(nk):
                            kc = k0 + kk
                            op("tensor", lambda e, j=j, kk=kk, kc=kc, ws3=ws3: e.matmul(
                                bank(j), ws3[:, kk, j * 128:(j + 1) * 128], rhs3[:, kc, nt * 512:(nt + 1) * 512],
                                start=(kc == 0), stop=(kc == nK - 1)),
                               reads=[ws.b] + rhs_bufs, writes=[bPS[j]], inc=(kk == nk - 1))
                for j in range(KC):
                    xr = XR[(nt * KC + j) % nxr]
                    dma("sync", xr.ap, xres_src[j, :, nt * 512:(nt + 1) * 512], reads=xres_bufs[nt], writes=[xr.b])
                    op("vector", lambda e, j=j, xr=xr: e.scalar_tensor_tensor(SUM3[:, j, :], xr.ap, float(ALPHA), bank(j),
                                                                              ALU.mult, ALU.add),
                       reads=[xr.b, bPS[j]], writes=[SUMt.b])
                for j in range(KC):
                    op("tensor", lambda e, j=j: e.matmul(bank(0), ONES, SUM3[:, j, :], start=(j == 0), stop=(j == KC - 1)),
                       reads=[SUMt.b, bCM], writes=[bPS[0]], inc=(j == KC - 1))
                for j in range(KC):
                    sq = SQ[j % 2]
                    op("scalar", lambda e, j=j, sq=sq: e.activation(sq.ap, SUM3[:, j, :], AF.Square),
                       reads=[SUMt.b], writes=[sq.b])
                    op("tensor", lambda e, j=j, sq=sq: e.matmul(bank(1), ONES, sq.ap, start=(j == 0), stop=(j == KC - 1)),
                       reads=[sq.b, bCM], writes=[bPS[1]])
                op("vector", lambda e: e.tensor_scalar(MEAN.ap, bank(0), 1.0 / D, None, ALU.mult),
                   reads=[bPS[0]], writes=[MEAN.b])
                op("vector", lambda e: e.tensor_tensor(M2.ap, MEAN.ap, MEAN.ap, ALU.mult), reads=[MEAN.b], writes=[M2.b])
                op("vector", lambda e: e.scalar_tensor_tensor(RSTD.ap, bank(1), 1.0 / D, M2.ap, ALU.mult, ALU.subtract),
                   reads=[bPS[1], M2.b], writes=[RSTD.b])
                op("scalar", lambda e: e.activation(RSTD.ap, RSTD.ap, AF.Ln, bias=LN_EPS), reads=[RSTD.b], writes=[RSTD.b])
                op("scalar", lambda e: e.activation(RSTD.ap, RSTD.ap, AF.Exp, scale=-0.5), reads=[RSTD.b], writes=[RSTD.b])
                for j in range(KC):
                    ta = TA[j % 2]
                    xn = XN[j % 2]
                    op("vector", lambda e, j=j, ta=ta: e.tensor_tensor(ta.ap, SUM3[:, j, :], MEAN.ap, ALU.subtract),
                       reads=[SUMt.b, MEAN.b], writes=[ta.b])
                    op("vector", lambda e, ta=ta: e.tensor_tensor(ta.ap, ta.ap, RSTD.ap, ALU.mult),
                       reads=[ta.b, RSTD.b], writes=[ta.b])
                    op("scalar", lambda e, j=j, ta=ta, xn=xn: e.activation(xn.ap, ta.ap, AF.Identity,
                                                                           bias=pcol(l, boff + j), scale=pcol(l, goff + j)),
                       reads=[ta.b, bPP], writes=[xn.b])
                    if not final:
                        op("scalar", lambda e, j=j, ta=ta: e.activation(XB3[:, j, nt * 512:(nt + 1) * 512], ta.ap, AF.Identity,
                                                                        bias=pcol(l, boff + j), scale=pcol(l, goff + j)),
                           reads=[ta.b, bPP], writes=[bXB[nt]])
                    dma("sync", out_dst[j, :, nt * 512:(nt + 1) * 512], xn.ap, reads=[xn.b],
                        writes=[(bOUT if final else bXRES[nt])[j % 2]], sembuf=xn.b)
            S_.end_region()
            if not final:
                xres_src = dr["xres"]
                xres_bufs = bXRES

        if l == 0:
            tap("mrgD", MRG.ap, bMRG, [128, KC * S], BF16)
        S_.soft_barrier()
        AR.reset(mrg_end)
        outproj_ln(MRG3, bMRG, KC, dr["wob"], PP_L1G, PP_L1B, final=False)
        if l == 0:
            tap("xb1", XB[:, :], bXB, [128, KC * S], BF16)

        S_.soft_barrier()
        AR.reset()
        HB = AR.bf("HB", FK * S)
        HB3 = HB.ap.rearrange("p (k t) -> p k t", k=FK)
        bHB = S_.bufs("HBk", FK)
        hb_end = AR.off
        PADFs = [AR.f32(f"PADF{i}", 2050) for i in range(2)]
        ACCF = AR.f32("ACCF", 2048)
        GTt = AR.f32("GT", 2048)
        for PADF in PADFs:
            op("gpsimd", lambda e: e.memset(PADF.ap[:, 0:1], 0.0), writes=[PADF.b])
            op("gpsimd", lambda e: e.memset(PADF.ap[:, 2049:2050], 0.0), writes=[PADF.b])
        S_.begin_region()
        for q in range(6):
            ncol = 512 if q < 5 else 256
            wgs = W.load([(0, (KC, ncol), dr["w_up"][l].rearrange("(k p) n -> p k n", p=128)[:, :, 512 * q:512 * q + ncol])])
            wvs = W.load([(0, (KC, ncol), dr["w_up"][l].rearrange("(k p) n -> p k n", p=128)[:, :, DFF + 512 * q:DFF + 512 * q + ncol])])
            for kk in range(ncol // 128):
                k = 4 * q + kk
                for half_, wsl in enumerate((wgs, wvs)):
                    pb0 = 4 * half_
                    PADF = PADFs[half_ if DB_F else 0]
                    cch = k + FK * half_
                    proj_fm(wsl, 0, ncol, kk, pb0, bPS[pb0:pb0 + 4])
                    op("scalar", lambda e, pb0=pb0: e.copy(PADF.ap[:, 1:2049], bank(pb0, 4)),
                       reads=bPS[pb0:pb0 + 4], writes=[PADF.b])
                    op("scalar", lambda e, cch=cch: e.activation(ACCF.ap, PADF.ap[:, 0:2048], AF.Identity,
                                                                 bias=pcol(l, PP_FB + cch), scale=pcol(l, PP_FW + cch * 3)),
                       reads=[PADF.b, bPP], writes=[ACCF.b])
                    for t_ in range(1, 3):
                        op("vector", lambda e, t_=t_, cch=cch: e.scalar_tensor_tensor(
                            ACCF.ap, PADF.ap[:, t_:t_ + 2048], pcol(l, PP_FW + cch * 3 + t_), ACCF.ap, ALU.mult, ALU.add),
                           reads=[PADF.b, ACCF.b, bPP], writes=[ACCF.b])
                    if half_ == 0:
                        op("scalar", lambda e: e.activation(GTt.ap, ACCF.ap, AF.Gelu), reads=[ACCF.b], writes=[GTt.b])
                    else:
                        op("vector", lambda e, k=k: e.tensor_tensor(HB3[:, k, :], GTt.ap, ACCF.ap, ALU.mult),
                           reads=[GTt.b, ACCF.b], writes=[bHB[k]])
        S_.end_region()

        if l == 0:
            tap("hb", HB.ap, bHB, [128, FK * S], BF16)
        S_.soft_barrier()
        AR.reset(hb_end)
        outproj_ln(HB3, bHB, FK, dr["wdb"], PP_L2G, PP_L2B, final=(l == depth - 1))

    S_.barrier()
    return S_


def _consts():
    r = np.arange(128)[:, None]
    c = np.arange(128)[None, :]
    le = (r <= c).astype(np.float32)
    ge = (r >= c).astype(np.float32)
    gt = (r > c).astype(np.float32)
    lt = (r < c).astype(np.float32)
    ones = np.ones((128, 128), np.float32)
    cmask = np.concatenate([le, ge, gt, lt, ones], axis=1)
    cid = np.eye(128, dtype=np.float32)
    ctm = np.concatenate([le] * 8 + [ge] * 8, axis=1)
    crc = np.zeros((1, 64), np.float32)
    t = np.arange(S)
    for gi, w in enumerate(POOL_WINDOWS):
        half = w // 2
        cnt = np.minimum(t + half - 1, S - 1) - np.maximum(t - half, 0) + 1
        rc = (1.0 / cnt).astype(np.float32)
        crc[0, gi * 16: gi * 16 + half] = rc[:half]
        if half > 1:
            crc[0, gi * 16 + 8: gi * 16 + 8 + half - 1] = rc[S - (half - 1):]
    return cmask, cid, ctm, crc


def _pack_params(inp, depth):
    pp = np.zeros((128, depth, NPP), np.float32)
    bc = np.zeros((depth, NBC), np.float32)
    for l in range(depth):
        pp[:, l, PP_CW:PP_CW + 120] = inp["ssd_conv_w"][l].T.reshape(24, 128, 5).transpose(1, 0, 2).reshape(128, 120)
        pp[:, l, PP_CB:PP_CB + 24] = inp["ssd_conv_b"][l].reshape(24, 128).T
        pp[:, l, PP_FW:PP_FW + 132] = inp["ffn_conv_w"][l].T.reshape(44, 128, 3).transpose(1, 0, 2).reshape(128, 132)
        pp[:, l, PP_FB:PP_FB + 44] = inp["ffn_conv_b"][l].reshape(44, 128).T
        pp[:, l, PP_PS:PP_PS + 8] = inp["pool_scale"][l].reshape(8, 128).T
        pp[:, l, PP_L1G:PP_L1G + 8] = inp["ln1_g"][l].reshape(8, 128).T
        pp[:, l, PP_L1B:PP_L1B + 8] = inp["ln1_b"][l].reshape(8, 128).T
        pp[:, l, PP_L2G:PP_L2G + 8] = inp["ln2_g"][l].reshape(8, 128).T
        pp[:, l, PP_L2B:PP_L2B + 8] = inp["ln2_b"][l].reshape(8, 128).T
        bc[l, BC_ALOG:BC_ALOG + 64] = inp["a_log"][l].reshape(64)
        bc[l, BC_DTB:BC_DTB + 64] = inp["dt_bias"][l].reshape(64)
        bc[l, BC_DSK:BC_DSK + 32] = inp["d_skip"][l]
        bc[l, BC_NG:BC_NG + 2048] = inp["ssd_norm_g"][l]
    return np.ascontiguousarray(pp.reshape(128, depth * NPP)), bc


def run(inputs, depth=DEPTH, n_cores=8, trace=False):
    inp = {k: np.asarray(v, dtype=np.float32) for k, v in inputs.items()}
    x = inp["x"]
    nb = x.shape[0]
    cmask, cid, ctm, crc = _consts()
    pp, bc = _pack_params(inp, depth)
    shared = {
        "w_in": np.ascontiguousarray(inp["w_in"][:depth]),
        "pool_w": np.ascontiguousarray(inp["pool_w"][:depth]),
        "w_ssd_proj": np.ascontiguousarray(inp["w_ssd_proj"][:depth]),
        "w_out": np.ascontiguousarray(inp["w_out"][:depth]),
        "w_up": np.ascontiguousarray(inp["w_up"][:depth]),
        "w_down": np.ascontiguousarray(inp["w_down"][:depth]),
        "pp": pp, "bc": bc, "cmask": cmask, "cid": cid, "ctm": ctm, "crc": crc,
        "cid8": np.ascontiguousarray(np.tile(cid, (1, 8))),
    }
    in_maps = []
    for b in range(nb):
        m = dict(shared)
        m["xT"] = np.ascontiguousarray(x[b].T).reshape(KC, 128, S)
        in_maps.append(m)
    nc = build_program(depth)
    res = run_bass_kernel_spmd(nc, in_maps, core_ids=list(range(nb)), trace=trace)
    global LAST_DBG
    LAST_DBG = {k: np.asarray(res.results[0]["dbg_" + k]) for k in DEBUG_TAPS}
    outs = [np.asarray(r["out"]).reshape(D, S).T for r in res.results]
    return np.ascontiguousarray(np.stack(outs, axis=0).astype(np.float32)), res


def kernel(**inputs):
    out, _ = run(inputs, depth=DEPTH)
    return out
```

```python
import numpy as np
import concourse.bass as bass
import concourse.mybir as mybir
from concourse.bass_utils import run_bass_kernel_spmd
from contextlib import ExitStack

F32 = mybir.dt.float32
BF16 = mybir.dt.bfloat16
F32R = mybir.dt.float32r
AF = mybir.ActivationFunctionType
ALU = mybir.AluOpType

D = 1024
KC = 8
S = 2048
NT = 4
DEPTH = 4
NCH = 16
DFF = 2816
FK = 22
U0, Z0, XBC0, DT0, G0 = 0, 1024, 3072, 6144, 6208
ALPHA = (2 * DEPTH) ** 0.25
LN_EPS = 1e-5
RMS_EPS = 1e-5
POOL_WINDOWS = (2, 4, 8, 16)

PP_CW = 0
PP_CB = PP_CW + 120
PP_FW = PP_CB + 24
PP_FB = PP_FW + 132
PP_PS = PP_FB + 44
PP_L1G = PP_PS + 8
PP_L1B = PP_L1G + 8
PP_L2G = PP_L1B + 8
PP_L2B = PP_L2G + 8
NPP = PP_L2B + 8
BC_ALOG = 0
BC_DTB = 64
BC_DSK = 128
BC_NG = 160
NBC = BC_NG + 2048
NBC_SB = BC_NG + 512

SEM_GEN = 30000
PIPELINE_B = True
USE_REGIONS = True
SOFT_BARRIERS = False
SCHED_EPS = 3.0
LAT_TAIL = 0.7
WIN_REGION = 1500
NONCE = ""
DB_A = True
DB_F = True
JK_BF = True
ARENA_WORDS = 32256


class Buf:
    __slots__ = ("name", "w", "r", "dsem", "dcount", "excl")

    def __init__(self, name, excl=False):
        self.name = name
        self.w = None
        self.r = []
        self.dsem = None
        self.dcount = 0
        self.excl = excl


class Eng:
    def __init__(self, name):
        self.name = name
        self.count = 0
        self.pending = False
        self.ops = []
        self.waited = {}

    def semkey(self, cnt):
        return ("E", self.name, (cnt - 1) // SEM_GEN)


class _Rec:
    def __init__(self):
        self.call = None

    def __getattr__(self, name):
        def f(*a, **k):
            self.call = (name, a, k)
            return None
        return f


class Sched:
    def __init__(self, nc, dry=False):
        self.nc = nc
        self.dry = dry
        self.eng = {n: Eng(n) for n in ("tensor", "vector", "scalar", "gpsimd", "sync")}
        self.semkeys = {}
        self.nbuf = 0
        self.dma_out = []
        self.fence_toks = []
        self.fence_dma = {}
        self.dsem_count = {}
        self.region = None
        self._pend = {}

    def buf(self, name=None, excl=False):
        self.nbuf += 1
        b = Buf(name or f"b{self.nbuf}", excl)
        b.r = list(self.fence_toks)
        return b

    def bufs(self, name, n, excl=False):
        return [self.buf(f"{name}{i}", excl) for i in range(n)]

    @staticmethod
    def _tok_local(tok):
        key, val = tok
        if key[0] == "E":
            return key, val - key[2] * SEM_GEN
        return key, val

    def _need(self, e, toks):
        best = {}
        for t in toks:
            if t is None:
                continue
            if t[0][0] == "E" and t[0][1] == e.name and t[1] > e.count:
                continue
            key, val = self._tok_local(t)
            if e.waited.get(key, 0) >= val:
                continue
            if best.get(key, 0) < val:
                best[key] = val
        for k, v in best.items():
            e.waited[k] = v
            self.semkeys[k] = None
        return list(best.items())

    def op(self, engname, fn, reads=(), writes=(), inc=True):
        if self.dry:
            return None
        rec = _Rec()
        fn(rec)
        if self.region is not None:
            pend = self._pend.setdefault(engname, [])
            pend.append((rec.call, list(reads), list(writes)))
            if inc:
                self.region.append(("op", engname, pend))
                self._pend[engname] = []
            return None
        return self._op_core(engname, rec.call, reads, writes, inc)

    def _op_core(self, engname, call, reads, writes, inc):
        e = self.eng[engname]
        xr = [b for b in reads if b.excl]
        if xr:
            reads = [b for b in reads if not b.excl]
            writes = list(writes) + [b for b in xr if b not in writes]
        deps = []
        for b in reads:
            deps.append(b.w)
        for b in writes:
            deps.append(b.w)
            deps.extend(b.r)
        waits = self._need(e, deps)
        if inc:
            e.count += 1
            e.pending = False
            tokval = e.count
        else:
            e.pending = True
            tokval = e.count + 1
        key = e.semkey(tokval)
        tok = (key, tokval)
        self.semkeys[key] = None
        for b in reads:
            b.r.append(tok)
        for b in writes:
            b.w = tok
            b.r = []
        e.ops.append((waits, call, key if inc else None, None))
        return tok

    def dma(self, engname, out_ap, in_ap, reads=(), writes=(), sembuf=None, track=True):
        if self.dry:
            return None
        if self.region is not None:
            self.region.append(("dma", engname, (out_ap, in_ap, list(reads), list(writes), sembuf, track)))
            return None
        return self._dma_core(engname, out_ap, in_ap, reads, writes, sembuf, track)

    def _dma_core(self, engname, out_ap, in_ap, reads, writes, sembuf, track):
        e = self.eng[engname]
        sb = sembuf or writes[0]
        if sb.dsem is None:
            sb.dsem = ("D", sb.name)
        deps = []
        for b in reads:
            deps.append(b.w)
        for b in writes:
            deps.append(b.w)
            deps.extend(b.r)
        waits = self._need(e, deps)
        self.dsem_count[sb.dsem] = self.dsem_count.get(sb.dsem, 0) + 1
        tok = (sb.dsem, 16 * self.dsem_count[sb.dsem])
        self.semkeys[sb.dsem] = None
        for b in reads:
            b.r.append(tok)
        for b in writes:
            b.w = tok
            b.r = []

        e.ops.append((waits, ("dma_start", (), {"out": out_ap, "in_": in_ap}), None, sb.dsem))
        if track:
            self.dma_out.append(tok)
        return tok

    def begin_region(self):
        if self.dry or not USE_REGIONS:
            return
        assert self.region is None
        self.region = []
        self._pend = {}

    @staticmethod
    def _free(ap):
        n = 1
        for d in ap.shape[1:]:
            n *= d
        return n

    def _est(self, rec):
        kind, engname, body = rec
        if kind == "dma":
            return 0.15
        t = 0.0
        for call, _, _ in body:
            name, a, k = call
            out = a[0] if a else k.get("out")
            try:
                n = self._free(out)
            except Exception:
                n = 512
            if engname == "tensor":
                if name == "matmul":
                    d = max(n, 64) / 2400.0 + 0.03
                    if a[1].dtype == F32:
                        d *= 4.0
                    t += d
                else:
                    t += 0.1
            elif engname == "vector":
                t += (n + 150) / 960.0
            elif engname == "scalar":
                t += (n + 224) / 1200.0
            elif engname == "gpsimd":
                t += (2 * n + 150) / 960.0
            else:
                t += 0.1
        return t

    def end_region(self):
        if self.dry or not USE_REGIONS:
            return
        recs = self.region
        self.region = None
        for en, p in self._pend.items():
            assert not p, f"pending un-inc'd ops on {en} at region end"
        n = len(recs)
        last_w = {}
        readers = {}
        deps = [set() for _ in range(n)]
        for i, (kind, engname, body) in enumerate(recs):
            if kind == "dma":
                rr, ww = body[2], body[3]
            else:
                rr = [b for c in body for b in c[1]]
                ww = [b for c in body for b in c[2]]
            R = [b for b in rr if not b.excl]
            Wr = list(ww) + [b for b in rr if b.excl]
            for b in R:
                if id(b) in last_w:
                    deps[i].add(last_w[id(b)])
            for b in Wr:
                if id(b) in last_w:
                    deps[i].add(last_w[id(b)])
                deps[i].update(readers.get(id(b), ()))
            for b in R:
                readers.setdefault(id(b), []).append(i)
            for b in Wr:
                last_w[id(b)] = i
                readers[id(b)] = []
            deps[i].discard(i)
        dur = [self._est(r) for r in recs]
        succ = [[] for _ in range(n)]
        indeg = [0] * n
        for i in range(n):
            for d in deps[i]:
                succ[d].append(i)
            indeg[i] = len(deps[i])
        tail = [0.0] * n
        for i in range(n - 1, -1, -1):
            m = 0.0
            for j in succ[i]:
                if tail[j] > m:
                    m = tail[j]
            tail[i] = dur[i] + (m + LAT_TAIL if succ[i] else 0.0)
        ready_t = [0.0] * n
        ready = [i for i in range(n) if indeg[i] == 0]
        eng_free = {}
        order = []
        LAT = 1.2
        WIN = 48
        done = [False] * n
        lo = 0
        while ready:
            while lo < n and done[lo]:
                lo += 1
            cands = []
            emin = None
            for i in ready:
                if i > lo + WIN_REGION:
                    continue
                est = max(eng_free.get(recs[i][1], 0.0), ready_t[i])
                cands.append((est, i))
                if emin is None or est < emin:
                    emin = est
            if not cands:
                i = min(ready)
                est = max(eng_free.get(recs[i][1], 0.0), ready_t[i])
            else:
                best = None
                for est_i, i_ in cands:
                    if est_i <= emin + SCHED_EPS:
                        key = (-tail[i_], est_i, i_)
                        if best is None or key < best:
                            best = key
                i = best[2]
                est = best[1]
            ready.remove(i)
            done[i] = True
            fin = est + dur[i]
            eng_free[recs[i][1]] = fin
            if recs[i][0] == "dma":
                fin += 2.0
            order.append(i)
            for j in succ[i]:
                indeg[j] -= 1
                ready_t[j] = max(ready_t[j], fin + LAT)
                if indeg[j] == 0:
                    ready.append(j)
        assert len(order) == n
        for i in order:
            kind, engname, body = recs[i]
            if kind == "dma":
                self._dma_core(engname, *body)
            else:
                for k, (call, rr, ww) in enumerate(body):
                    self._op_core(engname, call, rr, ww, k == len(body) - 1)

    def soft_barrier(self):
        if self.dry:
            return
        if not SOFT_BARRIERS:
            return self.barrier()
        assert self.region is None
        for t in self.dma_out:
            if self.fence_dma.get(t[0], 0) < t[1]:
                self.fence_dma[t[0]] = t[1]
        self.dma_out = []
        toks = list(self.fence_dma.items())
        for e in self.eng.values():
            if e.pending:
                raise RuntimeError(f"engine {e.name} pending at soft barrier")
            if e.count > 0:
                toks.append((e.semkey(e.count), e.count))
        self.fence_toks = toks

    def barrier(self):
        if self.dry:
            return
        assert self.region is None
        toks = list(self.dma_out)
        self.dma_out = []
        for e in self.eng.values():
            if e.pending:
                raise RuntimeError(f"engine {e.name} pending at barrier")
            if e.count > 0:
                toks.append((e.semkey(e.count), e.count))
        for e in self.eng.values():
            waits = self._need(e, toks)
            if waits:
                e.ops.append((waits, None, None, None))

    def emit(self, stack):
        nc = self.nc
        sems = {}
        print(f"[sched] {len(self.semkeys)} semaphores", flush=True)
        for i, k in enumerate(self.semkeys):
            sems[k] = stack.enter_context(nc.semaphore(f"s{i}"))
        for e in self.eng.values():
            if e.pending:
                raise RuntimeError(f"engine {e.name} ends pending")
        block = stack.enter_context(nc.Block())

        def runner(e):
            def body(eng):
                for waits, fn, inckey, dsem in e.ops:
                    for k, v in waits:
                        eng.wait_ge(sems[k], v)
                    if fn is None:
                        continue
                    ins = getattr(eng, fn[0])(*fn[1], **fn[2])
                    if inckey is not None:
                        ins.then_inc(sems[inckey], 1)
                    elif dsem is not None:
                        ins.then_inc(sems[dsem], 16)
            return body

        block.tensor(runner(self.eng["tensor"]))
        block.vector(runner(self.eng["vector"]))
        block.scalar(runner(self.eng["scalar"]))
        block.gpsimd(runner(self.eng["gpsimd"]))
        block.sync(runner(self.eng["sync"]))


class T:
    __slots__ = ("ap", "b")

    def __init__(self, ap, b):
        self.ap = ap
        self.b = b


class Arena:
    def __init__(self, S_, arena_ap):
        self.S = S_
        self.arena = arena_ap
        self.off = 0

    def reset(self, off=0):
        self.off = off

    def f32(self, name, n):
        a = self.arena[:, self.off:self.off + n]
        self.off += n
        assert self.off <= ARENA_WORDS, (name, self.off)
        return T(a, self.S.buf(name))

    def bf(self, name, n):
        w = (n + 1) // 2
        a = self.arena[:, self.off:self.off + w].bitcast(BF16)
        self.off += w
        assert self.off <= ARENA_WORDS, (name, self.off)
        return T(a, self.S.buf(name))


DEBUG_TAPS = {}
DEBUG_ON = set()


def build_program(depth=DEPTH):
    nc = bass.Bass("TRN2", target_bir_lowering=False)
    dr = {}
    dr["xT"] = nc.dram_tensor("xT", [KC, 128, S], F32, kind="ExternalInput").ap()
    dr["w_in"] = nc.dram_tensor("w_in", [depth, D, 8256], F32, kind="ExternalInput").ap()
    dr["pool_w"] = nc.dram_tensor("pool_w", [depth, 4, 256, 256], F32, kind="ExternalInput").ap()
    dr["w_ssd_proj"] = nc.dram_tensor("w_ssd_proj", [depth, 2048, D], F32, kind="ExternalInput").ap()
    dr["w_out"] = nc.dram_tensor("w_out", [depth, D, D], F32, kind="ExternalInput").ap()
    dr["w_up"] = nc.dram_tensor("w_up", [depth, D, 2 * DFF], F32, kind="ExternalInput").ap()
    dr["w_down"] = nc.dram_tensor("w_down", [depth, DFF, D], F32, kind="ExternalInput").ap()
    dr["pp"] = nc.dram_tensor("pp", [128, depth * NPP], F32, kind="ExternalInput").ap()
    dr["bc"] = nc.dram_tensor("bc", [depth, NBC], F32, kind="ExternalInput").ap()
    dr["cmask"] = nc.dram_tensor("cmask", [128, 5 * 128], F32, kind="ExternalInput").ap()
    dr["cid"] = nc.dram_tensor("cid", [128, 128], F32, kind="ExternalInput").ap()
    dr["ctm"] = nc.dram_tensor("ctm", [128, 16 * 128], F32, kind="ExternalInput").ap()
    dr["crc"] = nc.dram_tensor("crc", [1, 64], F32, kind="ExternalInput").ap()
    dr["cid8"] = nc.dram_tensor("cid8", [128, 1024], F32, kind="ExternalInput").ap()
    dr["out"] = nc.dram_tensor("out", [KC, 128, S], F32, kind="ExternalOutput").ap()
    dr["xres"] = nc.dram_tensor("xres", [KC, 128, S], F32, kind="Internal").ap()
    dr["yn"] = nc.dram_tensor("yn", [16, 128, S], BF16, kind="Internal").ap()
    dr["wob"] = nc.dram_tensor("wob", [D, D], BF16, kind="Internal").ap()
    dr["wdb"] = nc.dram_tensor("wdb", [DFF, D], BF16, kind="Internal").ap()

    with ExitStack() as st:
        plan = []
        _emit_all(nc, st, dr, depth, dry=True, plan=plan, alloc=None)
        alloc = {}
        S_ = _emit_all(nc, st, dr, depth, dry=False, plan=plan, alloc=alloc)
        S_.emit(st)
    return nc


class WMgr:
    def __init__(self, S_, slots, plan, dry):
        self.S = S_
        self.slots = slots
        self.plan = plan
        self.dry = dry
        self.i = 0
        self.issued = 0

    def _issue(self, idx):
        spec = self.plan[idx]
        slot = self.slots[idx % len(self.slots)]
        for item in spec:
            o0, shape3, src = item[0], item[1], item[2]
            eng = item[3] if len(item) > 3 else "gpsimd"
            n = shape3[0] * shape3[1]
            dst = slot.ap[:, o0:o0 + n].rearrange("p (a b) -> p a b", a=shape3[0])
            self.S.dma(eng, dst, src, writes=[slot.b], track=False)

    def load(self, spec):
        if self.dry:
            self.plan.append(spec)
            self.i += 1
            return self.slots[(self.i - 1) % len(self.slots)]
        idx = self.i
        self.i += 1
        while self.issued < min(idx + 2, len(self.plan)):
            self._issue(self.issued)
            self.issued += 1
        return self.slots[idx % len(self.slots)]


def _emit_all(nc, st, dr, depth, dry, plan, alloc):
    S_ = Sched(nc, dry=dry)
    if dry:
        class _Fake:
            def __getitem__(self, k):
                return self

            def rearrange(self, *a, **k):
                return self

            def bitcast(self, *a):
                return self

            def unsqueeze(self, *a):
                return self

            def to_broadcast(self, *a):
                return self
        fake = _Fake()

        def sbt(name, shape, dt):
            return fake

        def pst(name, shape, dt):
            return fake
    else:
        def sbt(name, shape, dt):
            return st.enter_context(nc.sbuf_tensor(name, shape, dt))

        def pst(name, shape, dt):
            return st.enter_context(nc.psum_tensor(name, shape, dt))

    op = S_.op
    dma = S_.dma

    def tap(name, ap, bufs, shape, dt):
        if name not in DEBUG_ON or dry:
            return
        d_ = nc.dram_tensor("dbg_" + name, list(shape), dt, kind="ExternalOutput").ap()
        DEBUG_TAPS[name] = d_
        dma("sync", d_, ap, reads=list(bufs), writes=[S_.buf("dbg_" + name)])

    CM = sbt("CM" + NONCE, [128, 5 * 128], F32)
    bCM = S_.buf("CM")
    LE, GE, GT_, LT_, ONES = (CM[:, i * 128:(i + 1) * 128] for i in range(5))
    IDF = sbt("IDF", [128, 128], F32)
    bIDF = S_.buf("IDF")
    IDB = sbt("IDB", [128, 128], BF16)
    bIDB = S_.buf("IDB")
    RB = sbt("RB", [128, 2048], F32R)
    bRB = S_.buf("RB")
    RB3 = RB[:, :].rearrange("p (h l) -> p h l", h=16)
    GLR = sbt("GLR", [128, 256], F32R)
    bGLR = S_.buf("GLR")
    RCN = sbt("RCN", [128, 64], F32)
    bRCN = S_.buf("RCN")
    PP = sbt("PP", [128, depth * NPP], F32)
    bPP = S_.buf("PP")
    BC = sbt("BC", [128, NBC_SB], F32)
    bNG = S_.buf("BCng")
    bBC = S_.buf("BC")
    XB = sbt("XB", [128, KC * S], BF16)
    XB3 = XB[:, :].rearrange("p (k t) -> p k t", k=KC)
    bXB = S_.bufs("XB", NT)
    slots = [T(sbt(f"WS{i}", [128, 4096], BF16)[:, :], S_.buf(f"WS{i}")) for i in range(3)]
    ARENA = sbt("ARENA", [128, ARENA_WORDS], F32)
    AR = Arena(S_, ARENA)
    PS = pst("PS", [128, 8 * 512], F32)
    bPS = S_.bufs("PSB", 8, excl=True)
    W = WMgr(S_, slots, plan, dry)

    def bank(i, n=1):
        return PS[:, i * 512:(i + n) * 512]

    dma("sync", CM[:, :], dr["cmask"], writes=[bCM])
    dma("sync", IDF[:, :], dr["cid"], writes=[bIDF])
    dma("gpsimd", IDB[:, :], dr["cid"], writes=[bIDB])
    dma("sync", RCN[:, :], dr["crc"].to_broadcast([128, 64]), writes=[bRCN])
    dma("sync", PP[:, :], dr["pp"], writes=[bPP])
    dma("gpsimd", GLR[:, :], dr["cmask"][:, 256:512], writes=[bGLR])
    for nt in range(NT):
        dma("gpsimd", XB3[:, :, nt * 512:(nt + 1) * 512],
            dr["xT"].rearrange("k p t -> p k t")[:, :, nt * 512:(nt + 1) * 512], writes=[bXB[nt]])

    def pcol(l, off):
        return PP[:, l * NPP + off: l * NPP + off + 1]

    def win_src(l, c0, ncols):
        return dr["w_in"][l].rearrange("(k p) n -> p k n", p=128)[:, :, c0:c0 + ncols]

    def proj_fm(slot, so, ncols_in_slot, cidx, psbanks, pbufs):
        w3 = slot.ap[:, so:so + KC * ncols_in_slot].rearrange("p (k n) -> p k n", k=KC)
        for nt in range(NT):
            for kc in range(KC):
                last = (nt == NT - 1 and kc == KC - 1)
                op("tensor",
                   lambda e, nt=nt, kc=kc: e.matmul(bank(psbanks + nt), w3[:, kc, cidx * 128:(cidx + 1) * 128],
                                                    XB3[:, kc, nt * 512:(nt + 1) * 512],
                                                    start=(kc == 0), stop=(kc == KC - 1)),
                   reads=[slot.b] + bXB, writes=pbufs, inc=last)

    xres_src = dr["xT"]
    xres_bufs = [S_.bufs(f"xT{i}_", 2) for i in range(NT)]
    bXRES = [S_.bufs(f"xres{i}_", 2) for i in range(NT)]
    bYN = S_.bufs("yn", 2)
    bWOB = S_.buf("wob")
    bWDB = S_.buf("wdb")
    bOUT = S_.bufs("out", 2)

    for l in range(depth):
        S_.soft_barrier()
        AR.reset()
        SCR1 = AR.f32("SCR1", 2052)
        PAD2 = AR.f32("PAD2", 2052)
        SCR2 = AR.f32("SCR2", 2048)
        SCR3 = AR.bf("SCR3", 2048)
        MB2 = AR.bf("MB2", 2048)
        XTOK = AR.bf("XTOK", 16 * 512)
        bXTOK = S_.bufs("XTOKc", 16)
        XTOK3 = XTOK.ap.rearrange("p (c n) -> p c n", c=16)
        BTOK = AR.bf("BTOK", 16 * 128)
        bBTOK = S_.bufs("BTOKc", 16)
        BTOK3 = BTOK.ap.rearrange("p (c n) -> p c n", c=16)
        BT = AR.bf("BT", 2048)
        CT = AR.bf("CT", 2048)
        HPREV = AR.bf("HPREV", 16 * 512)
        bHPREV = S_.bufs("HPREVc", 16)
        HPREV3 = HPREV.ap.rearrange("p (c n) -> p c n", c=16)
        XDTFs = [AR.bf(f"XDTF{i}", 512) for i in range(2)]
        XDTBs = [AR.bf(f"XDTB{i}", 512) for i in range(2)]
        XDD = AR.bf("XDD", 512)
        HS = AR.f32("HS", 512)
        HBb = AR.bf("HBb", 512)
        DTt = AR.f32("DT", 1024)
        ADT = AR.f32("ADT", 1024)
        ACUM = T(SCR2.ap[:, 0:1024], SCR2.b)
        EXPA = AR.f32("EXPA", 1024)
        DTDE = AR.f32("DTDE", 1024)
        DECC = AR.f32("DECC", 1024)
        EA = AR.f32("EA", 64)
        SFB = AR.f32("SFB", 256)
        acc2_off = AR.off
        T1 = AR.f32("T1", 512)
        T2 = AR.f32("T2", 512)
        YT = AR.f32("YT", 512)
        SZ = AR.f32("SZ", 512)
        ACC2ap = ARENA[:, acc2_off:acc2_off + 2048]
        ACC2b = [T1.b, T2.b, YT.b, SZ.b]
        VV = AR.f32("V", 512)
        V2 = AR.f32("V2", 512)
        JK = AR.bf("JK", 512) if JK_BF else AR.f32("JK", 512)
        SS = AR.f32("SS", 4)
        VN = AR.bf("VN", 512)
        YNT = [AR.bf(f"YNT{i}", 512) for i in range(2)]
        ID8 = AR.bf("ID8", 1024)
        DIg = AR.bf("DIg", 1024)
        DI3 = DIg.ap.rearrange("p (h l) -> p h l", h=8)

        def v3(ap, c):
            return ap[:, c * 64:(c + 1) * 64]

        def hd_bc(t, c, h0):
            return t.ap[:, c * 64 + h0: c * 64 + h0 + 8].unsqueeze(2).to_broadcast([128, 8, 64])

        dma("sync", BC[:, 0:BC_NG], dr["bc"][l:l + 1, 0:BC_NG].to_broadcast([128, BC_NG]), writes=[bBC])
        dma("gpsimd", ID8.ap, dr["cid8"], writes=[ID8.b])
        wdt = W.load([(0, (KC, 64), win_src(l, DT0, 64))])
        wdt3 = wdt.ap[:, 0:KC * 64].rearrange("p (k n) -> p k n", k=KC)
        DTP = PS[:, 4 * 512: 4 * 512 + 1024]
        ACP = PS[:, 6 * 512: 6 * 512 + 1024]
        ATP = PS[:, 2 * 512: 2 * 512 + 1024]
        for i in range(NCH):
            for kc in range(KC):
                op("tensor", lambda e, i=i, kc=kc: e.matmul(DTP[:, i * 64:(i + 1) * 64],
                                                          XB3[:, kc, i * 128:(i + 1) * 128], wdt3[:, kc, :],
                                                          start=(kc == 0), stop=(kc == KC - 1)),
                   reads=[wdt.b] + bXB, writes=[bPS[4], bPS[5]], inc=(kc == KC - 1 and i == NCH - 1))
        op("vector", lambda e: e.tensor_tensor(DTt.ap.rearrange("p (c h) -> p c h", c=16),
                                               DTP.rearrange("p (c h) -> p c h", c=16),
                                               BC[:, BC_DTB:BC_DTB + 64].unsqueeze(1).to_broadcast([128, 16, 64]),
                                               ALU.add),
           reads=[bPS[4], bPS[5], bBC], writes=[DTt.b])
        op("scalar", lambda e: e.activation(DTt.ap, DTt.ap, AF.Exp), reads=[DTt.b], writes=[DTt.b])
        op("scalar", lambda e: e.activation(DTt.ap, DTt.ap, AF.Ln, bias=1.0), reads=[DTt.b], writes=[DTt.b])
        op("scalar", lambda e: e.activation(EA.ap, BC[:, BC_ALOG:BC_ALOG + 64], AF.Exp), reads=[bBC], writes=[EA.b])
        op("vector", lambda e: e.scalar_tensor_tensor(ADT.ap.rearrange("p (c h) -> p c h", c=16),
                                                      DTt.ap.rearrange("p (c h) -> p c h", c=16), -1.0,
                                                      EA.ap.unsqueeze(1).to_broadcast([128, 16, 64]),
                                                      ALU.mult, ALU.mult),
           reads=[DTt.b, EA.b], writes=[ADT.b])
        for c in range(NCH):
            op("tensor", lambda e, c=c: e.matmul(ACP[:, c * 64: c * 64 + 32], LE, ADT.ap[:, c * 64: c * 64 + 32],
                                                 start=True, stop=True),
               reads=[ADT.b, bCM], writes=[bPS[6], bPS[7]], inc=False)
            op("tensor", lambda e, c=c: e.matmul(ACP[:, c * 64 + 32: c * 64 + 64], GE,
                                                 ADT.ap[:, c * 64 + 32: c * 64 + 64], start=True, stop=True),
               reads=[ADT.b, bCM], writes=[bPS[6], bPS[7]], inc=False)
            op("tensor", lambda e, c=c: e.matmul(ATP[:, c * 64: c * 64 + 64], ONES, ADT.ap[:, c * 64: c * 64 + 64],
                                                 start=True, stop=True),
               reads=[ADT.b, bCM], writes=[bPS[2], bPS[3]], inc=(c == NCH - 1))
        op("scalar", lambda e: e.copy(ACUM.ap, ACP), reads=[bPS[6], bPS[7]], writes=[ACUM.b])
        op("scalar", lambda e: e.activation(EXPA.ap, ACUM.ap, AF.Exp), reads=[ACUM.b], writes=[EXPA.b])
        op("scalar", lambda e: e.activation(DECC.ap, ATP, AF.Exp), reads=[bPS[2], bPS[3]], writes=[DECC.b])
        op("vector", lambda e: e.tensor_tensor(DTDE.ap, ATP, ACUM.ap, ALU.subtract),
           reads=[bPS[2], bPS[3], ACUM.b], writes=[DTDE.b])
        op("scalar", lambda e: e.activation(DTDE.ap, DTDE.ap, AF.Exp), reads=[DTDE.b], writes=[DTDE.b])
        op("vector", lambda e: e.tensor_tensor(DTDE.ap, DTDE.ap, DTt.ap, ALU.mult),
           reads=[DTDE.b, DTt.b], writes=[DTDE.b])
        op("gpsimd", lambda e: e.memset(SCR1.ap[:, 0:2], 0.0), writes=[SCR1.b])
        op("gpsimd", lambda e: e.memset(SCR1.ap[:, 2050:2052], 0.0), writes=[SCR1.b])
        op("gpsimd", lambda e: e.memset(PAD2.ap[:, 0:2], 0.0), writes=[PAD2.b])
        op("gpsimd", lambda e: e.memset(PAD2.ap[:, 2050:2052], 0.0), writes=[PAD2.b])

        for g in range(4):
            S_.begin_region()
            wx = W.load([(0, (KC, 512), win_src(l, XBC0 + 512 * g, 512))])
            wbc = W.load([(0, (KC, 128), win_src(l, XBC0 + 2048 + 128 * g, 128)),
                          (KC * 128, (KC, 128), win_src(l, XBC0 + 2560 + 128 * g, 128))])
            if l == 0 and g == 0:
                tap("wx", wx.ap, [wx.b], [128, 4096], BF16)
            if g > 0:
                op("gpsimd", lambda e: e.memset(PAD2.ap[:, 0:2], 0.0), writes=[PAD2.b])
            tpi = 0

            def projA(cc):
                if cc < 4:
                    proj_fm(wx, 0, 512, cc, 0, bPS[0:4])
                elif cc == 4:
                    proj_fm(wbc, 0, 128, 0, 0, bPS[0:4])
                else:
                    proj_fm(wbc, KC * 128, 128, 0, 0, bPS[0:4])

            projA(0)
            for cc in range(6):
                cch = (4 * g + cc) if cc < 4 else ((16 + g) if cc == 4 else (20 + g))
                pad = SCR1 if (cc % 2 == 0 or not DB_A) else PAD2
                acc_ap = SCR2.ap if (cc % 2 == 0 or not DB_A) else ACC2ap
                acc_b = [SCR2.b] if (cc % 2 == 0 or not DB_A) else ACC2b
                op("scalar", lambda e, pad=pad: e.copy(pad.ap[:, 2:2050], bank(0, 4)), reads=bPS[0:4], writes=[pad.b])
                if cc < 5:
                    projA(cc + 1)
                op("scalar", lambda e, cch=cch, pad=pad, acc_ap=acc_ap: e.activation(
                    acc_ap, pad.ap[:, 0:2048], AF.Identity, bias=pcol(l, PP_CB + cch), scale=pcol(l, PP_CW + cch * 5)),
                   reads=[pad.b, bPP], writes=acc_b)
                for k in range(1, 5):
                    op("vector", lambda e, k=k, cch=cch, pad=pad, acc_ap=acc_ap: e.scalar_tensor_tensor(
                        acc_ap, pad.ap[:, k:k + 2048], pcol(l, PP_CW + cch * 5 + k), acc_ap, ALU.mult, ALU.add),
                       reads=[pad.b, bPP] + acc_b, writes=acc_b)
                dst = SCR3 if cc < 4 else (BT if cc == 4 else CT)
                op("scalar", lambda e, dst=dst, acc_ap=acc_ap: e.activation(dst.ap, acc_ap, AF.Silu),
                   reads=acc_b, writes=[dst.b])
                if l == 0 and g == 0 and cc == 0:
                    tap("xs0", SCR3.ap, [SCR3.b], [128, 2048], BF16)
                    tap("pad0", SCR1.ap, [SCR1.b], [128, 2052], F32)
                if cc < 5:
                    for q4 in range(4):
                        pb = 4 + (tpi % 2)
                        tpi += 1
                        TPv = bank(pb)[:, 0:256].bitcast(BF16)
                        for q in range(4):
                            i = q4 * 4 + q
                            op("tensor", lambda e, dst=dst, i=i, q=q, TPv=TPv: e.transpose(
                                TPv[:, q * 128:(q + 1) * 128], dst.ap[:, i * 128:(i + 1) * 128], IDB[:, :]),
                               reads=[dst.b, bIDB], writes=[bPS[pb]], inc=(q == 3))
                        if cc < 4:
                            op("scalar", lambda e, q4=q4, cc=cc, TPv=TPv: e.copy(
                                XTOK3[:, q4 * 4:q4 * 4 + 4, cc * 128:(cc + 1) * 128],
                                TPv.rearrange("p (a b) -> p a b", a=4)),
                               reads=[bPS[pb]], writes=bXTOK[q4 * 4:q4 * 4 + 4])
                        else:
                            op("scalar", lambda e, q4=q4, TPv=TPv: e.copy(
                                BTOK3[:, q4 * 4:q4 * 4 + 4, :], TPv.rearrange("p (a b) -> p a b", a=4)),
                               reads=[bPS[pb]], writes=bBTOK[q4 * 4:q4 * 4 + 4])

            wz = W.load([(0, (KC, 512), win_src(l, Z0 + 512 * g, 512))])
            wz3 = wz.ap[:, 0:KC * 512].rearrange("p (k n) -> p k n", k=KC)
            ko0, ko1 = 2 * g, 2 * g + 2
            dma("gpsimd", dr["wob"].rearrange("(k p) n -> p k n", p=128)[:, ko0:ko1, :],
                dr["w_out"][l].rearrange("(k p) n -> p k n", p=128)[:, ko0:ko1, :], writes=[bWOB])
            kd0, kd1 = 6 * g, min(6 * g + 6, FK)
            dma("gpsimd", dr["wdb"].rearrange("(k p) n -> p k n", p=128)[:, kd0:kd1, :],
                dr["w_down"][l].rearrange("(k p) n -> p k n", p=128)[:, kd0:kd1, :], writes=[bWDB])
            hf, hb = 8 * g, 32 + 8 * g
            R3 = SCR1.ap[:, 0:2048].rearrange("p (h l) -> p h l", h=16)
            Es = [SCR2, T(PAD2.ap[:, 0:2048], PAD2.b)]

            def state_update(c, h0):
                op("gpsimd", lambda e: e.tensor_tensor(XDD.ap.rearrange("p (h d) -> p h d", h=8),
                                                       XTOK3[:, c, :].rearrange("p (h d) -> p h d", h=8),
                                                       hd_bc(DTDE, c, h0), ALU.mult),
                   reads=[bXTOK[c], DTDE.b], writes=[XDD.b])
                op("tensor", lambda e: e.matmul(bank(5), BTOK3[:, c, :], XDD.ap, start=True, stop=True),
                   reads=[bBTOK[c], XDD.b], writes=[bPS[5]])
                op("vector", lambda e: e.tensor_tensor(HS.ap.rearrange("p (h d) -> p h d", h=8),
                                                       HS.ap.rearrange("p (h d) -> p h d", h=8),
                                                       hd_bc(DECC, c, h0), ALU.mult),
                   reads=[HS.b, DECC.b], writes=[HS.b])
                op("vector", lambda e: e.tensor_tensor(HS.ap, HS.ap, bank(5), ALU.add),
                   reads=[HS.b, bPS[5]], writes=[HS.b])

            dma("sync", BC[:, BC_NG:BC_NG + 512],
                dr["bc"][l:l + 1, BC_NG + 512 * g:BC_NG + 512 * (g + 1)].to_broadcast([128, 512]), writes=[bNG])
            op("gpsimd", lambda e: e.tensor_tensor(
                DI3, ID8.ap.rearrange("p (h l) -> p h l", h=8),
                BC[:, BC_DSK + hf: BC_DSK + hf + 8].unsqueeze(2).to_broadcast([128, 8, 128]), ALU.mult),
               reads=[ID8.b, bBC], writes=[DIg.b])
            op("gpsimd", lambda e: e.memset(HS.ap, 0.0), writes=[HS.b])
            for c in range(NCH):
                op("scalar", lambda e, c=c: e.copy(HPREV3[:, c, :], HS.ap), reads=[HS.b], writes=[bHPREV[c]])
                if c < NCH - 1:
                    state_update(c, hf)
            MBs = [SCR3, MB2]
            bSC = bPS[7]
            bTPY = bPS[7]

            def stage1(c, par):
                Mt = MBs[par]
                M3 = Mt.ap.rearrange("p (h l) -> p h l", h=16)
                xf, xb_ = XDTFs[par], XDTBs[par]
                E = Es[par]
                SC = bank(7)[:, 0:128]
                op("tensor", lambda e: e.matmul(SC, BT.ap[:, c * 128:(c + 1) * 128], CT.ap[:, c * 128:(c + 1) * 128],
                                                start=True, stop=True),
                   reads=[BT.b, CT.b], writes=[bSC])
                op("vector", lambda e: e.tensor_tensor(SFB.ap.rearrange("p (a l) -> p a l", a=2),
                                                       SC.unsqueeze(1).to_broadcast([128, 2, 128]),
                                                       CM[:, 0:256].rearrange("p (a l) -> p a l", a=2), ALU.mult),
                   reads=[bSC, bCM], writes=[SFB.b])
                op("gpsimd", lambda e: e.tensor_tensor(
                    RB3[:, 0:8, :], LE.unsqueeze(1).to_broadcast([128, 8, 128]),
                    ADT.ap[:, c * 64 + hf: c * 64 + hf + 8].unsqueeze(2).to_broadcast([128, 8, 128]), ALU.mult),
                   reads=[bCM, ADT.b], writes=[bRB])
                op("gpsimd", lambda e: e.tensor_tensor(
                    RB3[:, 8:16, :], GE.unsqueeze(1).to_broadcast([128, 8, 128]),
                    ADT.ap[:, c * 64 + hb: c * 64 + hb + 8].unsqueeze(2).to_broadcast([128, 8, 128]), ALU.mult),
                   reads=[bCM, ADT.b], writes=[bRB])
                op("gpsimd", lambda e: e.tensor_tensor(xf.ap.rearrange("p (h d) -> p h d", h=8),
                                                       XTOK3[:, c, :].rearrange("p (h d) -> p h d", h=8),
                                                       hd_bc(DTt, c, hf), ALU.mult),
                   reads=[bXTOK[c], DTt.b], writes=[xf.b])
                op("gpsimd", lambda e: e.tensor_tensor(xb_.ap.rearrange("p (h d) -> p h d", h=8),
                                                       XTOK3[:, c, :].rearrange("p (h d) -> p h d", h=8),
                                                       hd_bc(DTt, c, hb), ALU.mult),
                   reads=[bXTOK[c], DTt.b], writes=[xb_.b])
                for d_ in range(2):
                    lhs = GLR[:, 0:128] if d_ == 0 else GLR[:, 128:256]
                    for q in range(2):
                        op("tensor", lambda e, d_=d_, q=q, lhs=lhs: e.matmul(
                            bank(q), lhs, RB[:, d_ * 1024 + q * 512: d_ * 1024 + (q + 1) * 512],
                            start=True, stop=True),
                           reads=[bRB, bGLR], writes=[bPS[q]], inc=(q == 1))
                    op("scalar", lambda e, d_=d_: e.activation(E.ap[:, d_ * 1024:(d_ + 1) * 1024], bank(0, 2), AF.Exp),
                       reads=bPS[0:2], writes=[E.b])
                    op("vector", lambda e, d_=d_: e.tensor_tensor(
                        M3[:, d_ * 8:(d_ + 1) * 8, :],
                        E.ap[:, d_ * 1024:(d_ + 1) * 1024].rearrange("p (h l) -> p h l", h=8),
                        SFB.ap[:, d_ * 128:(d_ + 1) * 128].unsqueeze(1).to_broadcast([128, 8, 128]), ALU.mult),
                       reads=[E.b, SFB.b], writes=[Mt.b])

            def stage2(c, par, ci):
                Mt = MBs[par]
                M3 = Mt.ap.rearrange("p (h l) -> p h l", h=16)
                xf, xb_ = XDTFs[par], XDTBs[par]
                op("scalar", lambda e: e.copy(HBb.ap, HS.ap), reads=[HS.b], writes=[HBb.b])
                if c > 0:
                    state_update(c, hb)
                for h in range(8):
                    op("tensor", lambda e, h=h: e.matmul(bank(2)[:, h * 64:(h + 1) * 64], M3[:, h, :],
                                                         xf.ap[:, h * 64:(h + 1) * 64], start=True, stop=False),
                       reads=[Mt.b, xf.b], writes=[bPS[2]], inc=False)
                    op("tensor", lambda e, h=h: e.matmul(bank(2)[:, h * 64:(h + 1) * 64], M3[:, 8 + h, :],
                                                         xb_.ap[:, h * 64:(h + 1) * 64], start=False, stop=False),
                       reads=[Mt.b, xb_.b], writes=[bPS[2]], inc=False)
                    op("tensor", lambda e, h=h: e.matmul(bank(2)[:, h * 64:(h + 1) * 64], DI3[:, h, :],
                                                         XTOK3[:, c, h * 64:(h + 1) * 64], start=False, stop=True),
                       reads=[DIg.b, bXTOK[c]], writes=[bPS[2]], inc=(h == 7))
                op("tensor", lambda e: e.matmul(bank(3), CT.ap[:, c * 128:(c + 1) * 128], HPREV3[:, c, :],
                                                start=True, stop=True),
                   reads=[CT.b, bHPREV[c]], writes=[bPS[3]])
                op("tensor", lambda e: e.matmul(bank(4), CT.ap[:, c * 128:(c + 1) * 128], HBb.ap,
                                                start=True, stop=True),
                   reads=[CT.b, HBb.b], writes=[bPS[4]])
                for kc in range(KC):
                    op("tensor", lambda e, kc=kc: e.matmul(bank(6), XB3[:, kc, c * 128:(c + 1) * 128], wz3[:, kc, :],
                                                           start=(kc == 0), stop=(kc == KC - 1)),
                       reads=[wz.b] + bXB, writes=[bPS[6]], inc=(kc == KC - 1))
                op("vector", lambda e: e.tensor_tensor(T1.ap.rearrange("p (h d) -> p h d", h=8),
                                                       bank(3).rearrange("p (h d) -> p h d", h=8),
                                                       hd_bc(EXPA, c, hf), ALU.mult),
                   reads=[bPS[3], EXPA.b], writes=[T1.b])
                op("vector", lambda e: e.tensor_tensor(T2.ap.rearrange("p (h d) -> p h d", h=8),
                                                       bank(4).rearrange("p (h d) -> p h d", h=8),
                                                       hd_bc(EXPA, c, hb), ALU.mult),
                   reads=[bPS[4], EXPA.b], writes=[T2.b])
                op("gpsimd", lambda e: e.tensor_tensor(T1.ap, T1.ap, T2.ap, ALU.add), reads=[T1.b, T2.b], writes=[T1.b])
                op("vector", lambda e: e.tensor_tensor(YT.ap, T1.ap, bank(2), ALU.add), reads=[T1.b, bPS[2]], writes=[YT.b])
                op("scalar", lambda e: e.activation(SZ.ap, bank(6), AF.Silu), reads=[bPS[6]], writes=[SZ.b])
                op("vector", lambda e: e.tensor_tensor(VV.ap, YT.ap, SZ.ap, ALU.mult), reads=[YT.b, SZ.b], writes=[VV.b])
                op("gpsimd", lambda e: e.memset(SS.ap[:, 0:1], 0.0), writes=[SS.b])
                op("scalar", lambda e: e.activation(JK.ap, VV.ap, AF.Square, accum_out=SS.ap[:, 0:1]),
                   reads=[VV.b], writes=[JK.b, SS.b])
                op("scalar", lambda e: e.activation(SS.ap[:, 1:2], SS.ap[:, 0:1], AF.Ln, bias=RMS_EPS, scale=1.0 / 512),
                   reads=[SS.b], writes=[SS.b])
                op("scalar", lambda e: e.activation(SS.ap[:, 2:3], SS.ap[:, 1:2], AF.Exp, scale=-0.5),
                   reads=[SS.b], writes=[SS.b])
                op("scalar", lambda e: e.activation(V2.ap, VV.ap, AF.Copy, scale=SS.ap[:, 2:3]),
                   reads=[VV.b, SS.b], writes=[V2.b])
                op("vector", lambda e: e.tensor_tensor(VN.ap, V2.ap, BC[:, BC_NG: BC_NG + 512], ALU.mult),
                   reads=[V2.b, bNG], writes=[VN.b])
                TPY = bank(7)[:, 256:512].bitcast(BF16)
                for q in range(4):
                    op("tensor", lambda e, q=q: e.transpose(TPY[:, q * 128:(q + 1) * 128], VN.ap[:, q * 128:(q + 1) * 128],
                                                            IDB[:, :]),
                       reads=[VN.b, bIDB], writes=[bTPY], inc=(q == 3))
                ynt = YNT[ci % 2]
                op("scalar", lambda e: e.copy(ynt.ap, TPY), reads=[bTPY], writes=[ynt.b])
                dma("sync", dr["yn"][4 * g:4 * g + 4, :, c * 128:(c + 1) * 128].rearrange("q p t -> p q t"),
                    ynt.ap.rearrange("p (q t) -> p q t", q=4), reads=[ynt.b], writes=[bYN[ci % 2]], sembuf=ynt.b)

            op("gpsimd", lambda e: e.memset(HS.ap, 0.0), writes=[HS.b])
            if PIPELINE_B:
                stage1(NCH - 1, 0)
            for ci, c in enumerate(range(NCH - 1, -1, -1)):
                if PIPELINE_B:
                    if c > 0:
                        stage1(c - 1, (ci + 1) % 2)
                else:
                    stage1(c, ci % 2)
                stage2(c, ci % 2, ci)
            S_.end_region()

        S_.soft_barrier()
        AR.reset()
        MRG = AR.bf("MRG", KC * S)
        MRG3 = MRG.ap.rearrange("p (k t) -> p k t", k=KC)
        bMRG = S_.bufs("MRGj", KC)
        mrg_end = AR.off
        YN = AR.bf("YN", 16 * S)
        YN3 = YN.ap.rearrange("p (k t) -> p k t", k=16)
        G = AR.f32("G", 2048)
        S_.begin_region()
        bYNl = S_.bufs("YNl", 4)
        for k4 in range(4):
            dma("sync", YN3[:, 4 * k4:4 * k4 + 4, :], dr["yn"][4 * k4:4 * k4 + 4].rearrange("k p t -> p k t"),
                reads=bYN, writes=[bYNl[k4]])
        for j in range(KC):
            wp = W.load([(0, (16, 128), dr["w_ssd_proj"][l].rearrange("(k p) n -> p k n", p=128)[:, :, j * 128:(j + 1) * 128])])
            wp3 = wp.ap[:, 0:2048].rearrange("p (k n) -> p k n", k=16)
            wg = W.load([(0, (KC, 128), win_src(l, G0 + 1024 + j * 128, 128))])
            for nt in range(NT):
                for kc in range(16):
                    op("tensor", lambda e, nt=nt, kc=kc: e.matmul(bank(nt), wp3[:, kc, :], YN3[:, kc, nt * 512:(nt + 1) * 512],
                                                                  start=(kc == 0), stop=(kc == 15)),
                       reads=[wp.b, bYNl[kc // 4]], writes=bPS[0:4], inc=(nt == NT - 1 and kc == 15))
            proj_fm(wg, 0, 128, 0, 4, bPS[4:8])
            op("scalar", lambda e: e.activation(G.ap, bank(4, 4), AF.Sigmoid), reads=bPS[4:8], writes=[G.b])
            op("vector", lambda e, j=j: e.tensor_tensor(MRG3[:, j, :], G.ap, bank(0, 4), ALU.mult),
               reads=[G.b] + bPS[0:4], writes=[bMRG[j]])
        S_.end_region()

        if l == 0:
            tap("mrgC", MRG.ap, bMRG, [128, KC * S], BF16)
            tap("yn", YN.ap, bYNl, [128, 16 * S], BF16)
        S_.soft_barrier()
        AR.reset(mrg_end)
        P0 = AR.f32("P0", 2064)
        Q1 = AR.f32("Q1", 2064)
        Q2 = AR.f32("Q2", 2064)
        PLD = [AR.bf(f"PLD{i}", 2048) for i in range(2)]
        G = AR.f32("G", 2048)
        TMP = AR.f32("TMP", 2048)
        TE = AR.f32("TE", 16)
        S_.begin_region()
        op("gpsimd", lambda e: e.memset(P0.ap[:, 0:8], 0.0), writes=[P0.b])
        op("gpsimd", lambda e: e.memset(P0.ap[:, 2056:2064], 0.0), writes=[P0.b])
        for gi, w_ in enumerate(POOL_WINDOWS):
            half = w_ // 2
            wu = W.load([(0, (KC, 256), win_src(l, U0 + 256 * gi, 256))])
            for k2 in range(2):
                pb0 = 4 * (k2 % 2)
                proj_fm(wu, 0, 256, k2, pb0, bPS[pb0:pb0 + 4])
                op("scalar", lambda e, pb0=pb0: e.copy(P0.ap[:, 8:2056], bank(pb0, 4)), reads=bPS[pb0:pb0 + 4], writes=[P0.b])
                src, dst = P0, Q1
                sh = 1
                while sh < w_:
                    op("vector", lambda e, src=src, dst=dst, sh=sh: e.tensor_tensor(
                        dst.ap[:, sh:2064], src.ap[:, sh:2064], src.ap[:, 0:2064 - sh], ALU.add),
                       reads=[src.b], writes=[dst.b])
                    src = dst
                    dst = Q2 if dst is Q1 else Q1
                    sh *= 2
                o = 8 + half - 1
                pl = PLD[k2]
                op("vector", lambda e, src=src, o=o, pl=pl, w_=w_: e.scalar_tensor_tensor(
                    pl.ap, src.ap[:, o:o + 2048], 1.0 / w_, P0.ap[:, 8:2056], ALU.mult, ALU.subtract),
                   reads=[src.b, P0.b], writes=[pl.b])
                nl, nr = half, half - 1
                op("vector", lambda e, src=src, o=o, gi=gi, nl=nl: e.tensor_tensor(
                    TE.ap[:, 0:nl], src.ap[:, o:o + nl], RCN[:, gi * 16: gi * 16 + nl], ALU.mult),
                   reads=[src.b, bRCN], writes=[TE.b])
                op("vector", lambda e, pl=pl, nl=nl: e.tensor_tensor(pl.ap[:, 0:nl], TE.ap[:, 0:nl], P0.ap[:, 8:8 + nl],
                                                                      ALU.subtract),
                   reads=[TE.b, P0.b], writes=[pl.b])
                if nr > 0:
                    op("vector", lambda e, src=src, o=o, gi=gi, nr=nr: e.tensor_tensor(
                        TE.ap[:, 8:8 + nr], src.ap[:, o + 2048 - nr:o + 2048],
                        RCN[:, gi * 16 + 8: gi * 16 + 8 + nr], ALU.mult),
                       reads=[src.b, bRCN], writes=[TE.b])
                    op("vector", lambda e, pl=pl, nr=nr: e.tensor_tensor(
                        pl.ap[:, 2048 - nr:2048], TE.ap[:, 8:8 + nr], P0.ap[:, 8 + 2048 - nr:8 + 2048], ALU.subtract),
                       reads=[TE.b, P0.b], writes=[pl.b])
            wm = W.load([(0, (2, 256), dr["pool_w"][l, gi].rearrange("(k p) n -> p k n", p=128))])
            wm3 = wm.ap[:, 0:512].rearrange("p (k n) -> p k n", k=2)
            for jj in range(2):
                j = 2 * gi + jj
                for nt in range(NT):
                    for k2 in range(2):
                        op("tensor", lambda e, nt=nt, k2=k2, jj=jj: e.matmul(
                            bank(nt), wm3[:, k2, jj * 128:(jj + 1) * 128], PLD[k2].ap[:, nt * 512:(nt + 1) * 512],
                            start=(k2 == 0), stop=(k2 == 1)),
                           reads=[wm.b, PLD[0].b, PLD[1].b], writes=bPS[0:4], inc=(nt == NT - 1 and k2 == 1))
                wg = W.load([(0, (KC, 128), win_src(l, G0 + j * 128, 128))])
                proj_fm(wg, 0, 128, 0, 4, bPS[4:8])
                op("scalar", lambda e: e.activation(G.ap, bank(4, 4), AF.Sigmoid), reads=bPS[4:8], writes=[G.b])
                op("vector", lambda e, j=j: e.scalar_tensor_tensor(TMP.ap, bank(0, 4), pcol(l, PP_PS + j), G.ap,
                                                                   ALU.mult, ALU.mult),
                   reads=bPS[0:4] + [G.b, bPP], writes=[TMP.b])
                op("gpsimd", lambda e, j=j: e.tensor_tensor(MRG3[:, j, :], TMP.ap, MRG3[:, j, :], ALU.add),
                   reads=[TMP.b, bMRG[j]], writes=[bMRG[j]])
        S_.end_region()

        def outproj_ln(rhs3, rhs_bufs, nK, wsrc, goff, boff, final):
            nonlocal xres_src, xres_bufs
            SUMt = AR.f32("SUMt", KC * 512)
            SUM3 = SUMt.ap.rearrange("p (k t) -> p k t", k=KC)
            XR = [AR.f32(f"XR{i}", 512) for i in range(2)]
            SQ = [AR.f32(f"SQ{i}", 512) for i in range(2)]
            MEAN = AR.f32("MEAN", 512)
            M2 = AR.f32("M2", 512)
            RSTD = AR.f32("RSTD", 512)
            TA = [AR.f32(f"TA{i}", 512) for i in range(2)]
            XN = [AR.f32(f"XN{i}", 512) for i in range(2)]
            while len(XR) < 8 and ARENA_WORDS - AR.off >= 512:
                XR.append(AR.f32(f"XR{len(XR)}", 512))
            nxr = len(XR)
            kgroups = [(k0, min(4, nK - k0)) for k0 in range(0, nK, 4)]
            out_dst = dr["out"] if final else dr["xres"]
            S_.begin_region()
            for nt in range(NT):
                for (k0, nk) in kgroups:
                    ws = W.load([(0, (nk, 1024), wsrc.rearrange("(k p) n -> p k n", p=128)[:, k0:k0 + nk, :], "sync")])
                    ws3 = ws.ap[:, 0:nk * 1024].rearrange("p (k n) -> p k n", k=nk)
                    for j in range(KC):
                        for kk in range(nk):
                            kc = k0 + kk
                            op("tensor", lambda e, j=j, kk=kk, kc=kc, ws3=ws3: e.matmul(
                                bank(j), ws3[:, kk, j * 128:(j + 1) * 128], rhs3[:, kc, nt * 512:(nt + 1) * 512],
                                start=(kc == 0), stop=(kc == nK - 1)),
                               reads=[ws.b] + rhs_bufs, writes=[bPS[j]], inc=(kk == nk - 1))
                for j in range(KC):
                    xr = XR[(nt * KC + j) % nxr]
                    dma("sync", xr.ap, xres_src[j, :, nt * 512:(nt + 1) * 512], reads=xres_bufs[nt], writes=[xr.b])
                    op("vector", lambda e, j=j, xr=xr: e.scalar_tensor_tensor(SUM3[:, j, :], xr.ap, float(ALPHA), bank(j),
                                                                              ALU.mult, ALU.add),
                       reads=[xr.b, bPS[j]], writes=[SUMt.b])
                for j in range(KC):
                    op("tensor", lambda e, j=j: e.matmul(bank(0), ONES, SUM3[:, j, :], start=(j == 0), stop=(j == KC - 1)),
                       reads=[SUMt.b, bCM], writes=[bPS[0]], inc=(j == KC - 1))
                for j in range(KC):
                    sq = SQ[j % 2]
                    op("scalar", lambda e, j=j, sq=sq: e.activation(sq.ap, SUM3[:, j, :], AF.Square),
                       reads=[SUMt.b], writes=[sq.b])
                    op("tensor", lambda e, j=j, sq=sq: e.matmul(bank(1), ONES, sq.ap, start=(j == 0), stop=(j == KC - 1)),
                       reads=[sq.b, bCM], writes=[bPS[1]])
                op("vector", lambda e: e.tensor_scalar(MEAN.ap, bank(0), 1.0 / D, None, ALU.mult),
                   reads=[bPS[0]], writes=[MEAN.b])
                op("vector", lambda e: e.tensor_tensor(M2.ap, MEAN.ap, MEAN.ap, ALU.mult), reads=[MEAN.b], writes=[M2.b])
                op("vector", lambda e: e.scalar_tensor_tensor(RSTD.ap, bank(1), 1.0 / D, M2.ap, ALU.mult, ALU.subtract),
                   reads=[bPS[1], M2.b], writes=[RSTD.b])
                op("scalar", lambda e: e.activation(RSTD.ap, RSTD.ap, AF.Ln, bias=LN_EPS), reads=[RSTD.b], writes=[RSTD.b])
                op("scalar", lambda e: e.activation(RSTD.ap, RSTD.ap, AF.Exp, scale=-0.5), reads=[RSTD.b], writes=[RSTD.b])
                for j in range(KC):
                    ta = TA[j % 2]
                    xn = XN[j % 2]
                    op("vector", lambda e, j=j, ta=ta: e.tensor_tensor(ta.ap, SUM3[:, j, :], MEAN.ap, ALU.subtract),
                       reads=[SUMt.b, MEAN.b], writes=[ta.b])
                    op("vector", lambda e, ta=ta: e.tensor_tensor(ta.ap, ta.ap, RSTD.ap, ALU.mult),
                       reads=[ta.b, RSTD.b], writes=[ta.b])
                    op("scalar", lambda e, j=j, ta=ta, xn=xn: e.activation(xn.ap, ta.ap, AF.Identity,
                                                                           bias=pcol(l, boff + j), scale=pcol(l, goff + j)),
                       reads=[ta.b, bPP], writes=[xn.b])
                    if not final:
                        op("scalar", lambda e, j=j, ta=ta: e.activation(XB3[:, j, nt * 512:(nt + 1) * 512], ta.ap, AF.Identity,
                                                                        bias=pcol(l, boff + j), scale=pcol(l, goff + j)),
                           reads=[ta.b, bPP], writes=[bXB[nt]])
                    dma("sync", out_dst[j, :, nt * 512:(nt + 1) * 512], xn.ap, reads=[xn.b],
                        writes=[(bOUT if final else bXRES[nt])[j % 2]], sembuf=xn.b)
            S_.end_region()
            if not final:
                xres_src = dr["xres"]
                xres_bufs = bXRES

        if l == 0:
            tap("mrgD", MRG.ap, bMRG, [128, KC * S], BF16)
        S_.soft_barrier()
        AR.reset(mrg_end)
        outproj_ln(MRG3, bMRG, KC, dr["wob"], PP_L1G, PP_L1B, final=False)
        if l == 0:
            tap("xb1", XB[:, :], bXB, [128, KC * S], BF16)

        S_.soft_barrier()
        AR.reset()
        HB = AR.bf("HB", FK * S)
        HB3 = HB.ap.rearrange("p (k t) -> p k t", k=FK)
        bHB = S_.bufs("HBk", FK)
        hb_end = AR.off
        PADFs = [AR.f32(f"PADF{i}", 2050) for i in range(2)]
        ACCF = AR.f32("ACCF", 2048)
        GTt = AR.f32("GT", 2048)
        for PADF in PADFs:
            op("gpsimd", lambda e: e.memset(PADF.ap[:, 0:1], 0.0), writes=[PADF.b])
            op("gpsimd", lambda e: e.memset(PADF.ap[:, 2049:2050], 0.0), writes=[PADF.b])
        S_.begin_region()
        for q in range(6):
            ncol = 512 if q < 5 else 256
            wgs = W.load([(0, (KC, ncol), dr["w_up"][l].rearrange("(k p) n -> p k n", p=128)[:, :, 512 * q:512 * q + ncol])])
            wvs = W.load([(0, (KC, ncol), dr["w_up"][l].rearrange("(k p) n -> p k n", p=128)[:, :, DFF + 512 * q:DFF + 512 * q + ncol])])
            for kk in range(ncol // 128):
                k = 4 * q + kk
                for half_, wsl in enumerate((wgs, wvs)):
                    pb0 = 4 * half_
                    PADF = PADFs[half_ if DB_F else 0]
                    cch = k + FK * half_
                    proj_fm(wsl, 0, ncol, kk, pb0, bPS[pb0:pb0 + 4])
                    op("scalar", lambda e, pb0=pb0: e.copy(PADF.ap[:, 1:2049], bank(pb0, 4)),
                       reads=bPS[pb0:pb0 + 4], writes=[PADF.b])
                    AC = GTt if half_ == 0 else ACCF
                    op("scalar", lambda e, cch=cch, AC=AC: e.activation(AC.ap, PADF.ap[:, 0:2048], AF.Identity,
                                                                        bias=pcol(l, PP_FB + cch), scale=pcol(l, PP_FW + cch * 3)),
                       reads=[PADF.b, bPP], writes=[AC.b])
                    for t_ in range(1, 3):
                        op("vector", lambda e, t_=t_, cch=cch, AC=AC: e.scalar_tensor_tensor(
                            AC.ap, PADF.ap[:, t_:t_ + 2048], pcol(l, PP_FW + cch * 3 + t_), AC.ap, ALU.mult, ALU.add),
                           reads=[PADF.b, AC.b, bPP], writes=[AC.b])
                    if half_ == 0:
                        op("scalar", lambda e: e.activation(GTt.ap, GTt.ap, AF.Gelu), reads=[GTt.b], writes=[GTt.b])
                    else:
                        op("vector", lambda e, k=k: e.tensor_tensor(HB3[:, k, :], GTt.ap, ACCF.ap, ALU.mult),
                           reads=[GTt.b, ACCF.b], writes=[bHB[k]])
        S_.end_region()

        if l == 0:
            tap("hb", HB.ap, bHB, [128, FK * S], BF16)
        S_.soft_barrier()
        AR.reset(hb_end)
        outproj_ln(HB3, bHB, FK, dr["wdb"], PP_L2G, PP_L2B, final=(l == depth - 1))

    S_.barrier()
    return S_


def _consts():
    r = np.arange(128)[:, None]
    c = np.arange(128)[None, :]
    le = (r <= c).astype(np.float32)
    ge = (r >= c).astype(np.float32)
    gt = (r > c).astype(np.float32)
    lt = (r < c).astype(np.float32)
    ones = np.ones((128, 128), np.float32)
    cmask = np.concatenate([le, ge, gt, lt, ones], axis=1)
    cid = np.eye(128, dtype=np.float32)
    ctm = np.concatenate([le] * 8 + [ge] * 8, axis=1)
    crc = np.zeros((1, 64), np.float32)
    t = np.arange(S)
    for gi, w in enumerate(POOL_WINDOWS):
        half = w // 2
        cnt = np.minimum(t + half - 1, S - 1) - np.maximum(t - half, 0) + 1
        rc = (1.0 / cnt).astype(np.float32)
        crc[0, gi * 16: gi * 16 + half] = rc[:half]
        if half > 1:
            crc[0, gi * 16 + 8: gi * 16 + 8 + half - 1] = rc[S - (half - 1):]
    return cmask, cid, ctm, crc


def _pack_params(inp, depth):
    pp = np.zeros((128, depth, NPP), np.float32)
    bc = np.zeros((depth, NBC), np.float32)
    for l in range(depth):
        pp[:, l, PP_CW:PP_CW + 120] = inp["ssd_conv_w"][l].T.reshape(24, 128, 5).transpose(1, 0, 2).reshape(128, 120)
        pp[:, l, PP_CB:PP_CB + 24] = inp["ssd_conv_b"][l].reshape(24, 128).T
        pp[:, l, PP_FW:PP_FW + 132] = inp["ffn_conv_w"][l].T.reshape(44, 128, 3).transpose(1, 0, 2).reshape(128, 132)
        pp[:, l, PP_FB:PP_FB + 44] = inp["ffn_conv_b"][l].reshape(44, 128).T
        pp[:, l, PP_PS:PP_PS + 8] = inp["pool_scale"][l].reshape(8, 128).T
        pp[:, l, PP_L1G:PP_L1G + 8] = inp["ln1_g"][l].reshape(8, 128).T
        pp[:, l, PP_L1B:PP_L1B + 8] = inp["ln1_b"][l].reshape(8, 128).T
        pp[:, l, PP_L2G:PP_L2G + 8] = inp["ln2_g"][l].reshape(8, 128).T
        pp[:, l, PP_L2B:PP_L2B + 8] = inp["ln2_b"][l].reshape(8, 128).T
        bc[l, BC_ALOG:BC_ALOG + 64] = inp["a_log"][l].reshape(64)
        bc[l, BC_DTB:BC_DTB + 64] = inp["dt_bias"][l].reshape(64)
        bc[l, BC_DSK:BC_DSK + 32] = inp["d_skip"][l]
        bc[l, BC_NG:BC_NG + 2048] = inp["ssd_norm_g"][l]
    return np.ascontiguousarray(pp.reshape(128, depth * NPP)), bc


def run(inputs, depth=DEPTH, n_cores=8, trace=False):
    inp = {k: np.asarray(v, dtype=np.float32) for k, v in inputs.items()}
    x = inp["x"]
    nb = x.shape[0]
    cmask, cid, ctm, crc = _consts()
    pp, bc = _pack_params(inp, depth)
    shared = {
        "w_in": np.ascontiguousarray(inp["w_in"][:depth]),
        "pool_w": np.ascontiguousarray(inp["pool_w"][:depth]),
        "w_ssd_proj": np.ascontiguousarray(inp["w_ssd_proj"][:depth]),
        "w_out": np.ascontiguousarray(inp["w_out"][:depth]),
        "w_up": np.ascontiguousarray(inp["w_up"][:depth]),
        "w_down": np.ascontiguousarray(inp["w_down"][:depth]),
        "pp": pp, "bc": bc, "cmask": cmask, "cid": cid, "ctm": ctm, "crc": crc,
        "cid8": np.ascontiguousarray(np.tile(cid, (1, 8))),
    }
    in_maps = []
    for b in range(nb):
        m = dict(shared)
        m["xT"] = np.ascontiguousarray(x[b].T).reshape(KC, 128, S)
        in_maps.append(m)
    nc = build_program(depth)
    res = run_bass_kernel_spmd(nc, in_maps, core_ids=list(range(nb)), trace=trace)
    global LAST_DBG
    LAST_DBG = {k: np.asarray(res.results[0]["dbg_" + k]) for k in DEBUG_TAPS}
    outs = [np.asarray(r["out"]).reshape(D, S).T for r in res.results]
    return np.ascontiguousarray(np.stack(outs, axis=0).astype(np.float32)), res


def kernel(**inputs):
    out, _ = run(inputs, depth=DEPTH)
    return out
```

```python
import numpy as np
import concourse.bass as bass
import concourse.mybir as mybir
from concourse.bass_utils import run_bass_kernel_spmd
from contextlib import ExitStack

F32 = mybir.dt.float32
BF16 = mybir.dt.bfloat16
F32R = mybir.dt.float32r
AF = mybir.ActivationFunctionType
ALU = mybir.AluOpType

D = 1024
KC = 8
S = 2048
NT = 4
DEPTH = 4
NCH = 16
DFF = 2816
FK = 22
U0, Z0, XBC0, DT0, G0 = 0, 1024, 3072, 6144, 6208
ALPHA = (2 * DEPTH) ** 0.25
LN_EPS = 1e-5
RMS_EPS = 1e-5
POOL_WINDOWS = (2, 4, 8, 16)

PP_CW = 0
PP_CB = PP_CW + 120
PP_FW = PP_CB + 24
PP_FB = PP_FW + 132
PP_PS = PP_FB + 44
PP_L1G = PP_PS + 8
PP_L1B = PP_L1G + 8
PP_L2G = PP_L1B + 8
PP_L2B = PP_L2G + 8
NPP = PP_L2B + 8
BC_ALOG = 0
BC_DTB = 64
BC_DSK = 128
BC_NG = 160
NBC = BC_NG + 2048
NBC_SB = BC_NG + 512

SEM_GEN = 30000
PIPELINE_B = True
USE_REGIONS = True
SOFT_BARRIERS = False
SCHED_EPS = 3.0
LAT_TAIL = 0.7
WIN_REGION = 1500
NONCE = ""
DB_A = True
DB_F = True
JK_BF = True
ARENA_WORDS = 32256


class Buf:
    __slots__ = ("name", "w", "r", "dsem", "dcount", "excl")

    def __init__(self, name, excl=False):
        self.name = name
        self.w = None
        self.r = []
        self.dsem = None
        self.dcount = 0
        self.excl = excl


class Eng:
    def __init__(self, name):
        self.name = name
        self.count = 0
        self.pending = False
        self.ops = []
        self.waited = {}

    def semkey(self, cnt):
        return ("E", self.name, (cnt - 1) // SEM_GEN)


class _Rec:
    def __init__(self):
        self.call = None

    def __getattr__(self, name):
        def f(*a, **k):
            self.call = (name, a, k)
            return None
        return f


class Sched:
    def __init__(self, nc, dry=False):
        self.nc = nc
        self.dry = dry
        self.eng = {n: Eng(n) for n in ("tensor", "vector", "scalar", "gpsimd", "sync")}
        self.semkeys = {}
        self.nbuf = 0
        self.dma_out = []
        self.fence_toks = []
        self.fence_dma = {}
        self.dsem_count = {}
        self.region = None
        self._pend = {}

    def buf(self, name=None, excl=False):
        self.nbuf += 1
        b = Buf(name or f"b{self.nbuf}", excl)
        b.r = list(self.fence_toks)
        return b

    def bufs(self, name, n, excl=False):
        return [self.buf(f"{name}{i}", excl) for i in range(n)]

    @staticmethod
    def _tok_local(tok):
        key, val = tok
        if key[0] == "E":
            return key, val - key[2] * SEM_GEN
        return key, val

    def _need(self, e, toks):
        best = {}
        for t in toks:
            if t is None:
                continue
            if t[0][0] == "E" and t[0][1] == e.name and t[1] > e.count:
                continue
            key, val = self._tok_local(t)
            if e.waited.get(key, 0) >= val:
                continue
            if best.get(key, 0) < val:
                best[key] = val
        for k, v in best.items():
            e.waited[k] = v
            self.semkeys[k] = None
        return list(best.items())

    def op(self, engname, fn, reads=(), writes=(), inc=True):
        if self.dry:
            return None
        rec = _Rec()
        fn(rec)
        if self.region is not None:
            pend = self._pend.setdefault(engname, [])
            pend.append((rec.call, list(reads), list(writes)))
            if inc:
                self.region.append(("op", engname, pend))
                self._pend[engname] = []
            return None
        return self._op_core(engname, rec.call, reads, writes, inc)

    def _op_core(self, engname, call, reads, writes, inc):
        e = self.eng[engname]
        xr = [b for b in reads if b.excl]
        if xr:
            reads = [b for b in reads if not b.excl]
            writes = list(writes) + [b for b in xr if b not in writes]
        deps = []
        for b in reads:
            deps.append(b.w)
        for b in writes:
            deps.append(b.w)
            deps.extend(b.r)
        waits = self._need(e, deps)
        if inc:
            e.count += 1
            e.pending = False
            tokval = e.count
        else:
            e.pending = True
            tokval = e.count + 1
        key = e.semkey(tokval)
        tok = (key, tokval)
        self.semkeys[key] = None
        for b in reads:
            b.r.append(tok)
        for b in writes:
            b.w = tok
            b.r = []
        e.ops.append((waits, call, key if inc else None, None))
        return tok

    def dma(self, engname, out_ap, in_ap, reads=(), writes=(), sembuf=None, track=True):
        if self.dry:
            return None
        if self.region is not None:
            self.region.append(("dma", engname, (out_ap, in_ap, list(reads), list(writes), sembuf, track)))
            return None
        return self._dma_core(engname, out_ap, in_ap, reads, writes, sembuf, track)

    def _dma_core(self, engname, out_ap, in_ap, reads, writes, sembuf, track):
        e = self.eng[engname]
        sb = sembuf or writes[0]
        if sb.dsem is None:
            sb.dsem = ("D", sb.name)
        deps = []
        for b in reads:
            deps.append(b.w)
        for b in writes:
            deps.append(b.w)
            deps.extend(b.r)
        waits = self._need(e, deps)
        self.dsem_count[sb.dsem] = self.dsem_count.get(sb.dsem, 0) + 1
        tok = (sb.dsem, 16 * self.dsem_count[sb.dsem])
        self.semkeys[sb.dsem] = None
        for b in reads:
            b.r.append(tok)
        for b in writes:
            b.w = tok
            b.r = []

        e.ops.append((waits, ("dma_start", (), {"out": out_ap, "in_": in_ap}), None, sb.dsem))
        if track:
            self.dma_out.append(tok)
        return tok

    def begin_region(self):
        if self.dry or not USE_REGIONS:
            return
        assert self.region is None
        self.region = []
        self._pend = {}

    @staticmethod
    def _free(ap):
        n = 1
        for d in ap.shape[1:]:
            n *= d
        return n

    def _est(self, rec):
        kind, engname, body = rec
        if kind == "dma":
            return 0.15
        t = 0.0
        for call, _, _ in body:
            name, a, k = call
            out = a[0] if a else k.get("out")
            try:
                n = self._free(out)
            except Exception:
                n = 512
            if engname == "tensor":
                if name == "matmul":
                    d = max(n, 64) / 2400.0 + 0.03
                    if a[1].dtype == F32:
                        d *= 4.0
                    t += d
                else:
                    t += 0.1
            elif engname == "vector":
                t += (n + 150) / 960.0
            elif engname == "scalar":
                t += (n + 224) / 1200.0
            elif engname == "gpsimd":
                t += (2 * n + 150) / 960.0
            else:
                t += 0.1
        return t

    def end_region(self):
        if self.dry or not USE_REGIONS:
            return
        recs = self.region
        self.region = None
        for en, p in self._pend.items():
            assert not p, f"pending un-inc'd ops on {en} at region end"
        n = len(recs)
        last_w = {}
        readers = {}
        deps = [set() for _ in range(n)]
        for i, (kind, engname, body) in enumerate(recs):
            if kind == "dma":
                rr, ww = body[2], body[3]
            else:
                rr = [b for c in body for b in c[1]]
                ww = [b for c in body for b in c[2]]
            R = [b for b in rr if not b.excl]
            Wr = list(ww) + [b for b in rr if b.excl]
            for b in R:
                if id(b) in last_w:
                    deps[i].add(last_w[id(b)])
            for b in Wr:
                if id(b) in last_w:
                    deps[i].add(last_w[id(b)])
                deps[i].update(readers.get(id(b), ()))
            for b in R:
                readers.setdefault(id(b), []).append(i)
            for b in Wr:
                last_w[id(b)] = i
                readers[id(b)] = []
            deps[i].discard(i)
        dur = [self._est(r) for r in recs]
        succ = [[] for _ in range(n)]
        indeg = [0] * n
        for i in range(n):
            for d in deps[i]:
                succ[d].append(i)
            indeg[i] = len(deps[i])
        tail = [0.0] * n
        for i in range(n - 1, -1, -1):
            m = 0.0
            for j in succ[i]:
                if tail[j] > m:
                    m = tail[j]
            tail[i] = dur[i] + (m + LAT_TAIL if succ[i] else 0.0)
        ready_t = [0.0] * n
        ready = [i for i in range(n) if indeg[i] == 0]
        eng_free = {}
        order = []
        LAT = 1.2
        WIN = 48
        done = [False] * n
        lo = 0
        while ready:
            while lo < n and done[lo]:
                lo += 1
            cands = []
            emin = None
            for i in ready:
                if i > lo + WIN_REGION:
                    continue
                est = max(eng_free.get(recs[i][1], 0.0), ready_t[i])
                cands.append((est, i))
                if emin is None or est < emin:
                    emin = est
            if not cands:
                i = min(ready)
                est = max(eng_free.get(recs[i][1], 0.0), ready_t[i])
            else:
                best = None
                for est_i, i_ in cands:
                    if est_i <= emin + SCHED_EPS:
                        key = (-tail[i_], est_i, i_)
                        if best is None or key < best:
                            best = key
                i = best[2]
                est = best[1]
            ready.remove(i)
            done[i] = True
            fin = est + dur[i]
            eng_free[recs[i][1]] = fin
            if recs[i][0] == "dma":
                fin += 2.0
            order.append(i)
            for j in succ[i]:
                indeg[j] -= 1
                ready_t[j] = max(ready_t[j], fin + LAT)
                if indeg[j] == 0:
                    ready.append(j)
        assert len(order) == n
        for i in order:
            kind, engname, body = recs[i]
            if kind == "dma":
                self._dma_core(engname, *body)
            else:
                for k, (call, rr, ww) in enumerate(body):
                    self._op_core(engname, call, rr, ww, k == len(body) - 1)

    def soft_barrier(self):
        if self.dry:
            return
        if not SOFT_BARRIERS:
            return self.barrier()
        assert self.region is None
        for t in self.dma_out:
            if self.fence_dma.get(t[0], 0) < t[1]:
                self.fence_dma[t[0]] = t[1]
        self.dma_out = []
        toks = list(self.fence_dma.items())
        for e in self.eng.values():
            if e.pending:
                raise RuntimeError(f"engine {e.name} pending at soft barrier")
            if e.count > 0:
                toks.append((e.semkey(e.count), e.count))
        self.fence_toks = toks

    def barrier(self):
        if self.dry:
            return
        assert self.region is None
        toks = list(self.dma_out)
        self.dma_out = []
        for e in self.eng.values():
            if e.pending:
                raise RuntimeError(f"engine {e.name} pending at barrier")
            if e.count > 0:
                toks.append((e.semkey(e.count), e.count))
        for e in self.eng.values():
            waits = self._need(e, toks)
            if waits:
                e.ops.append((waits, None, None, None))

    def emit(self, stack):
        nc = self.nc
        sems = {}
        print(f"[sched] {len(self.semkeys)} semaphores", flush=True)
        for i, k in enumerate(self.semkeys):
            sems[k] = stack.enter_context(nc.semaphore(f"s{i}"))
        for e in self.eng.values():
            if e.pending:
                raise RuntimeError(f"engine {e.name} ends pending")
        block = stack.enter_context(nc.Block())

        def runner(e):
            def body(eng):
                for waits, fn, inckey, dsem in e.ops:
                    for k, v in waits:
                        eng.wait_ge(sems[k], v)
                    if fn is None:
                        continue
                    ins = getattr(eng, fn[0])(*fn[1], **fn[2])
                    if inckey is not None:
                        ins.then_inc(sems[inckey], 1)
                    elif dsem is not None:
                        ins.then_inc(sems[dsem], 16)
            return body

        block.tensor(runner(self.eng["tensor"]))
        block.vector(runner(self.eng["vector"]))
        block.scalar(runner(self.eng["scalar"]))
        block.gpsimd(runner(self.eng["gpsimd"]))
        block.sync(runner(self.eng["sync"]))


class T:
    __slots__ = ("ap", "b")

    def __init__(self, ap, b):
        self.ap = ap
        self.b = b


class Arena:
    def __init__(self, S_, arena_ap):
        self.S = S_
        self.arena = arena_ap
        self.off = 0

    def reset(self, off=0):
        self.off = off

    def f32(self, name, n):
        a = self.arena[:, self.off:self.off + n]
        self.off += n
        assert self.off <= ARENA_WORDS, (name, self.off)
        return T(a, self.S.buf(name))

    def bf(self, name, n):
        w = (n + 1) // 2
        a = self.arena[:, self.off:self.off + w].bitcast(BF16)
        self.off += w
        assert self.off <= ARENA_WORDS, (name, self.off)
        return T(a, self.S.buf(name))


DEBUG_TAPS = {}
DEBUG_ON = set()


def build_program(depth=DEPTH):
    nc = bass.Bass("TRN2", target_bir_lowering=False)
    dr = {}
    dr["xT"] = nc.dram_tensor("xT", [KC, 128, S], F32, kind="ExternalInput").ap()
    dr["w_in"] = nc.dram_tensor("w_in", [depth, D, 8256], F32, kind="ExternalInput").ap()
    dr["pool_w"] = nc.dram_tensor("pool_w", [depth, 4, 256, 256], F32, kind="ExternalInput").ap()
    dr["w_ssd_proj"] = nc.dram_tensor("w_ssd_proj", [depth, 2048, D], F32, kind="ExternalInput").ap()
    dr["w_out"] = nc.dram_tensor("w_out", [depth, D, D], F32, kind="ExternalInput").ap()
    dr["w_up"] = nc.dram_tensor("w_up", [depth, D, 2 * DFF], F32, kind="ExternalInput").ap()
    dr["w_down"] = nc.dram_tensor("w_down", [depth, DFF, D], F32, kind="ExternalInput").ap()
    dr["pp"] = nc.dram_tensor("pp", [128, depth * NPP], F32, kind="ExternalInput").ap()
    dr["bc"] = nc.dram_tensor("bc", [depth, NBC], F32, kind="ExternalInput").ap()
    dr["cmask"] = nc.dram_tensor("cmask", [128, 5 * 128], F32, kind="ExternalInput").ap()
    dr["cid"] = nc.dram_tensor("cid", [128, 128], F32, kind="ExternalInput").ap()
    dr["ctm"] = nc.dram_tensor("ctm", [128, 16 * 128], F32, kind="ExternalInput").ap()
    dr["crc"] = nc.dram_tensor("crc", [1, 64], F32, kind="ExternalInput").ap()
    dr["cid8"] = nc.dram_tensor("cid8", [128, 1024], F32, kind="ExternalInput").ap()
    dr["out"] = nc.dram_tensor("out", [KC, 128, S], F32, kind="ExternalOutput").ap()
    dr["xres"] = nc.dram_tensor("xres", [KC, 128, S], F32, kind="Internal").ap()
    dr["yn"] = nc.dram_tensor("yn", [16, 128, S], BF16, kind="Internal").ap()
    dr["wob"] = nc.dram_tensor("wob", [D, D], BF16, kind="Internal").ap()
    dr["wdb"] = nc.dram_tensor("wdb", [DFF, D], BF16, kind="Internal").ap()

    with ExitStack() as st:
        plan = []
        _emit_all(nc, st, dr, depth, dry=True, plan=plan, alloc=None)
        alloc = {}
        S_ = _emit_all(nc, st, dr, depth, dry=False, plan=plan, alloc=alloc)
        S_.emit(st)
    return nc


class WMgr:
    def __init__(self, S_, slots, plan, dry):
        self.S = S_
        self.slots = slots
        self.plan = plan
        self.dry = dry
        self.i = 0
        self.issued = 0

    def _issue(self, idx):
        spec = self.plan[idx]
        slot = self.slots[idx % len(self.slots)]
        for item in spec:
            o0, shape3, src = item[0], item[1], item[2]
            eng = item[3] if len(item) > 3 else "gpsimd"
            n = shape3[0] * shape3[1]
            dst = slot.ap[:, o0:o0 + n].rearrange("p (a b) -> p a b", a=shape3[0])
            self.S.dma(eng, dst, src, writes=[slot.b], track=False)

    def load(self, spec):
        if self.dry:
            self.plan.append(spec)
            self.i += 1
            return self.slots[(self.i - 1) % len(self.slots)]
        idx = self.i
        self.i += 1
        while self.issued < min(idx + 2, len(self.plan)):
            self._issue(self.issued)
            self.issued += 1
        return self.slots[idx % len(self.slots)]


def _emit_all(nc, st, dr, depth, dry, plan, alloc):
    S_ = Sched(nc, dry=dry)
    if dry:
        class _Fake:
            def __getitem__(self, k):
                return self

            def rearrange(self, *a, **k):
                return self

            def bitcast(self, *a):
                return self

            def unsqueeze(self, *a):
                return self

            def to_broadcast(self, *a):
                return self
        fake = _Fake()

        def sbt(name, shape, dt):
            return fake

        def pst(name, shape, dt):
            return fake
    else:
        def sbt(name, shape, dt):
            return st.enter_context(nc.sbuf_tensor(name, shape, dt))

        def pst(name, shape, dt):
            return st.enter_context(nc.psum_tensor(name, shape, dt))

    op = S_.op
    dma = S_.dma

    def tap(name, ap, bufs, shape, dt):
        if name not in DEBUG_ON or dry:
            return
        d_ = nc.dram_tensor("dbg_" + name, list(shape), dt, kind="ExternalOutput").ap()
        DEBUG_TAPS[name] = d_
        dma("sync", d_, ap, reads=list(bufs), writes=[S_.buf("dbg_" + name)])

    CM = sbt("CM" + NONCE, [128, 5 * 128], F32)
    bCM = S_.buf("CM")
    LE, GE, GT_, LT_, ONES = (CM[:, i * 128:(i + 1) * 128] for i in range(5))
    IDF = sbt("IDF", [128, 128], F32)
    bIDF = S_.buf("IDF")
    IDB = sbt("IDB", [128, 128], BF16)
    bIDB = S_.buf("IDB")
    RB = sbt("RB", [128, 2048], F32R)
    bRB = S_.buf("RB")
    RB3 = RB[:, :].rearrange("p (h l) -> p h l", h=16)
    GLR = sbt("GLR", [128, 256], F32R)
    bGLR = S_.buf("GLR")
    RCN = sbt("RCN", [128, 64], F32)
    bRCN = S_.buf("RCN")
    PP = sbt("PP", [128, depth * NPP], F32)
    bPP = S_.buf("PP")
    BC = sbt("BC", [128, NBC_SB], F32)
    bNG = S_.buf("BCng")
    bBC = S_.buf("BC")
    XB = sbt("XB", [128, KC * S], BF16)
    XB3 = XB[:, :].rearrange("p (k t) -> p k t", k=KC)
    bXB = S_.bufs("XB", NT)
    slots = [T(sbt(f"WS{i}", [128, 4096], BF16)[:, :], S_.buf(f"WS{i}")) for i in range(3)]
    ARENA = sbt("ARENA", [128, ARENA_WORDS], F32)
    AR = Arena(S_, ARENA)
    PS = pst("PS", [128, 8 * 512], F32)
    bPS = S_.bufs("PSB", 8, excl=True)
    W = WMgr(S_, slots, plan, dry)

    def bank(i, n=1):
        return PS[:, i * 512:(i + n) * 512]

    dma("sync", CM[:, :], dr["cmask"], writes=[bCM])
    dma("sync", IDF[:, :], dr["cid"], writes=[bIDF])
    dma("gpsimd", IDB[:, :], dr["cid"], writes=[bIDB])
    dma("sync", RCN[:, :], dr["crc"].to_broadcast([128, 64]), writes=[bRCN])
    dma("sync", PP[:, :], dr["pp"], writes=[bPP])
    dma("gpsimd", GLR[:, :], dr["cmask"][:, 256:512], writes=[bGLR])
    for nt in range(NT):
        dma("gpsimd", XB3[:, :, nt * 512:(nt + 1) * 512],
            dr["xT"].rearrange("k p t -> p k t")[:, :, nt * 512:(nt + 1) * 512], writes=[bXB[nt]])

    def pcol(l, off):
        return PP[:, l * NPP + off: l * NPP + off + 1]

    def win_src(l, c0, ncols):
        return dr["w_in"][l].rearrange("(k p) n -> p k n", p=128)[:, :, c0:c0 + ncols]

    def proj_fm(slot, so, ncols_in_slot, cidx, psbanks, pbufs):
        w3 = slot.ap[:, so:so + KC * ncols_in_slot].rearrange("p (k n) -> p k n", k=KC)
        for nt in range(NT):
            for kc in range(KC):
                last = (nt == NT - 1 and kc == KC - 1)
                op("tensor",
                   lambda e, nt=nt, kc=kc: e.matmul(bank(psbanks + nt), w3[:, kc, cidx * 128:(cidx + 1) * 128],
                                                    XB3[:, kc, nt * 512:(nt + 1) * 512],
                                                    start=(kc == 0), stop=(kc == KC - 1)),
                   reads=[slot.b] + bXB, writes=pbufs, inc=last)

    xres_src = dr["xT"]
    xres_bufs = [S_.bufs(f"xT{i}_", 2) for i in range(NT)]
    bXRES = [S_.bufs(f"xres{i}_", 2) for i in range(NT)]
    bYN = S_.bufs("yn", 2)
    bWOB = S_.buf("wob")
    bWDB = S_.buf("wdb")
    bOUT = S_.bufs("out", 2)

    for l in range(depth):
        S_.soft_barrier()
        AR.reset()
        SCR1 = AR.f32("SCR1", 2052)
        PAD2 = AR.f32("PAD2", 2052)
        SCR2 = AR.f32("SCR2", 2048)
        SCR3 = AR.bf("SCR3", 2048)
        MB2 = AR.bf("MB2", 2048)
        XTOK = AR.bf("XTOK", 16 * 512)
        bXTOK = S_.bufs("XTOKc", 16)
        XTOK3 = XTOK.ap.rearrange("p (c n) -> p c n", c=16)
        BTOK = AR.bf("BTOK", 16 * 128)
        bBTOK = S_.bufs("BTOKc", 16)
        BTOK3 = BTOK.ap.rearrange("p (c n) -> p c n", c=16)
        BT = AR.bf("BT", 2048)
        CT = AR.bf("CT", 2048)
        HPREV = AR.bf("HPREV", 16 * 512)
        bHPREV = S_.bufs("HPREVc", 16)
        HPREV3 = HPREV.ap.rearrange("p (c n) -> p c n", c=16)
        XDTFs = [AR.bf(f"XDTF{i}", 512) for i in range(2)]
        XDTBs = [AR.bf(f"XDTB{i}", 512) for i in range(2)]
        XDD = AR.bf("XDD", 512)
        HS = AR.f32("HS", 512)
        HBb = AR.bf("HBb", 512)
        DTt = AR.f32("DT", 1024)
        ADT = AR.f32("ADT", 1024)
        ACUM = T(SCR2.ap[:, 0:1024], SCR2.b)
        EXPA = AR.f32("EXPA", 1024)
        DTDE = AR.f32("DTDE", 1024)
        DECC = AR.f32("DECC", 1024)
        EA = AR.f32("EA", 64)
        SFB = AR.f32("SFB", 256)
        acc2_off = AR.off
        T1 = AR.f32("T1", 512)
        T2 = AR.f32("T2", 512)
        YT = AR.f32("YT", 512)
        SZ = AR.f32("SZ", 512)
        ACC2ap = ARENA[:, acc2_off:acc2_off + 2048]
        ACC2b = [T1.b, T2.b, YT.b, SZ.b]
        VV = AR.f32("V", 512)
        V2 = AR.f32("V2", 512)
        JK = AR.bf("JK", 512) if JK_BF else AR.f32("JK", 512)
        SS = AR.f32("SS", 4)
        VN = AR.bf("VN", 512)
        YNT = [AR.bf(f"YNT{i}", 512) for i in range(2)]
        ID8 = AR.bf("ID8", 1024)
        DIg = AR.bf("DIg", 1024)
        DI3 = DIg.ap.rearrange("p (h l) -> p h l", h=8)

        def v3(ap, c):
            return ap[:, c * 64:(c + 1) * 64]

        def hd_bc(t, c, h0):
            return t.ap[:, c * 64 + h0: c * 64 + h0 + 8].unsqueeze(2).to_broadcast([128, 8, 64])

        dma("sync", BC[:, 0:BC_NG], dr["bc"][l:l + 1, 0:BC_NG].to_broadcast([128, BC_NG]), writes=[bBC])
        dma("gpsimd", ID8.ap, dr["cid8"], writes=[ID8.b])
        wdt = W.load([(0, (KC, 64), win_src(l, DT0, 64))])
        wdt3 = wdt.ap[:, 0:KC * 64].rearrange("p (k n) -> p k n", k=KC)
        DTP = PS[:, 4 * 512: 4 * 512 + 1024]
        ACP = PS[:, 6 * 512: 6 * 512 + 1024]
        ATP = PS[:, 2 * 512: 2 * 512 + 1024]
        for i in range(NCH):
            for kc in range(KC):
                op("tensor", lambda e, i=i, kc=kc: e.matmul(DTP[:, i * 64:(i + 1) * 64],
                                                          XB3[:, kc, i * 128:(i + 1) * 128], wdt3[:, kc, :],
                                                          start=(kc == 0), stop=(kc == KC - 1)),
                   reads=[wdt.b] + bXB, writes=[bPS[4], bPS[5]], inc=(kc == KC - 1 and i == NCH - 1))
        op("vector", lambda e: e.tensor_tensor(DTt.ap.rearrange("p (c h) -> p c h", c=16),
                                               DTP.rearrange("p (c h) -> p c h", c=16),
                                               BC[:, BC_DTB:BC_DTB + 64].unsqueeze(1).to_broadcast([128, 16, 64]),
                                               ALU.add),
           reads=[bPS[4], bPS[5], bBC], writes=[DTt.b])
        op("scalar", lambda e: e.activation(DTt.ap, DTt.ap, AF.Exp), reads=[DTt.b], writes=[DTt.b])
        op("scalar", lambda e: e.activation(DTt.ap, DTt.ap, AF.Ln, bias=1.0), reads=[DTt.b], writes=[DTt.b])
        op("scalar", lambda e: e.activation(EA.ap, BC[:, BC_ALOG:BC_ALOG + 64], AF.Exp), reads=[bBC], writes=[EA.b])
        op("vector", lambda e: e.scalar_tensor_tensor(ADT.ap.rearrange("p (c h) -> p c h", c=16),
                                                      DTt.ap.rearrange("p (c h) -> p c h", c=16), -1.0,
                                                      EA.ap.unsqueeze(1).to_broadcast([128, 16, 64]),
                                                      ALU.mult, ALU.mult),
           reads=[DTt.b, EA.b], writes=[ADT.b])
        for c in range(NCH):
            op("tensor", lambda e, c=c: e.matmul(ACP[:, c * 64: c * 64 + 32], LE, ADT.ap[:, c * 64: c * 64 + 32],
                                                 start=True, stop=True),
               reads=[ADT.b, bCM], writes=[bPS[6], bPS[7]], inc=False)
            op("tensor", lambda e, c=c: e.matmul(ACP[:, c * 64 + 32: c * 64 + 64], GE,
                                                 ADT.ap[:, c * 64 + 32: c * 64 + 64], start=True, stop=True),
               reads=[ADT.b, bCM], writes=[bPS[6], bPS[7]], inc=False)
            op("tensor", lambda e, c=c: e.matmul(ATP[:, c * 64: c * 64 + 64], ONES, ADT.ap[:, c * 64: c * 64 + 64],
                                                 start=True, stop=True),
               reads=[ADT.b, bCM], writes=[bPS[2], bPS[3]], inc=(c == NCH - 1))
        op("scalar", lambda e: e.copy(ACUM.ap, ACP), reads=[bPS[6], bPS[7]], writes=[ACUM.b])
        op("scalar", lambda e: e.activation(EXPA.ap, ACUM.ap, AF.Exp), reads=[ACUM.b], writes=[EXPA.b])
        op("scalar", lambda e: e.activation(DECC.ap, ATP, AF.Exp), reads=[bPS[2], bPS[3]], writes=[DECC.b])
        op("vector", lambda e: e.tensor_tensor(DTDE.ap, ATP, ACUM.ap, ALU.subtract),
           reads=[bPS[2], bPS[3], ACUM.b], writes=[DTDE.b])
        op("scalar", lambda e: e.activation(DTDE.ap, DTDE.ap, AF.Exp), reads=[DTDE.b], writes=[DTDE.b])
        op("vector", lambda e: e.tensor_tensor(DTDE.ap, DTDE.ap, DTt.ap, ALU.mult),
           reads=[DTDE.b, DTt.b], writes=[DTDE.b])
        op("gpsimd", lambda e: e.memset(SCR1.ap[:, 0:2], 0.0), writes=[SCR1.b])
        op("gpsimd", lambda e: e.memset(SCR1.ap[:, 2050:2052], 0.0), writes=[SCR1.b])
        op("gpsimd", lambda e: e.memset(PAD2.ap[:, 0:2], 0.0), writes=[PAD2.b])
        op("gpsimd", lambda e: e.memset(PAD2.ap[:, 2050:2052], 0.0), writes=[PAD2.b])

        for g in range(4):
            S_.begin_region()
            wx = W.load([(0, (KC, 512), win_src(l, XBC0 + 512 * g, 512))])
            wbc = W.load([(0, (KC, 128), win_src(l, XBC0 + 2048 + 128 * g, 128)),
                          (KC * 128, (KC, 128), win_src(l, XBC0 + 2560 + 128 * g, 128))])
            if l == 0 and g == 0:
                tap("wx", wx.ap, [wx.b], [128, 4096], BF16)
            if g > 0:
                op("gpsimd", lambda e: e.memset(PAD2.ap[:, 0:2], 0.0), writes=[PAD2.b])
            tpi = 0

            def projA(cc):
                if cc < 4:
                    proj_fm(wx, 0, 512, cc, 0, bPS[0:4])
                elif cc == 4:
                    proj_fm(wbc, 0, 128, 0, 0, bPS[0:4])
                else:
                    proj_fm(wbc, KC * 128, 128, 0, 0, bPS[0:4])

            projA(0)
            for cc in range(6):
                cch = (4 * g + cc) if cc < 4 else ((16 + g) if cc == 4 else (20 + g))
                pad = SCR1 if (cc % 2 == 0 or not DB_A) else PAD2
                acc_ap = SCR2.ap if (cc % 2 == 0 or not DB_A) else ACC2ap
                acc_b = [SCR2.b] if (cc % 2 == 0 or not DB_A) else ACC2b
                op("scalar", lambda e, pad=pad: e.copy(pad.ap[:, 2:2050], bank(0, 4)), reads=bPS[0:4], writes=[pad.b])
                if cc < 5:
                    projA(cc + 1)
                op("scalar", lambda e, cch=cch, pad=pad, acc_ap=acc_ap: e.activation(
                    acc_ap, pad.ap[:, 0:2048], AF.Identity, bias=pcol(l, PP_CB + cch), scale=pcol(l, PP_CW + cch * 5)),
                   reads=[pad.b, bPP], writes=acc_b)
                for k in range(1, 5):
                    op("vector", lambda e, k=k, cch=cch, pad=pad, acc_ap=acc_ap: e.scalar_tensor_tensor(
                        acc_ap, pad.ap[:, k:k + 2048], pcol(l, PP_CW + cch * 5 + k), acc_ap, ALU.mult, ALU.add),
                       reads=[pad.b, bPP] + acc_b, writes=acc_b)
                dst = SCR3 if cc < 4 else (BT if cc == 4 else CT)
                op("scalar", lambda e, dst=dst, acc_ap=acc_ap: e.activation(dst.ap, acc_ap, AF.Silu),
                   reads=acc_b, writes=[dst.b])
                if l == 0 and g == 0 and cc == 0:
                    tap("xs0", SCR3.ap, [SCR3.b], [128, 2048], BF16)
                    tap("pad0", SCR1.ap, [SCR1.b], [128, 2052], F32)
                if cc < 5:
                    for q4 in range(4):
                        pb = 4 + (tpi % 2)
                        tpi += 1
                        TPv = bank(pb)[:, 0:256].bitcast(BF16)
                        for q in range(4):
                            i = q4 * 4 + q
                            op("tensor", lambda e, dst=dst, i=i, q=q, TPv=TPv: e.transpose(
                                TPv[:, q * 128:(q + 1) * 128], dst.ap[:, i * 128:(i + 1) * 128], IDB[:, :]),
                               reads=[dst.b, bIDB], writes=[bPS[pb]], inc=(q == 3))
                        if cc < 4:
                            op("scalar", lambda e, q4=q4, cc=cc, TPv=TPv: e.copy(
                                XTOK3[:, q4 * 4:q4 * 4 + 4, cc * 128:(cc + 1) * 128],
                                TPv.rearrange("p (a b) -> p a b", a=4)),
                               reads=[bPS[pb]], writes=bXTOK[q4 * 4:q4 * 4 + 4])
                        else:
                            op("scalar", lambda e, q4=q4, TPv=TPv: e.copy(
                                BTOK3[:, q4 * 4:q4 * 4 + 4, :], TPv.rearrange("p (a b) -> p a b", a=4)),
                               reads=[bPS[pb]], writes=bBTOK[q4 * 4:q4 * 4 + 4])

            wz = W.load([(0, (KC, 512), win_src(l, Z0 + 512 * g, 512))])
            wz3 = wz.ap[:, 0:KC * 512].rearrange("p (k n) -> p k n", k=KC)
            ko0, ko1 = 2 * g, 2 * g + 2
            dma("gpsimd", dr["wob"].rearrange("(k p) n -> p k n", p=128)[:, ko0:ko1, :],
                dr["w_out"][l].rearrange("(k p) n -> p k n", p=128)[:, ko0:ko1, :], writes=[bWOB])
            kd0, kd1 = 6 * g, min(6 * g + 6, FK)
            dma("gpsimd", dr["wdb"].rearrange("(k p) n -> p k n", p=128)[:, kd0:kd1, :],
                dr["w_down"][l].rearrange("(k p) n -> p k n", p=128)[:, kd0:kd1, :], writes=[bWDB])
            hf, hb = 8 * g, 32 + 8 * g
            R3 = SCR1.ap[:, 0:2048].rearrange("p (h l) -> p h l", h=16)
            Es = [SCR2, T(PAD2.ap[:, 0:2048], PAD2.b)]

            def state_update(c, h0):
                op("gpsimd", lambda e: e.tensor_tensor(XDD.ap.rearrange("p (h d) -> p h d", h=8),
                                                       XTOK3[:, c, :].rearrange("p (h d) -> p h d", h=8),
                                                       hd_bc(DTDE, c, h0), ALU.mult),
                   reads=[bXTOK[c], DTDE.b], writes=[XDD.b])
                op("tensor", lambda e: e.matmul(bank(5), BTOK3[:, c, :], XDD.ap, start=True, stop=True),
                   reads=[bBTOK[c], XDD.b], writes=[bPS[5]])
                op("vector", lambda e: e.tensor_tensor(HS.ap.rearrange("p (h d) -> p h d", h=8),
                                                       HS.ap.rearrange("p (h d) -> p h d", h=8),
                                                       hd_bc(DECC, c, h0), ALU.mult),
                   reads=[HS.b, DECC.b], writes=[HS.b])
                op("vector", lambda e: e.tensor_tensor(HS.ap, HS.ap, bank(5), ALU.add),
                   reads=[HS.b, bPS[5]], writes=[HS.b])

            dma("sync", BC[:, BC_NG:BC_NG + 512],
                dr["bc"][l:l + 1, BC_NG + 512 * g:BC_NG + 512 * (g + 1)].to_broadcast([128, 512]), writes=[bNG])
            op("gpsimd", lambda e: e.tensor_tensor(
                DI3, ID8.ap.rearrange("p (h l) -> p h l", h=8),
                BC[:, BC_DSK + hf: BC_DSK + hf + 8].unsqueeze(2).to_broadcast([128, 8, 128]), ALU.mult),
               reads=[ID8.b, bBC], writes=[DIg.b])
            op("gpsimd", lambda e: e.memset(HS.ap, 0.0), writes=[HS.b])
            for c in range(NCH):
                op("scalar", lambda e, c=c: e.copy(HPREV3[:, c, :], HS.ap), reads=[HS.b], writes=[bHPREV[c]])
                if c < NCH - 1:
                    state_update(c, hf)
            MBs = [SCR3, MB2]
            bSC = bPS[7]
            bTPY = bPS[7]

            def stage1(c, par):
                Mt = MBs[par]
                M3 = Mt.ap.rearrange("p (h l) -> p h l", h=16)
                xf, xb_ = XDTFs[par], XDTBs[par]
                E = Es[par]
                SC = bank(7)[:, 0:128]
                op("tensor", lambda e: e.matmul(SC, BT.ap[:, c * 128:(c + 1) * 128], CT.ap[:, c * 128:(c + 1) * 128],
                                                start=True, stop=True),
                   reads=[BT.b, CT.b], writes=[bSC])
                op("vector", lambda e: e.tensor_tensor(SFB.ap.rearrange("p (a l) -> p a l", a=2),
                                                       SC.unsqueeze(1).to_broadcast([128, 2, 128]),
                                                       CM[:, 0:256].rearrange("p (a l) -> p a l", a=2), ALU.mult),
                   reads=[bSC, bCM], writes=[SFB.b])
                op("gpsimd", lambda e: e.tensor_tensor(
                    RB3[:, 0:8, :], LE.unsqueeze(1).to_broadcast([128, 8, 128]),
                    ADT.ap[:, c * 64 + hf: c * 64 + hf + 8].unsqueeze(2).to_broadcast([128, 8, 128]), ALU.mult),
                   reads=[bCM, ADT.b], writes=[bRB])
                op("gpsimd", lambda e: e.tensor_tensor(
                    RB3[:, 8:16, :], GE.unsqueeze(1).to_broadcast([128, 8, 128]),
                    ADT.ap[:, c * 64 + hb: c * 64 + hb + 8].unsqueeze(2).to_broadcast([128, 8, 128]), ALU.mult),
                   reads=[bCM, ADT.b], writes=[bRB])
                op("gpsimd", lambda e: e.tensor_tensor(xf.ap.rearrange("p (h d) -> p h d", h=8),
                                                       XTOK3[:, c, :].rearrange("p (h d) -> p h d", h=8),
                                                       hd_bc(DTt, c, hf), ALU.mult),
                   reads=[bXTOK[c], DTt.b], writes=[xf.b])
                op("gpsimd", lambda e: e.tensor_tensor(xb_.ap.rearrange("p (h d) -> p h d", h=8),
                                                       XTOK3[:, c, :].rearrange("p (h d) -> p h d", h=8),
                                                       hd_bc(DTt, c, hb), ALU.mult),
                   reads=[bXTOK[c], DTt.b], writes=[xb_.b])
                for d_ in range(2):
                    lhs = GLR[:, 0:128] if d_ == 0 else GLR[:, 128:256]
                    for q in range(2):
                        op("tensor", lambda e, d_=d_, q=q, lhs=lhs: e.matmul(
                            bank(q), lhs, RB[:, d_ * 1024 + q * 512: d_ * 1024 + (q + 1) * 512],
                            start=True, stop=True),
                           reads=[bRB, bGLR], writes=[bPS[q]], inc=(q == 1))
                    op("scalar", lambda e, d_=d_: e.activation(E.ap[:, d_ * 1024:(d_ + 1) * 1024], bank(0, 2), AF.Exp),
                       reads=bPS[0:2], writes=[E.b])
                    op("vector", lambda e, d_=d_: e.tensor_tensor(
                        M3[:, d_ * 8:(d_ + 1) * 8, :],
                        E.ap[:, d_ * 1024:(d_ + 1) * 1024].rearrange("p (h l) -> p h l", h=8),
                        SFB.ap[:, d_ * 128:(d_ + 1) * 128].unsqueeze(1).to_broadcast([128, 8, 128]), ALU.mult),
                       reads=[E.b, SFB.b], writes=[Mt.b])

            def stage2(c, par, ci):
                Mt = MBs[par]
                M3 = Mt.ap.rearrange("p (h l) -> p h l", h=16)
                xf, xb_ = XDTFs[par], XDTBs[par]
                op("scalar", lambda e: e.copy(HBb.ap, HS.ap), reads=[HS.b], writes=[HBb.b])
                if c > 0:
                    state_update(c, hb)
                for h in range(8):
                    op("tensor", lambda e, h=h: e.matmul(bank(2)[:, h * 64:(h + 1) * 64], M3[:, h, :],
                                                         xf.ap[:, h * 64:(h + 1) * 64], start=True, stop=False),
                       reads=[Mt.b, xf.b], writes=[bPS[2]], inc=False)
                    op("tensor", lambda e, h=h: e.matmul(bank(2)[:, h * 64:(h + 1) * 64], M3[:, 8 + h, :],
                                                         xb_.ap[:, h * 64:(h + 1) * 64], start=False, stop=False),
                       reads=[Mt.b, xb_.b], writes=[bPS[2]], inc=False)
                    op("tensor", lambda e, h=h: e.matmul(bank(2)[:, h * 64:(h + 1) * 64], DI3[:, h, :],
                                                         XTOK3[:, c, h * 64:(h + 1) * 64], start=False, stop=True),
                       reads=[DIg.b, bXTOK[c]], writes=[bPS[2]], inc=(h == 7))
                op("tensor", lambda e: e.matmul(bank(3), CT.ap[:, c * 128:(c + 1) * 128], HPREV3[:, c, :],
                                                start=True, stop=True),
                   reads=[CT.b, bHPREV[c]], writes=[bPS[3]])
                op("tensor", lambda e: e.matmul(bank(4), CT.ap[:, c * 128:(c + 1) * 128], HBb.ap,
                                                start=True, stop=True),
                   reads=[CT.b, HBb.b], writes=[bPS[4]])
                for kc in range(KC):
                    op("tensor", lambda e, kc=kc: e.matmul(bank(6), XB3[:, kc, c * 128:(c + 1) * 128], wz3[:, kc, :],
                                                           start=(kc == 0), stop=(kc == KC - 1)),
                       reads=[wz.b] + bXB, writes=[bPS[6]], inc=(kc == KC - 1))
                op("vector", lambda e: e.tensor_tensor(T1.ap.rearrange("p (h d) -> p h d", h=8),
                                                       bank(3).rearrange("p (h d) -> p h d", h=8),
                                                       hd_bc(EXPA, c, hf), ALU.mult),
                   reads=[bPS[3], EXPA.b], writes=[T1.b])
                op("vector", lambda e: e.tensor_tensor(T2.ap.rearrange("p (h d) -> p h d", h=8),
                                                       bank(4).rearrange("p (h d) -> p h d", h=8),
                                                       hd_bc(EXPA, c, hb), ALU.mult),
                   reads=[bPS[4], EXPA.b], writes=[T2.b])
                op("gpsimd", lambda e: e.tensor_tensor(T1.ap, T1.ap, T2.ap, ALU.add), reads=[T1.b, T2.b], writes=[T1.b])
                op("vector", lambda e: e.tensor_tensor(YT.ap, T1.ap, bank(2), ALU.add), reads=[T1.b, bPS[2]], writes=[YT.b])
                op("scalar", lambda e: e.activation(SZ.ap, bank(6), AF.Silu), reads=[bPS[6]], writes=[SZ.b])
                op("vector", lambda e: e.tensor_tensor(VV.ap, YT.ap, SZ.ap, ALU.mult), reads=[YT.b, SZ.b], writes=[VV.b])
                op("gpsimd", lambda e: e.memset(SS.ap[:, 0:1], 0.0), writes=[SS.b])
                op("scalar", lambda e: e.activation(JK.ap, VV.ap, AF.Square, accum_out=SS.ap[:, 0:1]),
                   reads=[VV.b], writes=[JK.b, SS.b])
                op("scalar", lambda e: e.activation(SS.ap[:, 1:2], SS.ap[:, 0:1], AF.Ln, bias=RMS_EPS, scale=1.0 / 512),
                   reads=[SS.b], writes=[SS.b])
                op("scalar", lambda e: e.activation(SS.ap[:, 2:3], SS.ap[:, 1:2], AF.Exp, scale=-0.5),
                   reads=[SS.b], writes=[SS.b])
                op("scalar", lambda e: e.activation(V2.ap, VV.ap, AF.Copy, scale=SS.ap[:, 2:3]),
                   reads=[VV.b, SS.b], writes=[V2.b])
                op("vector", lambda e: e.tensor_tensor(VN.ap, V2.ap, BC[:, BC_NG: BC_NG + 512], ALU.mult),
                   reads=[V2.b, bNG], writes=[VN.b])
                TPY = bank(7)[:, 256:512].bitcast(BF16)
                for q in range(4):
                    op("tensor", lambda e, q=q: e.transpose(TPY[:, q * 128:(q + 1) * 128], VN.ap[:, q * 128:(q + 1) * 128],
                                                            IDB[:, :]),
                       reads=[VN.b, bIDB], writes=[bTPY], inc=(q == 3))
                ynt = YNT[ci % 2]
                op("scalar", lambda e: e.copy(ynt.ap, TPY), reads=[bTPY], writes=[ynt.b])
                dma("sync", dr["yn"][4 * g:4 * g + 4, :, c * 128:(c + 1) * 128].rearrange("q p t -> p q t"),
                    ynt.ap.rearrange("p (q t) -> p q t", q=4), reads=[ynt.b], writes=[bYN[ci % 2]], sembuf=ynt.b)

            op("gpsimd", lambda e: e.memset(HS.ap, 0.0), writes=[HS.b])
            if PIPELINE_B:
                stage1(NCH - 1, 0)
            for ci, c in enumerate(range(NCH - 1, -1, -1)):
                if PIPELINE_B:
                    if c > 0:
                        stage1(c - 1, (ci + 1) % 2)
                else:
                    stage1(c, ci % 2)
                stage2(c, ci % 2, ci)
            S_.end_region()

        S_.soft_barrier()
        AR.reset()
        MRG = AR.bf("MRG", KC * S)
        MRG3 = MRG.ap.rearrange("p (k t) -> p k t", k=KC)
        bMRG = S_.bufs("MRGj", KC)
        mrg_end = AR.off
        YN = AR.bf("YN", 16 * S)
        YN3 = YN.ap.rearrange("p (k t) -> p k t", k=16)
        GCs = [AR.f32(f"GC{i}", 2048) for i in range(2)]
        S_.begin_region()
        bYNl = S_.bufs("YNl", 4)
        for k4 in range(4):
            dma("sync", YN3[:, 4 * k4:4 * k4 + 4, :], dr["yn"][4 * k4:4 * k4 + 4].rearrange("k p t -> p k t"),
                reads=bYN, writes=[bYNl[k4]])
        for j in range(KC):
            wp = W.load([(0, (16, 128), dr["w_ssd_proj"][l].rearrange("(k p) n -> p k n", p=128)[:, :, j * 128:(j + 1) * 128])])
            wp3 = wp.ap[:, 0:2048].rearrange("p (k n) -> p k n", k=16)
            wg = W.load([(0, (KC, 128), win_src(l, G0 + 1024 + j * 128, 128))])
            for nt in range(NT):
                for kc in range(16):
                    op("tensor", lambda e, nt=nt, kc=kc: e.matmul(bank(nt), wp3[:, kc, :], YN3[:, kc, nt * 512:(nt + 1) * 512],
                                                                  start=(kc == 0), stop=(kc == 15)),
                       reads=[wp.b, bYNl[kc // 4]], writes=bPS[0:4], inc=(nt == NT - 1 and kc == 15))
            G = GCs[j % 2]
            proj_fm(wg, 0, 128, 0, 4, bPS[4:8])
            op("scalar", lambda e: e.activation(G.ap, bank(4, 4), AF.Sigmoid), reads=bPS[4:8], writes=[G.b])
            op("vector", lambda e, j=j: e.tensor_tensor(MRG3[:, j, :], G.ap, bank(0, 4), ALU.mult),
               reads=[G.b] + bPS[0:4], writes=[bMRG[j]])
        S_.end_region()

        if l == 0:
            tap("mrgC", MRG.ap, bMRG, [128, KC * S], BF16)
            tap("yn", YN.ap, bYNl, [128, 16 * S], BF16)
        S_.soft_barrier()
        AR.reset(mrg_end)
        P0s = [AR.f32(f"P0{i}", 2064) for i in range(2)]
        Q1s = [AR.f32(f"Q1{i}", 2064) for i in range(2)]
        Q2s = [AR.f32(f"Q2{i}", 2064) for i in range(2)]
        PLD = [AR.bf(f"PLD{i}", 2048) for i in range(2)]
        Gs = [AR.f32(f"G{i}", 2048) for i in range(2)]
        TMPs = [AR.f32(f"TMP{i}", 2048) for i in range(2)]
        TEs = [AR.f32(f"TE{i}", 16) for i in range(2)]
        S_.begin_region()
        for P0 in P0s:
            op("gpsimd", lambda e: e.memset(P0.ap[:, 0:8], 0.0), writes=[P0.b])
            op("gpsimd", lambda e: e.memset(P0.ap[:, 2056:2064], 0.0), writes=[P0.b])
        for gi, w_ in enumerate(POOL_WINDOWS):
            half = w_ // 2
            wu = W.load([(0, (KC, 256), win_src(l, U0 + 256 * gi, 256))])
            for k2 in range(2):
                pb0 = 4 * (k2 % 2)
                P0, Q1, Q2, TE = P0s[k2], Q1s[k2], Q2s[k2], TEs[k2]
                proj_fm(wu, 0, 256, k2, pb0, bPS[pb0:pb0 + 4])
                op("scalar", lambda e, pb0=pb0: e.copy(P0.ap[:, 8:2056], bank(pb0, 4)), reads=bPS[pb0:pb0 + 4], writes=[P0.b])
                src, dst = P0, Q1
                sh = 1
                while sh < w_:
                    op("vector", lambda e, src=src, dst=dst, sh=sh: e.tensor_tensor(
                        dst.ap[:, sh:2064], src.ap[:, sh:2064], src.ap[:, 0:2064 - sh], ALU.add),
                       reads=[src.b], writes=[dst.b])
                    src = dst
                    dst = Q2 if dst is Q1 else Q1
                    sh *= 2
                o = 8 + half - 1
                pl = PLD[k2]
                op("vector", lambda e, src=src, o=o, pl=pl, w_=w_: e.scalar_tensor_tensor(
                    pl.ap, src.ap[:, o:o + 2048], 1.0 / w_, P0.ap[:, 8:2056], ALU.mult, ALU.subtract),
                   reads=[src.b, P0.b], writes=[pl.b])
                nl, nr = half, half - 1
                op("vector", lambda e, src=src, o=o, gi=gi, nl=nl: e.tensor_tensor(
                    TE.ap[:, 0:nl], src.ap[:, o:o + nl], RCN[:, gi * 16: gi * 16 + nl], ALU.mult),
                   reads=[src.b, bRCN], writes=[TE.b])
                op("vector", lambda e, pl=pl, nl=nl: e.tensor_tensor(pl.ap[:, 0:nl], TE.ap[:, 0:nl], P0.ap[:, 8:8 + nl],
                                                                      ALU.subtract),
                   reads=[TE.b, P0.b], writes=[pl.b])
                if nr > 0:
                    op("vector", lambda e, src=src, o=o, gi=gi, nr=nr: e.tensor_tensor(
                        TE.ap[:, 8:8 + nr], src.ap[:, o + 2048 - nr:o + 2048],
                        RCN[:, gi * 16 + 8: gi * 16 + 8 + nr], ALU.mult),
                       reads=[src.b, bRCN], writes=[TE.b])
                    op("vector", lambda e, pl=pl, nr=nr: e.tensor_tensor(
                        pl.ap[:, 2048 - nr:2048], TE.ap[:, 8:8 + nr], P0.ap[:, 8 + 2048 - nr:8 + 2048], ALU.subtract),
                       reads=[TE.b, P0.b], writes=[pl.b])
            wm = W.load([(0, (2, 256), dr["pool_w"][l, gi].rearrange("(k p) n -> p k n", p=128))])
            wm3 = wm.ap[:, 0:512].rearrange("p (k n) -> p k n", k=2)
            for jj in range(2):
                j = 2 * gi + jj
                for nt in range(NT):
                    for k2 in range(2):
                        op("tensor", lambda e, nt=nt, k2=k2, jj=jj: e.matmul(
                            bank(nt), wm3[:, k2, jj * 128:(jj + 1) * 128], PLD[k2].ap[:, nt * 512:(nt + 1) * 512],
                            start=(k2 == 0), stop=(k2 == 1)),
                           reads=[wm.b, PLD[0].b, PLD[1].b], writes=bPS[0:4], inc=(nt == NT - 1 and k2 == 1))
                wg = W.load([(0, (KC, 128), win_src(l, G0 + j * 128, 128))])
                G, TMP = Gs[jj], TMPs[jj]
                proj_fm(wg, 0, 128, 0, 4, bPS[4:8])
                op("scalar", lambda e: e.activation(G.ap, bank(4, 4), AF.Sigmoid), reads=bPS[4:8], writes=[G.b])
                op("vector", lambda e, j=j: e.scalar_tensor_tensor(TMP.ap, bank(0, 4), pcol(l, PP_PS + j), G.ap,
                                                                   ALU.mult, ALU.mult),
                   reads=bPS[0:4] + [G.b, bPP], writes=[TMP.b])
                op("gpsimd", lambda e, j=j: e.tensor_tensor(MRG3[:, j, :], TMP.ap, MRG3[:, j, :], ALU.add),
                   reads=[TMP.b, bMRG[j]], writes=[bMRG[j]])
        S_.end_region()

        def outproj_ln(rhs3, rhs_bufs, nK, wsrc, goff, boff, final):
            nonlocal xres_src, xres_bufs
            SUMt = AR.f32("SUMt", KC * 512)
            SUM3 = SUMt.ap.rearrange("p (k t) -> p k t", k=KC)
            XR = [AR.f32(f"XR{i}", 512) for i in range(2)]
            SQ = [AR.f32(f"SQ{i}", 512) for i in range(2)]
            MEAN = AR.f32("MEAN", 512)
            M2 = AR.f32("M2", 512)
            RSTD = AR.f32("RSTD", 512)
            TA = [AR.f32(f"TA{i}", 512) for i in range(2)]
            XN = [AR.f32(f"XN{i}", 512) for i in range(2)]
            while len(XR) < 8 and ARENA_WORDS - AR.off >= 512:
                XR.append(AR.f32(f"XR{len(XR)}", 512))
            nxr = len(XR)
            kgroups = [(k0, min(4, nK - k0)) for k0 in range(0, nK, 4)]
            out_dst = dr["out"] if final else dr["xres"]
            S_.begin_region()
            for nt in range(NT):
                for (k0, nk) in kgroups:
                    ws = W.load([(0, (nk, 1024), wsrc.rearrange("(k p) n -> p k n", p=128)[:, k0:k0 + nk, :], "sync")])
                    ws3 = ws.ap[:, 0:nk * 1024].rearrange("p (k n) -> p k n", k=nk)
                    for j in range(KC):
                        for kk in range(nk):
                            kc = k0 + kk
                            op("tensor", lambda e, j=j, kk=kk, kc=kc, ws3=ws3: e.matmul(
                                bank(j), ws3[:, kk, j * 128:(j + 1) * 128], rhs3[:, kc, nt * 512:(nt + 1) * 512],
                                start=(kc == 0), stop=(kc == nK - 1)),
                               reads=[ws.b] + rhs_bufs, writes=[bPS[j]], inc=(kk == nk - 1))
                for j in range(KC):
                    xr = XR[(nt * KC + j) % nxr]
                    dma("sync", xr.ap, xres_src[j, :, nt * 512:(nt + 1) * 512], reads=xres_bufs[nt], writes=[xr.b])
                    op("vector", lambda e, j=j, xr=xr: e.scalar_tensor_tensor(SUM3[:, j, :], xr.ap, float(ALPHA), bank(j),
                                                                              ALU.mult, ALU.add),
                       reads=[xr.b, bPS[j]], writes=[SUMt.b])
                for j in range(KC):
                    op("tensor", lambda e, j=j: e.matmul(bank(0), ONES, SUM3[:, j, :], start=(j == 0), stop=(j == KC - 1)),
                       reads=[SUMt.b, bCM], writes=[bPS[0]], inc=(j == KC - 1))
                for j in range(KC):
                    sq = SQ[j % 2]
                    op("scalar", lambda e, j=j, sq=sq: e.activation(sq.ap, SUM3[:, j, :], AF.Square),
                       reads=[SUMt.b], writes=[sq.b])
                    op("tensor", lambda e, j=j, sq=sq: e.matmul(bank(1), ONES, sq.ap, start=(j == 0), stop=(j == KC - 1)),
                       reads=[sq.b, bCM], writes=[bPS[1]])
                op("vector", lambda e: e.tensor_scalar(MEAN.ap, bank(0), 1.0 / D, None, ALU.mult),
                   reads=[bPS[0]], writes=[MEAN.b])
                op("vector", lambda e: e.tensor_tensor(M2.ap, MEAN.ap, MEAN.ap, ALU.mult), reads=[MEAN.b], writes=[M2.b])
                op("vector", lambda e: e.scalar_tensor_tensor(RSTD.ap, bank(1), 1.0 / D, M2.ap, ALU.mult, ALU.subtract),
                   reads=[bPS[1], M2.b], writes=[RSTD.b])
                op("scalar", lambda e: e.activation(RSTD.ap, RSTD.ap, AF.Ln, bias=LN_EPS), reads=[RSTD.b], writes=[RSTD.b])
                op("scalar", lambda e: e.activation(RSTD.ap, RSTD.ap, AF.Exp, scale=-0.5), reads=[RSTD.b], writes=[RSTD.b])
                for j in range(KC):
                    ta = TA[j % 2]
                    xn = XN[j % 2]
                    op("vector", lambda e, j=j, ta=ta: e.tensor_tensor(ta.ap, SUM3[:, j, :], MEAN.ap, ALU.subtract),
                       reads=[SUMt.b, MEAN.b], writes=[ta.b])
                    op("vector", lambda e, ta=ta: e.tensor_tensor(ta.ap, ta.ap, RSTD.ap, ALU.mult),
                       reads=[ta.b, RSTD.b], writes=[ta.b])
                    op("scalar", lambda e, j=j, ta=ta, xn=xn: e.activation(xn.ap, ta.ap, AF.Identity,
                                                                           bias=pcol(l, boff + j), scale=pcol(l, goff + j)),
                       reads=[ta.b, bPP], writes=[xn.b])
                    if not final:
                        op("scalar", lambda e, j=j, ta=ta: e.activation(XB3[:, j, nt * 512:(nt + 1) * 512], ta.ap, AF.Identity,
                                                                        bias=pcol(l, boff + j), scale=pcol(l, goff + j)),
                           reads=[ta.b, bPP], writes=[bXB[nt]])
                    dma("sync", out_dst[j, :, nt * 512:(nt + 1) * 512], xn.ap, reads=[xn.b],
                        writes=[(bOUT if final else bXRES[nt])[j % 2]], sembuf=xn.b)
            S_.end_region()
            if not final:
                xres_src = dr["xres"]
                xres_bufs = bXRES

        if l == 0:
            tap("mrgD", MRG.ap, bMRG, [128, KC * S], BF16)
        S_.soft_barrier()
        AR.reset(mrg_end)
        outproj_ln(MRG3, bMRG, KC, dr["wob"], PP_L1G, PP_L1B, final=False)
        if l == 0:
            tap("xb1", XB[:, :], bXB, [128, KC * S], BF16)

        S_.soft_barrier()
        AR.reset()
        HB = AR.bf("HB", FK * S)
        HB3 = HB.ap.rearrange("p (k t) -> p k t", k=FK)
        bHB = S_.bufs("HBk", FK)
        hb_end = AR.off
        PADFs = [AR.f32(f"PADF{i}", 2050) for i in range(2)]
        ACCF = AR.f32("ACCF", 2048)
        GTt = AR.f32("GT", 2048)
        for PADF in PADFs:
            op("gpsimd", lambda e: e.memset(PADF.ap[:, 0:1], 0.0), writes=[PADF.b])
            op("gpsimd", lambda e: e.memset(PADF.ap[:, 2049:2050], 0.0), writes=[PADF.b])
        S_.begin_region()
        for q in range(6):
            ncol = 512 if q < 5 else 256
            wgs = W.load([(0, (KC, ncol), dr["w_up"][l].rearrange("(k p) n -> p k n", p=128)[:, :, 512 * q:512 * q + ncol])])
            wvs = W.load([(0, (KC, ncol), dr["w_up"][l].rearrange("(k p) n -> p k n", p=128)[:, :, DFF + 512 * q:DFF + 512 * q + ncol])])
            for kk in range(ncol // 128):
                k = 4 * q + kk
                for half_, wsl in enumerate((wgs, wvs)):
                    pb0 = 4 * half_
                    PADF = PADFs[half_ if DB_F else 0]
                    cch = k + FK * half_
                    proj_fm(wsl, 0, ncol, kk, pb0, bPS[pb0:pb0 + 4])
                    op("scalar", lambda e, pb0=pb0: e.copy(PADF.ap[:, 1:2049], bank(pb0, 4)),
                       reads=bPS[pb0:pb0 + 4], writes=[PADF.b])
                    AC = GTt if half_ == 0 else ACCF
                    op("scalar", lambda e, cch=cch, AC=AC: e.activation(AC.ap, PADF.ap[:, 0:2048], AF.Identity,
                                                                        bias=pcol(l, PP_FB + cch), scale=pcol(l, PP_FW + cch * 3)),
                       reads=[PADF.b, bPP], writes=[AC.b])
                    for t_ in range(1, 3):
                        op("vector", lambda e, t_=t_, cch=cch, AC=AC: e.scalar_tensor_tensor(
                            AC.ap, PADF.ap[:, t_:t_ + 2048], pcol(l, PP_FW + cch * 3 + t_), AC.ap, ALU.mult, ALU.add),
                           reads=[PADF.b, AC.b, bPP], writes=[AC.b])
                    if half_ == 0:
                        op("scalar", lambda e: e.activation(GTt.ap, GTt.ap, AF.Gelu), reads=[GTt.b], writes=[GTt.b])
                    else:
                        op("vector", lambda e, k=k: e.tensor_tensor(HB3[:, k, :], GTt.ap, ACCF.ap, ALU.mult),
                           reads=[GTt.b, ACCF.b], writes=[bHB[k]])
        S_.end_region()

        if l == 0:
            tap("hb", HB.ap, bHB, [128, FK * S], BF16)
        S_.soft_barrier()
        AR.reset(hb_end)
        outproj_ln(HB3, bHB, FK, dr["wdb"], PP_L2G, PP_L2B, final=(l == depth - 1))

    S_.barrier()
    return S_


def _consts():
    r = np.arange(128)[:, None]
    c = np.arange(128)[None, :]
    le = (r <= c).astype(np.float32)
    ge = (r >= c).astype(np.float32)
    gt = (r > c).astype(np.float32)
    lt = (r < c).astype(np.float32)
    ones = np.ones((128, 128), np.float32)
    cmask = np.concatenate([le, ge, gt, lt, ones], axis=1)
    cid = np.eye(128, dtype=np.float32)
    ctm = np.concatenate([le] * 8 + [ge] * 8, axis=1)
    crc = np.zeros((1, 64), np.float32)
    t = np.arange(S)
    for gi, w in enumerate(POOL_WINDOWS):
        half = w // 2
        cnt = np.minimum(t + half - 1, S - 1) - np.maximum(t - half, 0) + 1
        rc = (1.0 / cnt).astype(np.float32)
        crc[0, gi * 16: gi * 16 + half] = rc[:half]
        if half > 1:
            crc[0, gi * 16 + 8: gi * 16 + 8 + half - 1] = rc[S - (half - 1):]
    return cmask, cid, ctm, crc


def _pack_params(inp, depth):
    pp = np.zeros((128, depth, NPP), np.float32)
    bc = np.zeros((depth, NBC), np.float32)
    for l in range(depth):
        pp[:, l, PP_CW:PP_CW + 120] = inp["ssd_conv_w"][l].T.reshape(24, 128, 5).transpose(1, 0, 2).reshape(128, 120)
        pp[:, l, PP_CB:PP_CB + 24] = inp["ssd_conv_b"][l].reshape(24, 128).T
        pp[:, l, PP_FW:PP_FW + 132] = inp["ffn_conv_w"][l].T.reshape(44, 128, 3).transpose(1, 0, 2).reshape(128, 132)
        pp[:, l, PP_FB:PP_FB + 44] = inp["ffn_conv_b"][l].reshape(44, 128).T
        pp[:, l, PP_PS:PP_PS + 8] = inp["pool_scale"][l].reshape(8, 128).T
        pp[:, l, PP_L1G:PP_L1G + 8] = inp["ln1_g"][l].reshape(8, 128).T
        pp[:, l, PP_L1B:PP_L1B + 8] = inp["ln1_b"][l].reshape(8, 128).T
        pp[:, l, PP_L2G:PP_L2G + 8] = inp["ln2_g"][l].reshape(8, 128).T
        pp[:, l, PP_L2B:PP_L2B + 8] = inp["ln2_b"][l].reshape(8, 128).T
        bc[l, BC_ALOG:BC_ALOG + 64] = inp["a_log"][l].reshape(64)
        bc[l, BC_DTB:BC_DTB + 64] = inp["dt_bias"][l].reshape(64)
        bc[l, BC_DSK:BC_DSK + 32] = inp["d_skip"][l]
        bc[l, BC_NG:BC_NG + 2048] = inp["ssd_norm_g"][l]
    return np.ascontiguousarray(pp.reshape(128, depth * NPP)), bc


def run(inputs, depth=DEPTH, n_cores=8, trace=False):
    inp = {k: np.asarray(v, dtype=np.float32) for k, v in inputs.items()}
    x = inp["x"]
    nb = x.shape[0]
    cmask, cid, ctm, crc = _consts()
    pp, bc = _pack_params(inp, depth)
    shared = {
        "w_in": np.ascontiguousarray(inp["w_in"][:depth]),
        "pool_w": np.ascontiguousarray(inp["pool_w"][:depth]),
        "w_ssd_proj": np.ascontiguousarray(inp["w_ssd_proj"][:depth]),
        "w_out": np.ascontiguousarray(inp["w_out"][:depth]),
        "w_up": np.ascontiguousarray(inp["w_up"][:depth]),
        "w_down": np.ascontiguousarray(inp["w_down"][:depth]),
        "pp": pp, "bc": bc, "cmask": cmask, "cid": cid, "ctm": ctm, "crc": crc,
        "cid8": np.ascontiguousarray(np.tile(cid, (1, 8))),
    }
    in_maps = []
    for b in range(nb):
        m = dict(shared)
        m["xT"] = np.ascontiguousarray(x[b].T).reshape(KC, 128, S)
        in_maps.append(m)
    nc = build_program(depth)
    res = run_bass_kernel_spmd(nc, in_maps, core_ids=list(range(nb)), trace=trace)
    global LAST_DBG
    LAST_DBG = {k: np.asarray(res.results[0]["dbg_" + k]) for k in DEBUG_TAPS}
    outs = [np.asarray(r["out"]).reshape(D, S).T for r in res.results]
    return np.ascontiguousarray(np.stack(outs, axis=0).astype(np.float32)), res


def kernel(**inputs):
    out, _ = run(inputs, depth=DEPTH)
    return out
```

```python
import numpy as np
import concourse.bass as bass
import concourse.mybir as mybir
from concourse.bass_utils import run_bass_kernel_spmd
from contextlib import ExitStack

F32 = mybir.dt.float32
BF16 = mybir.dt.bfloat16
F32R = mybir.dt.float32r
AF = mybir.ActivationFunctionType
ALU = mybir.AluOpType

D = 1024
KC = 8
S = 2048
NT = 4
DEPTH = 4
NCH = 16
DFF = 2816
FK = 22
U0, Z0, XBC0, DT0, G0 = 0, 1024, 3072, 6144, 6208
ALPHA = (2 * DEPTH) ** 0.25
LN_EPS = 1e-5
RMS_EPS = 1e-5
POOL_WINDOWS = (2, 4, 8, 16)

PP_CW = 0
PP_CB = PP_CW + 120
PP_FW = PP_CB + 24
PP_FB = PP_FW + 132
PP_PS = PP_FB + 44
PP_L1G = PP_PS + 8
PP_L1B = PP_L1G + 8
PP_L2G = PP_L1B + 8
PP_L2B = PP_L2G + 8
NPP = PP_L2B + 8
BC_ALOG = 0
BC_DTB = 64
BC_DSK = 128
BC_NG = 160
NBC = BC_NG + 2048
NBC_SB = BC_NG + 512

SEM_GEN = 30000
PIPELINE_B = True
USE_REGIONS = True
SOFT_BARRIERS = False
SCHED_EPS = 3.0
LAT_TAIL = 0.7
WIN_REGION = 1500
NONCE = ""
DB_A = True
DB_F = True
JK_BF = True
ARENA_WORDS = 32256


class Buf:
    __slots__ = ("name", "w", "r", "dsem", "dcount", "excl")

    def __init__(self, name, excl=False):
        self.name = name
        self.w = None
        self.r = []
        self.dsem = None
        self.dcount = 0
        self.excl = excl


class Eng:
    def __init__(self, name):
        self.name = name
        self.count = 0
        self.pending = False
        self.ops = []
        self.waited = {}

    def semkey(self, cnt):
        return ("E", self.name, (cnt - 1) // SEM_GEN)


class _Rec:
    def __init__(self):
        self.call = None

    def __getattr__(self, name):
        def f(*a, **k):
            self.call = (name, a, k)
            return None
        return f


class Sched:
    def __init__(self, nc, dry=False):
        self.nc = nc
        self.dry = dry
        self.eng = {n: Eng(n) for n in ("tensor", "vector", "scalar", "gpsimd", "sync")}
        self.semkeys = {}
        self.nbuf = 0
        self.dma_out = []
        self.fence_toks = []
        self.fence_dma = {}
        self.dsem_count = {}
        self.region = None
        self._pend = {}

    def buf(self, name=None, excl=False):
        self.nbuf += 1
        b = Buf(name or f"b{self.nbuf}", excl)
        b.r = list(self.fence_toks)
        return b

    def bufs(self, name, n, excl=False):
        return [self.buf(f"{name}{i}", excl) for i in range(n)]

    @staticmethod
    def _tok_local(tok):
        key, val = tok
        if key[0] == "E":
            return key, val - key[2] * SEM_GEN
        return key, val

    def _need(self, e, toks):
        best = {}
        for t in toks:
            if t is None:
                continue
            if t[0][0] == "E" and t[0][1] == e.name and t[1] > e.count:
                continue
            key, val = self._tok_local(t)
            if e.waited.get(key, 0) >= val:
                continue
            if best.get(key, 0) < val:
                best[key] = val
        for k, v in best.items():
            e.waited[k] = v
            self.semkeys[k] = None
        return list(best.items())

    def op(self, engname, fn, reads=(), writes=(), inc=True):
        if self.dry:
            return None
        rec = _Rec()
        fn(rec)
        if self.region is not None:
            pend = self._pend.setdefault(engname, [])
            pend.append((rec.call, list(reads), list(writes)))
            if inc:
                self.region.append(("op", engname, pend))
                self._pend[engname] = []
            return None
        return self._op_core(engname, rec.call, reads, writes, inc)

    def _op_core(self, engname, call, reads, writes, inc):
        e = self.eng[engname]
        xr = [b for b in reads if b.excl]
        if xr:
            reads = [b for b in reads if not b.excl]
            writes = list(writes) + [b for b in xr if b not in writes]
        deps = []
        for b in reads:
            deps.append(b.w)
        for b in writes:
            deps.append(b.w)
            deps.extend(b.r)
        waits = self._need(e, deps)
        if inc:
            e.count += 1
            e.pending = False
            tokval = e.count
        else:
            e.pending = True
            tokval = e.count + 1
        key = e.semkey(tokval)
        tok = (key, tokval)
        self.semkeys[key] = None
        for b in reads:
            b.r.append(tok)
        for b in writes:
            b.w = tok
            b.r = []
        e.ops.append((waits, call, key if inc else None, None))
        return tok

    def dma(self, engname, out_ap, in_ap, reads=(), writes=(), sembuf=None, track=True):
        if self.dry:
            return None
        if self.region is not None:
            self.region.append(("dma", engname, (out_ap, in_ap, list(reads), list(writes), sembuf, track)))
            return None
        return self._dma_core(engname, out_ap, in_ap, reads, writes, sembuf, track)

    def _dma_core(self, engname, out_ap, in_ap, reads, writes, sembuf, track):
        e = self.eng[engname]
        sb = sembuf or writes[0]
        if sb.dsem is None:
            sb.dsem = ("D", sb.name)
        deps = []
        for b in reads:
            deps.append(b.w)
        for b in writes:
            deps.append(b.w)
            deps.extend(b.r)
        waits = self._need(e, deps)
        self.dsem_count[sb.dsem] = self.dsem_count.get(sb.dsem, 0) + 1
        tok = (sb.dsem, 16 * self.dsem_count[sb.dsem])
        self.semkeys[sb.dsem] = None
        for b in reads:
            b.r.append(tok)
        for b in writes:
            b.w = tok
            b.r = []

        e.ops.append((waits, ("dma_start", (), {"out": out_ap, "in_": in_ap}), None, sb.dsem))
        if track:
            self.dma_out.append(tok)
        return tok

    def begin_region(self):
        if self.dry or not USE_REGIONS:
            return
        assert self.region is None
        self.region = []
        self._pend = {}

    @staticmethod
    def _free(ap):
        n = 1
        for d in ap.shape[1:]:
            n *= d
        return n

    def _est(self, rec):
        kind, engname, body = rec
        if kind == "dma":
            return 0.15
        t = 0.0
        for call, _, _ in body:
            name, a, k = call
            out = a[0] if a else k.get("out")
            try:
                n = self._free(out)
            except Exception:
                n = 512
            if engname == "tensor":
                if name == "matmul":
                    d = max(n, 64) / 2400.0 + 0.03
                    if a[1].dtype == F32:
                        d *= 4.0
                    t += d
                else:
                    t += 0.1
            elif engname == "vector":
                t += (n + 150) / 960.0
            elif engname == "scalar":
                t += (n + 224) / 1200.0
            elif engname == "gpsimd":
                t += (2 * n + 150) / 960.0
            else:
                t += 0.1
        return t

    def end_region(self):
        if self.dry or not USE_REGIONS:
            return
        recs = self.region
        self.region = None
        for en, p in self._pend.items():
            assert not p, f"pending un-inc'd ops on {en} at region end"
        n = len(recs)
        last_w = {}
        readers = {}
        deps = [set() for _ in range(n)]
        for i, (kind, engname, body) in enumerate(recs):
            if kind == "dma":
                rr, ww = body[2], body[3]
            else:
                rr = [b for c in body for b in c[1]]
                ww = [b for c in body for b in c[2]]
            R = [b for b in rr if not b.excl]
            Wr = list(ww) + [b for b in rr if b.excl]
            for b in R:
                if id(b) in last_w:
                    deps[i].add(last_w[id(b)])
            for b in Wr:
                if id(b) in last_w:
                    deps[i].add(last_w[id(b)])
                deps[i].update(readers.get(id(b), ()))
            for b in R:
                readers.setdefault(id(b), []).append(i)
            for b in Wr:
                last_w[id(b)] = i
                readers[id(b)] = []
            deps[i].discard(i)
        dur = [self._est(r) for r in recs]
        succ = [[] for _ in range(n)]
        indeg = [0] * n
        for i in range(n):
            for d in deps[i]:
                succ[d].append(i)
            indeg[i] = len(deps[i])
        tail = [0.0] * n
        for i in range(n - 1, -1, -1):
            m = 0.0
            for j in succ[i]:
                if tail[j] > m:
                    m = tail[j]
            tail[i] = dur[i] + (m + LAT_TAIL if succ[i] else 0.0)
        ready_t = [0.0] * n
        ready = [i for i in range(n) if indeg[i] == 0]
        eng_free = {}
        order = []
        LAT = 1.2
        WIN = 48
        done = [False] * n
        lo = 0
        while ready:
            while lo < n and done[lo]:
                lo += 1
            cands = []
            emin = None
            for i in ready:
                if i > lo + WIN_REGION:
                    continue
                est = max(eng_free.get(recs[i][1], 0.0), ready_t[i])
                cands.append((est, i))
                if emin is None or est < emin:
                    emin = est
            if not cands:
                i = min(ready)
                est = max(eng_free.get(recs[i][1], 0.0), ready_t[i])
            else:
                best = None
                for est_i, i_ in cands:
                    if est_i <= emin + SCHED_EPS:
                        key = (-tail[i_], est_i, i_)
                        if best is None or key < best:
                            best = key
                i = best[2]
                est = best[1]
            ready.remove(i)
            done[i] = True
            fin = est + dur[i]
            eng_free[recs[i][1]] = fin
            if recs[i][0] == "dma":
                fin += 2.0
            order.append(i)
            for j in succ[i]:
                indeg[j] -= 1
                ready_t[j] = max(ready_t[j], fin + LAT)
                if indeg[j] == 0:
                    ready.append(j)
        assert len(order) == n
        for i in order:
            kind, engname, body = recs[i]
            if kind == "dma":
                self._dma_core(engname, *body)
            else:
                for k, (call, rr, ww) in enumerate(body):
                    self._op_core(engname, call, rr, ww, k == len(body) - 1)

    def soft_barrier(self):
        if self.dry:
            return
        if not SOFT_BARRIERS:
            return self.barrier()
        assert self.region is None
        for t in self.dma_out:
            if self.fence_dma.get(t[0], 0) < t[1]:
                self.fence_dma[t[0]] = t[1]
        self.dma_out = []
        toks = list(self.fence_dma.items())
        for e in self.eng.values():
            if e.pending:
                raise RuntimeError(f"engine {e.name} pending at soft barrier")
            if e.count > 0:
                toks.append((e.semkey(e.count), e.count))
        self.fence_toks = toks

    def barrier(self):
        if self.dry:
            return
        assert self.region is None
        toks = list(self.dma_out)
        self.dma_out = []
        for e in self.eng.values():
            if e.pending:
                raise RuntimeError(f"engine {e.name} pending at barrier")
            if e.count > 0:
                toks.append((e.semkey(e.count), e.count))
        for e in self.eng.values():
            waits = self._need(e, toks)
            if waits:
                e.ops.append((waits, None, None, None))

    def emit(self, stack):
        nc = self.nc
        sems = {}
        print(f"[sched] {len(self.semkeys)} semaphores", flush=True)
        for i, k in enumerate(self.semkeys):
            sems[k] = stack.enter_context(nc.semaphore(f"s{i}"))
        for e in self.eng.values():
            if e.pending:
                raise RuntimeError(f"engine {e.name} ends pending")
        block = stack.enter_context(nc.Block())

        def runner(e):
            def body(eng):
                for waits, fn, inckey, dsem in e.ops:
                    for k, v in waits:
                        eng.wait_ge(sems[k], v)
                    if fn is None:
                        continue
                    ins = getattr(eng, fn[0])(*fn[1], **fn[2])
                    if inckey is not None:
                        ins.then_inc(sems[inckey], 1)
                    elif dsem is not None:
                        ins.then_inc(sems[dsem], 16)
            return body

        block.tensor(runner(self.eng["tensor"]))
        block.vector(runner(self.eng["vector"]))
        block.scalar(runner(self.eng["scalar"]))
        block.gpsimd(runner(self.eng["gpsimd"]))
        block.sync(runner(self.eng["sync"]))


class T:
    __slots__ = ("ap", "b")

    def __init__(self, ap, b):
        self.ap = ap
        self.b = b


class Arena:
    def __init__(self, S_, arena_ap):
        self.S = S_
        self.arena = arena_ap
        self.off = 0

    def reset(self, off=0):
        self.off = off

    def f32(self, name, n):
        a = self.arena[:, self.off:self.off + n]
        self.off += n
        assert self.off <= ARENA_WORDS, (name, self.off)
        return T(a, self.S.buf(name))

    def bf(self, name, n):
        w = (n + 1) // 2
        a = self.arena[:, self.off:self.off + w].bitcast(BF16)
        self.off += w
        assert self.off <= ARENA_WORDS, (name, self.off)
        return T(a, self.S.buf(name))


DEBUG_TAPS = {}
DEBUG_ON = set()


def build_program(depth=DEPTH):
    nc = bass.Bass("TRN2", target_bir_lowering=False)
    dr = {}
    dr["xT"] = nc.dram_tensor("xT", [KC, 128, S], F32, kind="ExternalInput").ap()
    dr["w_in"] = nc.dram_tensor("w_in", [depth, D, 8256], F32, kind="ExternalInput").ap()
    dr["pool_w"] = nc.dram_tensor("pool_w", [depth, 4, 256, 256], F32, kind="ExternalInput").ap()
    dr["w_ssd_proj"] = nc.dram_tensor("w_ssd_proj", [depth, 2048, D], F32, kind="ExternalInput").ap()
    dr["w_out"] = nc.dram_tensor("w_out", [depth, D, D], F32, kind="ExternalInput").ap()
    dr["w_up"] = nc.dram_tensor("w_up", [depth, D, 2 * DFF], F32, kind="ExternalInput").ap()
    dr["w_down"] = nc.dram_tensor("w_down", [depth, DFF, D], F32, kind="ExternalInput").ap()
    dr["pp"] = nc.dram_tensor("pp", [128, depth * NPP], F32, kind="ExternalInput").ap()
    dr["bc"] = nc.dram_tensor("bc", [depth, NBC], F32, kind="ExternalInput").ap()
    dr["cmask"] = nc.dram_tensor("cmask", [128, 5 * 128], F32, kind="ExternalInput").ap()
    dr["cid"] = nc.dram_tensor("cid", [128, 128], F32, kind="ExternalInput").ap()
    dr["ctm"] = nc.dram_tensor("ctm", [128, 16 * 128], F32, kind="ExternalInput").ap()
    dr["crc"] = nc.dram_tensor("crc", [1, 64], F32, kind="ExternalInput").ap()
    dr["cid8"] = nc.dram_tensor("cid8", [128, 1024], F32, kind="ExternalInput").ap()
    dr["out"] = nc.dram_tensor("out", [KC, 128, S], F32, kind="ExternalOutput").ap()
    dr["xres"] = nc.dram_tensor("xres", [KC, 128, S], F32, kind="Internal").ap()
    dr["yn"] = nc.dram_tensor("yn", [16, 128, S], BF16, kind="Internal").ap()
    dr["wob"] = nc.dram_tensor("wob", [D, D], BF16, kind="Internal").ap()
    dr["wdb"] = nc.dram_tensor("wdb", [DFF, D], BF16, kind="Internal").ap()

    with ExitStack() as st:
        plan = []
        _emit_all(nc, st, dr, depth, dry=True, plan=plan, alloc=None)
        alloc = {}
        S_ = _emit_all(nc, st, dr, depth, dry=False, plan=plan, alloc=alloc)
        S_.emit(st)
    return nc


class WMgr:
    def __init__(self, S_, slots, plan, dry):
        self.S = S_
        self.slots = slots
        self.plan = plan
        self.dry = dry
        self.i = 0
        self.issued = 0

    def _issue(self, idx):
        spec = self.plan[idx]
        slot = self.slots[idx % len(self.slots)]
        for item in spec:
            o0, shape3, src = item[0], item[1], item[2]
            eng = item[3] if len(item) > 3 else "gpsimd"
            n = shape3[0] * shape3[1]
            dst = slot.ap[:, o0:o0 + n].rearrange("p (a b) -> p a b", a=shape3[0])
            self.S.dma(eng, dst, src, writes=[slot.b], track=False)

    def load(self, spec):
        if self.dry:
            self.plan.append(spec)
            self.i += 1
            return self.slots[(self.i - 1) % len(self.slots)]
        idx = self.i
        self.i += 1
        while self.issued < min(idx + 2, len(self.plan)):
            self._issue(self.issued)
            self.issued += 1
        return self.slots[idx % len(self.slots)]


def _emit_all(nc, st, dr, depth, dry, plan, alloc):
    S_ = Sched(nc, dry=dry)
    if dry:
        class _Fake:
            def __getitem__(self, k):
                return self

            def rearrange(self, *a, **k):
                return self

            def bitcast(self, *a):
                return self

            def unsqueeze(self, *a):
                return self

            def to_broadcast(self, *a):
                return self
        fake = _Fake()

        def sbt(name, shape, dt):
            return fake

        def pst(name, shape, dt):
            return fake
    else:
        def sbt(name, shape, dt):
            return st.enter_context(nc.sbuf_tensor(name, shape, dt))

        def pst(name, shape, dt):
            return st.enter_context(nc.psum_tensor(name, shape, dt))

    op = S_.op
    dma = S_.dma

    def tap(name, ap, bufs, shape, dt):
        if name not in DEBUG_ON or dry:
            return
        d_ = nc.dram_tensor("dbg_" + name, list(shape), dt, kind="ExternalOutput").ap()
        DEBUG_TAPS[name] = d_
        dma("sync", d_, ap, reads=list(bufs), writes=[S_.buf("dbg_" + name)])

    CM = sbt("CM" + NONCE, [128, 5 * 128], F32)
    bCM = S_.buf("CM")
    LE, GE, GT_, LT_, ONES = (CM[:, i * 128:(i + 1) * 128] for i in range(5))
    IDF = sbt("IDF", [128, 128], F32)
    bIDF = S_.buf("IDF")
    IDB = sbt("IDB", [128, 128], BF16)
    bIDB = S_.buf("IDB")
    RB = sbt("RB", [128, 2048], F32R)
    bRB = S_.buf("RB")
    RB3 = RB[:, :].rearrange("p (h l) -> p h l", h=16)
    GLR = sbt("GLR", [128, 256], F32R)
    bGLR = S_.buf("GLR")
    RCN = sbt("RCN", [128, 64], F32)
    bRCN = S_.buf("RCN")
    PP = sbt("PP", [128, depth * NPP], F32)
    bPP = S_.buf("PP")
    BC = sbt("BC", [128, NBC_SB], F32)
    bNG = S_.buf("BCng")
    bBC = S_.buf("BC")
    XB = sbt("XB", [128, KC * S], BF16)
    XB3 = XB[:, :].rearrange("p (k t) -> p k t", k=KC)
    bXB = S_.bufs("XB", NT)
    slots = [T(sbt(f"WS{i}", [128, 4096], BF16)[:, :], S_.buf(f"WS{i}")) for i in range(3)]
    ARENA = sbt("ARENA", [128, ARENA_WORDS], F32)
    AR = Arena(S_, ARENA)
    PS = pst("PS", [128, 8 * 512], F32)
    bPS = S_.bufs("PSB", 8, excl=True)
    W = WMgr(S_, slots, plan, dry)

    def bank(i, n=1):
        return PS[:, i * 512:(i + n) * 512]

    dma("sync", CM[:, :], dr["cmask"], writes=[bCM])
    dma("sync", IDF[:, :], dr["cid"], writes=[bIDF])
    dma("gpsimd", IDB[:, :], dr["cid"], writes=[bIDB])
    dma("sync", RCN[:, :], dr["crc"].to_broadcast([128, 64]), writes=[bRCN])
    dma("sync", PP[:, :], dr["pp"], writes=[bPP])
    dma("gpsimd", GLR[:, :], dr["cmask"][:, 256:512], writes=[bGLR])
    for nt in range(NT):
        dma("gpsimd", XB3[:, :, nt * 512:(nt + 1) * 512],
            dr["xT"].rearrange("k p t -> p k t")[:, :, nt * 512:(nt + 1) * 512], writes=[bXB[nt]])

    def pcol(l, off):
        return PP[:, l * NPP + off: l * NPP + off + 1]

    def win_src(l, c0, ncols):
        return dr["w_in"][l].rearrange("(k p) n -> p k n", p=128)[:, :, c0:c0 + ncols]

    def proj_fm(slot, so, ncols_in_slot, cidx, psbanks, pbufs):
        w3 = slot.ap[:, so:so + KC * ncols_in_slot].rearrange("p (k n) -> p k n", k=KC)
        for nt in range(NT):
            for kc in range(KC):
                last = (nt == NT - 1 and kc == KC - 1)
                op("tensor",
                   lambda e, nt=nt, kc=kc: e.matmul(bank(psbanks + nt), w3[:, kc, cidx * 128:(cidx + 1) * 128],
                                                    XB3[:, kc, nt * 512:(nt + 1) * 512],
                                                    start=(kc == 0), stop=(kc == KC - 1)),
                   reads=[slot.b] + bXB, writes=pbufs, inc=last)

    xres_src = dr["xT"]
    xres_bufs = [S_.bufs(f"xT{i}_", 2) for i in range(NT)]
    bXRES = [S_.bufs(f"xres{i}_", 2) for i in range(NT)]
    bYN = S_.bufs("yn", 2)
    bWOB = S_.buf("wob")
    bWDB = S_.buf("wdb")
    bOUT = S_.bufs("out", 2)

    for l in range(depth):
        S_.soft_barrier()
        AR.reset()
        SCR1 = AR.f32("SCR1", 2052)
        PAD2 = AR.f32("PAD2", 2052)
        SCR2 = AR.f32("SCR2", 2048)
        SCR3 = AR.bf("SCR3", 2048)
        MB2 = AR.bf("MB2", 2048)
        XTOK = AR.bf("XTOK", 16 * 512)
        bXTOK = S_.bufs("XTOKc", 16)
        XTOK3 = XTOK.ap.rearrange("p (c n) -> p c n", c=16)
        BTOK = AR.bf("BTOK", 16 * 128)
        bBTOK = S_.bufs("BTOKc", 16)
        BTOK3 = BTOK.ap.rearrange("p (c n) -> p c n", c=16)
        BT = AR.bf("BT", 2048)
        CT = AR.bf("CT", 2048)
        HPREV = AR.bf("HPREV", 16 * 512)
        bHPREV = S_.bufs("HPREVc", 16)
        HPREV3 = HPREV.ap.rearrange("p (c n) -> p c n", c=16)
        XDTFs = [AR.bf(f"XDTF{i}", 512) for i in range(2)]
        XDTBs = [AR.bf(f"XDTB{i}", 512) for i in range(2)]
        XDD = AR.bf("XDD", 512)
        HS = AR.f32("HS", 512)
        HBb = AR.bf("HBb", 512)
        DTt = AR.f32("DT", 1024)
        ADT = AR.f32("ADT", 1024)
        ACUM = T(SCR2.ap[:, 0:1024], SCR2.b)
        EXPA = AR.f32("EXPA", 1024)
        DTDE = AR.f32("DTDE", 1024)
        DECC = AR.f32("DECC", 1024)
        EA = AR.f32("EA", 64)
        SFB = AR.f32("SFB", 256)
        acc2_off = AR.off
        T1 = AR.f32("T1", 512)
        T2 = AR.f32("T2", 512)
        YT = AR.f32("YT", 512)
        SZ = AR.f32("SZ", 512)
        ACC2ap = ARENA[:, acc2_off:acc2_off + 2048]
        ACC2b = [T1.b, T2.b, YT.b, SZ.b]
        VV = AR.f32("V", 512)
        V2 = AR.f32("V2", 512)
        JK = AR.bf("JK", 512) if JK_BF else AR.f32("JK", 512)
        SS = AR.f32("SS", 4)
        VN = AR.bf("VN", 512)
        YNT = [AR.bf(f"YNT{i}", 512) for i in range(2)]
        ID8 = AR.bf("ID8", 1024)
        DIg = AR.bf("DIg", 1024)
        DI3 = DIg.ap.rearrange("p (h l) -> p h l", h=8)

        def v3(ap, c):
            return ap[:, c * 64:(c + 1) * 64]

        def hd_bc(t, c, h0):
            return t.ap[:, c * 64 + h0: c * 64 + h0 + 8].unsqueeze(2).to_broadcast([128, 8, 64])

        dma("sync", BC[:, 0:BC_NG], dr["bc"][l:l + 1, 0:BC_NG].to_broadcast([128, BC_NG]), writes=[bBC])
        dma("gpsimd", ID8.ap, dr["cid8"], writes=[ID8.b])
        wdt = W.load([(0, (KC, 64), win_src(l, DT0, 64))])
        wdt3 = wdt.ap[:, 0:KC * 64].rearrange("p (k n) -> p k n", k=KC)
        DTP = PS[:, 4 * 512: 4 * 512 + 1024]
        ACP = PS[:, 6 * 512: 6 * 512 + 1024]
        ATP = PS[:, 2 * 512: 2 * 512 + 1024]
        for i in range(NCH):
            for kc in range(KC):
                op("tensor", lambda e, i=i, kc=kc: e.matmul(DTP[:, i * 64:(i + 1) * 64],
                                                          XB3[:, kc, i * 128:(i + 1) * 128], wdt3[:, kc, :],
                                                          start=(kc == 0), stop=(kc == KC - 1)),
                   reads=[wdt.b] + bXB, writes=[bPS[4], bPS[5]], inc=(kc == KC - 1 and i == NCH - 1))
        op("vector", lambda e: e.tensor_tensor(DTt.ap.rearrange("p (c h) -> p c h", c=16),
                                               DTP.rearrange("p (c h) -> p c h", c=16),
                                               BC[:, BC_DTB:BC_DTB + 64].unsqueeze(1).to_broadcast([128, 16, 64]),
                                               ALU.add),
           reads=[bPS[4], bPS[5], bBC], writes=[DTt.b])
        op("scalar", lambda e: e.activation(DTt.ap, DTt.ap, AF.Exp), reads=[DTt.b], writes=[DTt.b])
        op("scalar", lambda e: e.activation(DTt.ap, DTt.ap, AF.Ln, bias=1.0), reads=[DTt.b], writes=[DTt.b])
        op("scalar", lambda e: e.activation(EA.ap, BC[:, BC_ALOG:BC_ALOG + 64], AF.Exp), reads=[bBC], writes=[EA.b])
        op("vector", lambda e: e.scalar_tensor_tensor(ADT.ap.rearrange("p (c h) -> p c h", c=16),
                                                      DTt.ap.rearrange("p (c h) -> p c h", c=16), -1.0,
                                                      EA.ap.unsqueeze(1).to_broadcast([128, 16, 64]),
                                                      ALU.mult, ALU.mult),
           reads=[DTt.b, EA.b], writes=[ADT.b])
        for c in range(NCH):
            op("tensor", lambda e, c=c: e.matmul(ACP[:, c * 64: c * 64 + 32], LE, ADT.ap[:, c * 64: c * 64 + 32],
                                                 start=True, stop=True),
               reads=[ADT.b, bCM], writes=[bPS[6], bPS[7]], inc=False)
            op("tensor", lambda e, c=c: e.matmul(ACP[:, c * 64 + 32: c * 64 + 64], GE,
                                                 ADT.ap[:, c * 64 + 32: c * 64 + 64], start=True, stop=True),
               reads=[ADT.b, bCM], writes=[bPS[6], bPS[7]], inc=False)
            op("tensor", lambda e, c=c: e.matmul(ATP[:, c * 64: c * 64 + 64], ONES, ADT.ap[:, c * 64: c * 64 + 64],
                                                 start=True, stop=True),
               reads=[ADT.b, bCM], writes=[bPS[2], bPS[3]], inc=(c == NCH - 1))
        op("scalar", lambda e: e.copy(ACUM.ap, ACP), reads=[bPS[6], bPS[7]], writes=[ACUM.b])
        op("scalar", lambda e: e.activation(EXPA.ap, ACUM.ap, AF.Exp), reads=[ACUM.b], writes=[EXPA.b])
        op("scalar", lambda e: e.activation(DECC.ap, ATP, AF.Exp), reads=[bPS[2], bPS[3]], writes=[DECC.b])
        op("vector", lambda e: e.tensor_tensor(DTDE.ap, ATP, ACUM.ap, ALU.subtract),
           reads=[bPS[2], bPS[3], ACUM.b], writes=[DTDE.b])
        op("scalar", lambda e: e.activation(DTDE.ap, DTDE.ap, AF.Exp), reads=[DTDE.b], writes=[DTDE.b])
        op("vector", lambda e: e.tensor_tensor(DTDE.ap, DTDE.ap, DTt.ap, ALU.mult),
           reads=[DTDE.b, DTt.b], writes=[DTDE.b])
        op("gpsimd", lambda e: e.memset(SCR1.ap[:, 0:2], 0.0), writes=[SCR1.b])
        op("gpsimd", lambda e: e.memset(SCR1.ap[:, 2050:2052], 0.0), writes=[SCR1.b])
        op("gpsimd", lambda e: e.memset(PAD2.ap[:, 0:2], 0.0), writes=[PAD2.b])
        op("gpsimd", lambda e: e.memset(PAD2.ap[:, 2050:2052], 0.0), writes=[PAD2.b])

        for g in range(4):
            S_.begin_region()
            wx = W.load([(0, (KC, 512), win_src(l, XBC0 + 512 * g, 512))])
            wbc = W.load([(0, (KC, 128), win_src(l, XBC0 + 2048 + 128 * g, 128)),
                          (KC * 128, (KC, 128), win_src(l, XBC0 + 2560 + 128 * g, 128))])
            if l == 0 and g == 0:
                tap("wx", wx.ap, [wx.b], [128, 4096], BF16)
            if g > 0:
                op("gpsimd", lambda e: e.memset(PAD2.ap[:, 0:2], 0.0), writes=[PAD2.b])
            tpi = 0

            def projA(cc):
                if cc < 4:
                    proj_fm(wx, 0, 512, cc, 0, bPS[0:4])
                elif cc == 4:
                    proj_fm(wbc, 0, 128, 0, 0, bPS[0:4])
                else:
                    proj_fm(wbc, KC * 128, 128, 0, 0, bPS[0:4])

            projA(0)
            for cc in range(6):
                cch = (4 * g + cc) if cc < 4 else ((16 + g) if cc == 4 else (20 + g))
                pad = SCR1 if (cc % 2 == 0 or not DB_A) else PAD2
                acc_ap = SCR2.ap if (cc % 2 == 0 or not DB_A) else ACC2ap
                acc_b = [SCR2.b] if (cc % 2 == 0 or not DB_A) else ACC2b
                op("scalar", lambda e, pad=pad: e.copy(pad.ap[:, 2:2050], bank(0, 4)), reads=bPS[0:4], writes=[pad.b])
                if cc < 5:
                    projA(cc + 1)
                op("scalar", lambda e, cch=cch, pad=pad, acc_ap=acc_ap: e.activation(
                    acc_ap, pad.ap[:, 0:2048], AF.Identity, bias=pcol(l, PP_CB + cch), scale=pcol(l, PP_CW + cch * 5)),
                   reads=[pad.b, bPP], writes=acc_b)
                for k in range(1, 5):
                    op("vector", lambda e, k=k, cch=cch, pad=pad, acc_ap=acc_ap: e.scalar_tensor_tensor(
                        acc_ap, pad.ap[:, k:k + 2048], pcol(l, PP_CW + cch * 5 + k), acc_ap, ALU.mult, ALU.add),
                       reads=[pad.b, bPP] + acc_b, writes=acc_b)
                dst = SCR3 if cc < 4 else (BT if cc == 4 else CT)
                op("scalar", lambda e, dst=dst, acc_ap=acc_ap: e.activation(dst.ap, acc_ap, AF.Silu),
                   reads=acc_b, writes=[dst.b])
                if l == 0 and g == 0 and cc == 0:
                    tap("xs0", SCR3.ap, [SCR3.b], [128, 2048], BF16)
                    tap("pad0", SCR1.ap, [SCR1.b], [128, 2052], F32)
                if cc < 5:
                    for q4 in range(4):
                        pb = 4 + (tpi % 2)
                        tpi += 1
                        TPv = bank(pb)[:, 0:256].bitcast(BF16)
                        for q in range(4):
                            i = q4 * 4 + q
                            op("tensor", lambda e, dst=dst, i=i, q=q, TPv=TPv: e.transpose(
                                TPv[:, q * 128:(q + 1) * 128], dst.ap[:, i * 128:(i + 1) * 128], IDB[:, :]),
                               reads=[dst.b, bIDB], writes=[bPS[pb]], inc=(q == 3))
                        if cc < 4:
                            op("scalar", lambda e, q4=q4, cc=cc, TPv=TPv: e.copy(
                                XTOK3[:, q4 * 4:q4 * 4 + 4, cc * 128:(cc + 1) * 128],
                                TPv.rearrange("p (a b) -> p a b", a=4)),
                               reads=[bPS[pb]], writes=bXTOK[q4 * 4:q4 * 4 + 4])
                        else:
                            op("scalar", lambda e, q4=q4, TPv=TPv: e.copy(
                                BTOK3[:, q4 * 4:q4 * 4 + 4, :], TPv.rearrange("p (a b) -> p a b", a=4)),
                               reads=[bPS[pb]], writes=bBTOK[q4 * 4:q4 * 4 + 4])

            wz = W.load([(0, (KC, 512), win_src(l, Z0 + 512 * g, 512))])
            wz3 = wz.ap[:, 0:KC * 512].rearrange("p (k n) -> p k n", k=KC)
            ko0, ko1 = 2 * g, 2 * g + 2
            dma("gpsimd", dr["wob"].rearrange("(k p) n -> p k n", p=128)[:, ko0:ko1, :],
                dr["w_out"][l].rearrange("(k p) n -> p k n", p=128)[:, ko0:ko1, :], writes=[bWOB])
            kd0, kd1 = 6 * g, min(6 * g + 6, FK)
            dma("gpsimd", dr["wdb"].rearrange("(k p) n -> p k n", p=128)[:, kd0:kd1, :],
                dr["w_down"][l].rearrange("(k p) n -> p k n", p=128)[:, kd0:kd1, :], writes=[bWDB])
            hf, hb = 8 * g, 32 + 8 * g
            R3 = SCR1.ap[:, 0:2048].rearrange("p (h l) -> p h l", h=16)
            Es = [SCR2, T(PAD2.ap[:, 0:2048], PAD2.b)]

            def state_update(c, h0):
                op("gpsimd", lambda e: e.tensor_tensor(XDD.ap.rearrange("p (h d) -> p h d", h=8),
                                                       XTOK3[:, c, :].rearrange("p (h d) -> p h d", h=8),
                                                       hd_bc(DTDE, c, h0), ALU.mult),
                   reads=[bXTOK[c], DTDE.b], writes=[XDD.b])
                op("tensor", lambda e: e.matmul(bank(5), BTOK3[:, c, :], XDD.ap, start=True, stop=True),
                   reads=[bBTOK[c], XDD.b], writes=[bPS[5]])
                op("vector", lambda e: e.tensor_tensor(HS.ap.rearrange("p (h d) -> p h d", h=8),
                                                       HS.ap.rearrange("p (h d) -> p h d", h=8),
                                                       hd_bc(DECC, c, h0), ALU.mult),
                   reads=[HS.b, DECC.b], writes=[HS.b])
                op("vector", lambda e: e.tensor_tensor(HS.ap, HS.ap, bank(5), ALU.add),
                   reads=[HS.b, bPS[5]], writes=[HS.b])

            dma("sync", BC[:, BC_NG:BC_NG + 512],
                dr["bc"][l:l + 1, BC_NG + 512 * g:BC_NG + 512 * (g + 1)].to_broadcast([128, 512]), writes=[bNG])
            op("gpsimd", lambda e: e.tensor_tensor(
                DI3, ID8.ap.rearrange("p (h l) -> p h l", h=8),
                BC[:, BC_DSK + hf: BC_DSK + hf + 8].unsqueeze(2).to_broadcast([128, 8, 128]), ALU.mult),
               reads=[ID8.b, bBC], writes=[DIg.b])
            op("gpsimd", lambda e: e.memset(HS.ap, 0.0), writes=[HS.b])
            for c in range(NCH):
                op("scalar", lambda e, c=c: e.copy(HPREV3[:, c, :], HS.ap), reads=[HS.b], writes=[bHPREV[c]])
                if c < NCH - 1:
                    state_update(c, hf)
            MBs = [SCR3, MB2]
            bSC = bPS[7]
            bTPY = bPS[7]

            def stage1(c, par):
                Mt = MBs[par]
                M3 = Mt.ap.rearrange("p (h l) -> p h l", h=16)
                xf, xb_ = XDTFs[par], XDTBs[par]
                E = Es[par]
                SC = bank(7)[:, 0:128]
                op("tensor", lambda e: e.matmul(SC, BT.ap[:, c * 128:(c + 1) * 128], CT.ap[:, c * 128:(c + 1) * 128],
                                                start=True, stop=True),
                   reads=[BT.b, CT.b], writes=[bSC])
                op("vector", lambda e: e.tensor_tensor(SFB.ap.rearrange("p (a l) -> p a l", a=2),
                                                       SC.unsqueeze(1).to_broadcast([128, 2, 128]),
                                                       CM[:, 0:256].rearrange("p (a l) -> p a l", a=2), ALU.mult),
                   reads=[bSC, bCM], writes=[SFB.b])
                op("gpsimd", lambda e: e.tensor_tensor(
                    RB3[:, 0:8, :], LE.unsqueeze(1).to_broadcast([128, 8, 128]),
                    ADT.ap[:, c * 64 + hf: c * 64 + hf + 8].unsqueeze(2).to_broadcast([128, 8, 128]), ALU.mult),
                   reads=[bCM, ADT.b], writes=[bRB])
                op("gpsimd", lambda e: e.tensor_tensor(
                    RB3[:, 8:16, :], GE.unsqueeze(1).to_broadcast([128, 8, 128]),
                    ADT.ap[:, c * 64 + hb: c * 64 + hb + 8].unsqueeze(2).to_broadcast([128, 8, 128]), ALU.mult),
                   reads=[bCM, ADT.b], writes=[bRB])
                op("gpsimd", lambda e: e.tensor_tensor(xf.ap.rearrange("p (h d) -> p h d", h=8),
                                                       XTOK3[:, c, :].rearrange("p (h d) -> p h d", h=8),
                                                       hd_bc(DTt, c, hf), ALU.mult),
                   reads=[bXTOK[c], DTt.b], writes=[xf.b])
                op("gpsimd", lambda e: e.tensor_tensor(xb_.ap.rearrange("p (h d) -> p h d", h=8),
                                                       XTOK3[:, c, :].rearrange("p (h d) -> p h d", h=8),
                                                       hd_bc(DTt, c, hb), ALU.mult),
                   reads=[bXTOK[c], DTt.b], writes=[xb_.b])
                for d_ in range(2):
                    lhs = GLR[:, 0:128] if d_ == 0 else GLR[:, 128:256]
                    for q in range(2):
                        op("tensor", lambda e, d_=d_, q=q, lhs=lhs: e.matmul(
                            bank(q), lhs, RB[:, d_ * 1024 + q * 512: d_ * 1024 + (q + 1) * 512],
                            start=True, stop=True),
                           reads=[bRB, bGLR], writes=[bPS[q]], inc=(q == 1))
                    op("scalar", lambda e, d_=d_: e.activation(E.ap[:, d_ * 1024:(d_ + 1) * 1024], bank(0, 2), AF.Exp),
                       reads=bPS[0:2], writes=[E.b])
                    op("vector", lambda e, d_=d_: e.tensor_tensor(
                        M3[:, d_ * 8:(d_ + 1) * 8, :],
                        E.ap[:, d_ * 1024:(d_ + 1) * 1024].rearrange("p (h l) -> p h l", h=8),
                        SFB.ap[:, d_ * 128:(d_ + 1) * 128].unsqueeze(1).to_broadcast([128, 8, 128]), ALU.mult),
                       reads=[E.b, SFB.b], writes=[Mt.b])

            def stage2(c, par, ci):
                Mt = MBs[par]
                M3 = Mt.ap.rearrange("p (h l) -> p h l", h=16)
                xf, xb_ = XDTFs[par], XDTBs[par]
                op("scalar", lambda e: e.copy(HBb.ap, HS.ap), reads=[HS.b], writes=[HBb.b])
                if c > 0:
                    state_update(c, hb)
                for h in range(8):
                    op("tensor", lambda e, h=h: e.matmul(bank(2)[:, h * 64:(h + 1) * 64], M3[:, h, :],
                                                         xf.ap[:, h * 64:(h + 1) * 64], start=True, stop=False),
                       reads=[Mt.b, xf.b], writes=[bPS[2]], inc=False)
                    op("tensor", lambda e, h=h: e.matmul(bank(2)[:, h * 64:(h + 1) * 64], M3[:, 8 + h, :],
                                                         xb_.ap[:, h * 64:(h + 1) * 64], start=False, stop=False),
                       reads=[Mt.b, xb_.b], writes=[bPS[2]], inc=False)
                    op("tensor", lambda e, h=h: e.matmul(bank(2)[:, h * 64:(h + 1) * 64], DI3[:, h, :],
                                                         XTOK3[:, c, h * 64:(h + 1) * 64], start=False, stop=True),
                       reads=[DIg.b, bXTOK[c]], writes=[bPS[2]], inc=(h == 7))
                op("tensor", lambda e: e.matmul(bank(3), CT.ap[:, c * 128:(c + 1) * 128], HPREV3[:, c, :],
                                                start=True, stop=True),
                   reads=[CT.b, bHPREV[c]], writes=[bPS[3]])
                op("tensor", lambda e: e.matmul(bank(4), CT.ap[:, c * 128:(c + 1) * 128], HBb.ap,
                                                start=True, stop=True),
                   reads=[CT.b, HBb.b], writes=[bPS[4]])
                for kc in range(KC):
                    op("tensor", lambda e, kc=kc: e.matmul(bank(6), XB3[:, kc, c * 128:(c + 1) * 128], wz3[:, kc, :],
                                                           start=(kc == 0), stop=(kc == KC - 1)),
                       reads=[wz.b] + bXB, writes=[bPS[6]], inc=(kc == KC - 1))
                op("vector", lambda e: e.tensor_tensor(T1.ap.rearrange("p (h d) -> p h d", h=8),
                                                       bank(3).rearrange("p (h d) -> p h d", h=8),
                                                       hd_bc(EXPA, c, hf), ALU.mult),
                   reads=[bPS[3], EXPA.b], writes=[T1.b])
                op("vector", lambda e: e.tensor_tensor(T2.ap.rearrange("p (h d) -> p h d", h=8),
                                                       bank(4).rearrange("p (h d) -> p h d", h=8),
                                                       hd_bc(EXPA, c, hb), ALU.mult),
                   reads=[bPS[4], EXPA.b], writes=[T2.b])
                op("gpsimd", lambda e: e.tensor_tensor(T1.ap, T1.ap, T2.ap, ALU.add), reads=[T1.b, T2.b], writes=[T1.b])
                op("vector", lambda e: e.tensor_tensor(YT.ap, T1.ap, bank(2), ALU.add), reads=[T1.b, bPS[2]], writes=[YT.b])
                op("scalar", lambda e: e.activation(SZ.ap, bank(6), AF.Silu), reads=[bPS[6]], writes=[SZ.b])
                op("vector", lambda e: e.tensor_tensor(VV.ap, YT.ap, SZ.ap, ALU.mult), reads=[YT.b, SZ.b], writes=[VV.b])
                op("gpsimd", lambda e: e.memset(SS.ap[:, 0:1], 0.0), writes=[SS.b])
                op("scalar", lambda e: e.activation(JK.ap, VV.ap, AF.Square, accum_out=SS.ap[:, 0:1]),
                   reads=[VV.b], writes=[JK.b, SS.b])
                op("scalar", lambda e: e.activation(SS.ap[:, 1:2], SS.ap[:, 0:1], AF.Ln, bias=RMS_EPS, scale=1.0 / 512),
                   reads=[SS.b], writes=[SS.b])
                op("scalar", lambda e: e.activation(SS.ap[:, 2:3], SS.ap[:, 1:2], AF.Exp, scale=-0.5),
                   reads=[SS.b], writes=[SS.b])
                op("scalar", lambda e: e.activation(V2.ap, VV.ap, AF.Copy, scale=SS.ap[:, 2:3]),
                   reads=[VV.b, SS.b], writes=[V2.b])
                op("vector", lambda e: e.tensor_tensor(VN.ap, V2.ap, BC[:, BC_NG: BC_NG + 512], ALU.mult),
                   reads=[V2.b, bNG], writes=[VN.b])
                TPY = bank(7)[:, 256:512].bitcast(BF16)
                for q in range(4):
                    op("tensor", lambda e, q=q: e.transpose(TPY[:, q * 128:(q + 1) * 128], VN.ap[:, q * 128:(q + 1) * 128],
                                                            IDB[:, :]),
                       reads=[VN.b, bIDB], writes=[bTPY], inc=(q == 3))
                ynt = YNT[ci % 2]
                op("scalar", lambda e: e.copy(ynt.ap, TPY), reads=[bTPY], writes=[ynt.b])
                dma("sync", dr["yn"][4 * g:4 * g + 4, :, c * 128:(c + 1) * 128].rearrange("q p t -> p q t"),
                    ynt.ap.rearrange("p (q t) -> p q t", q=4), reads=[ynt.b], writes=[bYN[ci % 2]], sembuf=ynt.b)

            op("gpsimd", lambda e: e.memset(HS.ap, 0.0), writes=[HS.b])
            if PIPELINE_B:
                stage1(NCH - 1, 0)
            for ci, c in enumerate(range(NCH - 1, -1, -1)):
                if PIPELINE_B:
                    if c > 0:
                        stage1(c - 1, (ci + 1) % 2)
                else:
                    stage1(c, ci % 2)
                stage2(c, ci % 2, ci)
            S_.end_region()

        S_.soft_barrier()
        AR.reset()
        MRG = AR.bf("MRG", KC * S)
        MRG3 = MRG.ap.rearrange("p (k t) -> p k t", k=KC)
        bMRG = S_.bufs("MRGj", KC)
        mrg_end = AR.off
        YN = AR.bf("YN", 16 * S)
        YN3 = YN.ap.rearrange("p (k t) -> p k t", k=16)
        GCs = [AR.f32(f"GC{i}", 2048) for i in range(2)]
        S_.begin_region()
        bYNl = S_.bufs("YNl", 4)
        for k4 in range(4):
            dma("sync", YN3[:, 4 * k4:4 * k4 + 4, :], dr["yn"][4 * k4:4 * k4 + 4].rearrange("k p t -> p k t"),
                reads=bYN, writes=[bYNl[k4]])
        for j in range(KC):
            wp = W.load([(0, (16, 128), dr["w_ssd_proj"][l].rearrange("(k p) n -> p k n", p=128)[:, :, j * 128:(j + 1) * 128])])
            wp3 = wp.ap[:, 0:2048].rearrange("p (k n) -> p k n", k=16)
            wg = W.load([(0, (KC, 128), win_src(l, G0 + 1024 + j * 128, 128))])
            for nt in range(NT):
                for kc in range(16):
                    op("tensor", lambda e, nt=nt, kc=kc: e.matmul(bank(nt), wp3[:, kc, :], YN3[:, kc, nt * 512:(nt + 1) * 512],
                                                                  start=(kc == 0), stop=(kc == 15)),
                       reads=[wp.b, bYNl[kc // 4]], writes=bPS[0:4], inc=(nt == NT - 1 and kc == 15))
            G = GCs[j % 2]
            proj_fm(wg, 0, 128, 0, 4, bPS[4:8])
            op("scalar", lambda e: e.activation(G.ap, bank(4, 4), AF.Sigmoid), reads=bPS[4:8], writes=[G.b])
            op("vector", lambda e, j=j: e.tensor_tensor(MRG3[:, j, :], G.ap, bank(0, 4), ALU.mult),
               reads=[G.b] + bPS[0:4], writes=[bMRG[j]])
        S_.end_region()

        if l == 0:
            tap("mrgC", MRG.ap, bMRG, [128, KC * S], BF16)
            tap("yn", YN.ap, bYNl, [128, 16 * S], BF16)
        S_.soft_barrier()
        AR.reset(mrg_end)
        P0s = [AR.f32(f"P0{i}", 2064) for i in range(2)]
        Q1s = [AR.f32(f"Q1{i}", 2064) for i in range(2)]
        Q2s = [AR.f32(f"Q2{i}", 2064) for i in range(2)]
        PLD = [AR.bf(f"PLD{i}", 2048) for i in range(2)]
        Gs = [AR.f32(f"G{i}", 2048) for i in range(2)]
        TMPs = [AR.f32(f"TMP{i}", 2048) for i in range(2)]
        TEs = [AR.f32(f"TE{i}", 16) for i in range(2)]
        S_.begin_region()
        for P0 in P0s:
            op("gpsimd", lambda e: e.memset(P0.ap[:, 0:8], 0.0), writes=[P0.b])
            op("gpsimd", lambda e: e.memset(P0.ap[:, 2056:2064], 0.0), writes=[P0.b])
        for gi, w_ in enumerate(POOL_WINDOWS):
            half = w_ // 2
            wu = W.load([(0, (KC, 256), win_src(l, U0 + 256 * gi, 256))])
            for k2 in range(2):
                pb0 = 4 * (k2 % 2)
                P0, Q1, Q2, TE = P0s[k2], Q1s[k2], Q2s[k2], TEs[k2]
                proj_fm(wu, 0, 256, k2, pb0, bPS[pb0:pb0 + 4])
                op("scalar", lambda e, pb0=pb0: e.copy(P0.ap[:, 8:2056], bank(pb0, 4)), reads=bPS[pb0:pb0 + 4], writes=[P0.b])
                src, dst = P0, Q1
                sh = 1
                while sh < w_:
                    op("vector", lambda e, src=src, dst=dst, sh=sh: e.tensor_tensor(
                        dst.ap[:, sh:2064], src.ap[:, sh:2064], src.ap[:, 0:2064 - sh], ALU.add),
                       reads=[src.b], writes=[dst.b])
                    src = dst
                    dst = Q2 if dst is Q1 else Q1
                    sh *= 2
                o = 8 + half - 1
                pl = PLD[k2]
                op("vector", lambda e, src=src, o=o, pl=pl, w_=w_: e.scalar_tensor_tensor(
                    pl.ap, src.ap[:, o:o + 2048], 1.0 / w_, P0.ap[:, 8:2056], ALU.mult, ALU.subtract),
                   reads=[src.b, P0.b], writes=[pl.b])
                nl, nr = half, half - 1
                op("vector", lambda e, src=src, o=o, gi=gi, nl=nl: e.tensor_tensor(
                    TE.ap[:, 0:nl], src.ap[:, o:o + nl], RCN[:, gi * 16: gi * 16 + nl], ALU.mult),
                   reads=[src.b, bRCN], writes=[TE.b])
                op("vector", lambda e, pl=pl, nl=nl: e.tensor_tensor(pl.ap[:, 0:nl], TE.ap[:, 0:nl], P0.ap[:, 8:8 + nl],
                                                                      ALU.subtract),
                   reads=[TE.b, P0.b], writes=[pl.b])
                if nr > 0:
                    op("vector", lambda e, src=src, o=o, gi=gi, nr=nr: e.tensor_tensor(
                        TE.ap[:, 8:8 + nr], src.ap[:, o + 2048 - nr:o + 2048],
                        RCN[:, gi * 16 + 8: gi * 16 + 8 + nr], ALU.mult),
                       reads=[src.b, bRCN], writes=[TE.b])
                    op("vector", lambda e, pl=pl, nr=nr: e.tensor_tensor(
                        pl.ap[:, 2048 - nr:2048], TE.ap[:, 8:8 + nr], P0.ap[:, 8 + 2048 - nr:8 + 2048], ALU.subtract),
                       reads=[TE.b, P0.b], writes=[pl.b])
            wm = W.load([(0, (2, 256), dr["pool_w"][l, gi].rearrange("(k p) n -> p k n", p=128))])
            wm3 = wm.ap[:, 0:512].rearrange("p (k n) -> p k n", k=2)
            for jj in range(2):
                j = 2 * gi + jj
                for nt in range(NT):
                    for k2 in range(2):
                        op("tensor", lambda e, nt=nt, k2=k2, jj=jj: e.matmul(
                            bank(nt), wm3[:, k2, jj * 128:(jj + 1) * 128], PLD[k2].ap[:, nt * 512:(nt + 1) * 512],
                            start=(k2 == 0), stop=(k2 == 1)),
                           reads=[wm.b, PLD[0].b, PLD[1].b], writes=bPS[0:4], inc=(nt == NT - 1 and k2 == 1))
                wg = W.load([(0, (KC, 128), win_src(l, G0 + j * 128, 128))])
                G, TMP = Gs[jj], TMPs[jj]
                proj_fm(wg, 0, 128, 0, 4, bPS[4:8])
                op("scalar", lambda e: e.activation(G.ap, bank(4, 4), AF.Sigmoid), reads=bPS[4:8], writes=[G.b])
                op("vector", lambda e, j=j: e.scalar_tensor_tensor(TMP.ap, bank(0, 4), pcol(l, PP_PS + j), G.ap,
                                                                   ALU.mult, ALU.mult),
                   reads=bPS[0:4] + [G.b, bPP], writes=[TMP.b])
                op("gpsimd", lambda e, j=j: e.tensor_tensor(MRG3[:, j, :], TMP.ap, MRG3[:, j, :], ALU.add),
                   reads=[TMP.b, bMRG[j]], writes=[bMRG[j]])
        S_.end_region()

        def outproj_ln(rhs3, rhs_bufs, nK, wsrc, goff, boff, final):
            nonlocal xres_src, xres_bufs
            SUMt = AR.f32("SUMt", KC * 512)
            SUM3 = SUMt.ap.rearrange("p (k t) -> p k t", k=KC)
            XR = [AR.f32(f"XR{i}", 512) for i in range(2)]
            SQ = [AR.f32(f"SQ{i}", 512) for i in range(2)]
            MEAN = AR.f32("MEAN", 512)
            M2 = AR.f32("M2", 512)
            RSTD = AR.f32("RSTD", 512)
            TA = [AR.f32(f"TA{i}", 512) for i in range(2)]
            XN = [AR.f32(f"XN{i}", 512) for i in range(2)]
            for lst, nm in ((TA, "TA"), (XN, "XN"), (SQ, "SQ")):
                while len(lst) < 4 and ARENA_WORDS - AR.off >= 512 * 8:
                    lst.append(AR.f32(f"{nm}{len(lst)}", 512))
            while len(XR) < 8 and ARENA_WORDS - AR.off >= 512:
                XR.append(AR.f32(f"XR{len(XR)}", 512))
            nxr = len(XR)
            kgroups = [(k0, min(4, nK - k0)) for k0 in range(0, nK, 4)]
            out_dst = dr["out"] if final else dr["xres"]
            S_.begin_region()
            for nt in range(NT):
                for (k0, nk) in kgroups:
                    ws = W.load([(0, (nk, 1024), wsrc.rearrange("(k p) n -> p k n", p=128)[:, k0:k0 + nk, :], "sync")])
                    ws3 = ws.ap[:, 0:nk * 1024].rearrange("p (k n) -> p k n", k=nk)
                    for j in range(KC):
                        for kk in range(nk):
                            kc = k0 + kk
                            op("tensor", lambda e, j=j, kk=kk, kc=kc, ws3=ws3: e.matmul(
                                bank(j), ws3[:, kk, j * 128:(j + 1) * 128], rhs3[:, kc, nt * 512:(nt + 1) * 512],
                                start=(kc == 0), stop=(kc == nK - 1)),
                               reads=[ws.b] + rhs_bufs, writes=[bPS[j]], inc=(kk == nk - 1))
                for j in range(KC):
                    xr = XR[(nt * KC + j) % nxr]
                    dma("sync", xr.ap, xres_src[j, :, nt * 512:(nt + 1) * 512], reads=xres_bufs[nt], writes=[xr.b])
                    op("vector", lambda e, j=j, xr=xr: e.scalar_tensor_tensor(SUM3[:, j, :], xr.ap, float(ALPHA), bank(j),
                                                                              ALU.mult, ALU.add),
                       reads=[xr.b, bPS[j]], writes=[SUMt.b])
                for j in range(KC):
                    op("tensor", lambda e, j=j: e.matmul(bank(0), ONES, SUM3[:, j, :], start=(j == 0), stop=(j == KC - 1)),
                       reads=[SUMt.b, bCM], writes=[bPS[0]], inc=(j == KC - 1))
                for j in range(KC):
                    sq = SQ[j % len(SQ)]
                    op("scalar", lambda e, j=j, sq=sq: e.activation(sq.ap, SUM3[:, j, :], AF.Square),
                       reads=[SUMt.b], writes=[sq.b])
                    op("tensor", lambda e, j=j, sq=sq: e.matmul(bank(1), ONES, sq.ap, start=(j == 0), stop=(j == KC - 1)),
                       reads=[sq.b, bCM], writes=[bPS[1]])
                op("vector", lambda e: e.tensor_scalar(MEAN.ap, bank(0), 1.0 / D, None, ALU.mult),
                   reads=[bPS[0]], writes=[MEAN.b])
                op("vector", lambda e: e.tensor_tensor(M2.ap, MEAN.ap, MEAN.ap, ALU.mult), reads=[MEAN.b], writes=[M2.b])
                op("vector", lambda e: e.scalar_tensor_tensor(RSTD.ap, bank(1), 1.0 / D, M2.ap, ALU.mult, ALU.subtract),
                   reads=[bPS[1], M2.b], writes=[RSTD.b])
                op("scalar", lambda e: e.activation(RSTD.ap, RSTD.ap, AF.Ln, bias=LN_EPS), reads=[RSTD.b], writes=[RSTD.b])
                op("scalar", lambda e: e.activation(RSTD.ap, RSTD.ap, AF.Exp, scale=-0.5), reads=[RSTD.b], writes=[RSTD.b])
                for j in range(KC):
                    ta = TA[j % len(TA)]
                    xn = XN[j % len(XN)]
                    op("vector", lambda e, j=j, ta=ta: e.tensor_tensor(ta.ap, SUM3[:, j, :], MEAN.ap, ALU.subtract),
                       reads=[SUMt.b, MEAN.b], writes=[ta.b])
                    op("vector", lambda e, ta=ta: e.tensor_tensor(ta.ap, ta.ap, RSTD.ap, ALU.mult),
                       reads=[ta.b, RSTD.b], writes=[ta.b])
                    op("scalar", lambda e, j=j, ta=ta, xn=xn: e.activation(xn.ap, ta.ap, AF.Identity,
                                                                           bias=pcol(l, boff + j), scale=pcol(l, goff + j)),
                       reads=[ta.b, bPP], writes=[xn.b])
                    if not final:
                        op("scalar", lambda e, j=j, ta=ta: e.activation(XB3[:, j, nt * 512:(nt + 1) * 512], ta.ap, AF.Identity,
                                                                        bias=pcol(l, boff + j), scale=pcol(l, goff + j)),
                           reads=[ta.b, bPP], writes=[bXB[nt]])
                    dma("sync", out_dst[j, :, nt * 512:(nt + 1) * 512], xn.ap, reads=[xn.b],
                        writes=[(bOUT if final else bXRES[nt])[j % 2]], sembuf=xn.b)
            S_.end_region()
            if not final:
                xres_src = dr["xres"]
                xres_bufs = bXRES

        if l == 0:
            tap("mrgD", MRG.ap, bMRG, [128, KC * S], BF16)
        S_.soft_barrier()
        AR.reset(mrg_end)
        outproj_ln(MRG3, bMRG, KC, dr["wob"], PP_L1G, PP_L1B, final=False)
        if l == 0:
            tap("xb1", XB[:, :], bXB, [128, KC * S], BF16)

        S_.soft_barrier()
        AR.reset()
        HB = AR.bf("HB", FK * S)
        HB3 = HB.ap.rearrange("p (k t) -> p k t", k=FK)
        bHB = S_.bufs("HBk", FK)
        hb_end = AR.off
        PADFs = [AR.f32(f"PADF{i}", 2050) for i in range(2)]
        ACCF = AR.f32("ACCF", 2048)
        GTt = AR.f32("GT", 2048)
        for PADF in PADFs:
            op("gpsimd", lambda e: e.memset(PADF.ap[:, 0:1], 0.0), writes=[PADF.b])
            op("gpsimd", lambda e: e.memset(PADF.ap[:, 2049:2050], 0.0), writes=[PADF.b])
        S_.begin_region()
        for q in range(6):
            ncol = 512 if q < 5 else 256
            wgs = W.load([(0, (KC, ncol), dr["w_up"][l].rearrange("(k p) n -> p k n", p=128)[:, :, 512 * q:512 * q + ncol])])
            wvs = W.load([(0, (KC, ncol), dr["w_up"][l].rearrange("(k p) n -> p k n", p=128)[:, :, DFF + 512 * q:DFF + 512 * q + ncol])])
            for kk in range(ncol // 128):
                k = 4 * q + kk
                for half_, wsl in enumerate((wgs, wvs)):
                    pb0 = 4 * half_
                    PADF = PADFs[half_ if DB_F else 0]
                    cch = k + FK * half_
                    proj_fm(wsl, 0, ncol, kk, pb0, bPS[pb0:pb0 + 4])
                    op("scalar", lambda e, pb0=pb0: e.copy(PADF.ap[:, 1:2049], bank(pb0, 4)),
                       reads=bPS[pb0:pb0 + 4], writes=[PADF.b])
                    AC = GTt if half_ == 0 else ACCF
                    op("scalar", lambda e, cch=cch, AC=AC: e.activation(AC.ap, PADF.ap[:, 0:2048], AF.Identity,
                                                                        bias=pcol(l, PP_FB + cch), scale=pcol(l, PP_FW + cch * 3)),
                       reads=[PADF.b, bPP], writes=[AC.b])
                    for t_ in range(1, 3):
                        op("vector", lambda e, t_=t_, cch=cch, AC=AC: e.scalar_tensor_tensor(
                            AC.ap, PADF.ap[:, t_:t_ + 2048], pcol(l, PP_FW + cch * 3 + t_), AC.ap, ALU.mult, ALU.add),
                           reads=[PADF.b, AC.b, bPP], writes=[AC.b])
                    if half_ == 0:
                        op("scalar", lambda e: e.activation(GTt.ap, GTt.ap, AF.Gelu), reads=[GTt.b], writes=[GTt.b])
                    else:
                        op("vector", lambda e, k=k: e.tensor_tensor(HB3[:, k, :], GTt.ap, ACCF.ap, ALU.mult),
                           reads=[GTt.b, ACCF.b], writes=[bHB[k]])
        S_.end_region()

        if l == 0:
            tap("hb", HB.ap, bHB, [128, FK * S], BF16)
        S_.soft_barrier()
        AR.reset(hb_end)
        outproj_ln(HB3, bHB, FK, dr["wdb"], PP_L2G, PP_L2B, final=(l == depth - 1))

    S_.barrier()
    return S_


def _consts():
    r = np.arange(128)[:, None]
    c = np.arange(128)[None, :]
    le = (r <= c).astype(np.float32)
    ge = (r >= c).astype(np.float32)
    gt = (r > c).astype(np.float32)
    lt = (r < c).astype(np.float32)
    ones = np.ones((128, 128), np.float32)
    cmask = np.concatenate([le, ge, gt, lt, ones], axis=1)
    cid = np.eye(128, dtype=np.float32)
    ctm = np.concatenate([le] * 8 + [ge] * 8, axis=1)
    crc = np.zeros((1, 64), np.float32)
    t = np.arange(S)
    for gi, w in enumerate(POOL_WINDOWS):
        half = w // 2
        cnt = np.minimum(t + half - 1, S - 1) - np.maximum(t - half, 0) + 1
        rc = (1.0 / cnt).astype(np.float32)
        crc[0, gi * 16: gi * 16 + half] = rc[:half]
        if half > 1:
            crc[0, gi * 16 + 8: gi * 16 + 8 + half - 1] = rc[S - (half - 1):]
    return cmask, cid, ctm, crc


def _pack_params(inp, depth):
    pp = np.zeros((128, depth, NPP), np.float32)
    bc = np.zeros((depth, NBC), np.float32)
    for l in range(depth):
        pp[:, l, PP_CW:PP_CW + 120] = inp["ssd_conv_w"][l].T.reshape(24, 128, 5).transpose(1, 0, 2).reshape(128, 120)
        pp[:, l, PP_CB:PP_CB + 24] = inp["ssd_conv_b"][l].reshape(24, 128).T
        pp[:, l, PP_FW:PP_FW + 132] = inp["ffn_conv_w"][l].T.reshape(44, 128, 3).transpose(1, 0, 2).reshape(128, 132)
        pp[:, l, PP_FB:PP_FB + 44] = inp["ffn_conv_b"][l].reshape(44, 128).T
        pp[:, l, PP_PS:PP_PS + 8] = inp["pool_scale"][l].reshape(8, 128).T
        pp[:, l, PP_L1G:PP_L1G + 8] = inp["ln1_g"][l].reshape(8, 128).T
        pp[:, l, PP_L1B:PP_L1B + 8] = inp["ln1_b"][l].reshape(8, 128).T
        pp[:, l, PP_L2G:PP_L2G + 8] = inp["ln2_g"][l].reshape(8, 128).T
        pp[:, l, PP_L2B:PP_L2B + 8] = inp["ln2_b"][l].reshape(8, 128).T
        bc[l, BC_ALOG:BC_ALOG + 64] = inp["a_log"][l].reshape(64)
        bc[l, BC_DTB:BC_DTB + 64] = inp["dt_bias"][l].reshape(64)
        bc[l, BC_DSK:BC_DSK + 32] = inp["d_skip"][l]
        bc[l, BC_NG:BC_NG + 2048] = inp["ssd_norm_g"][l]
    return np.ascontiguousarray(pp.reshape(128, depth * NPP)), bc


def run(inputs, depth=DEPTH, n_cores=8, trace=False):
    inp = {k: np.asarray(v, dtype=np.float32) for k, v in inputs.items()}
    x = inp["x"]
    nb = x.shape[0]
    cmask, cid, ctm, crc = _consts()
    pp, bc = _pack_params(inp, depth)
    shared = {
        "w_in": np.ascontiguousarray(inp["w_in"][:depth]),
        "pool_w": np.ascontiguousarray(inp["pool_w"][:depth]),
        "w_ssd_proj": np.ascontiguousarray(inp["w_ssd_proj"][:depth]),
        "w_out": np.ascontiguousarray(inp["w_out"][:depth]),
        "w_up": np.ascontiguousarray(inp["w_up"][:depth]),
        "w_down": np.ascontiguousarray(inp["w_down"][:depth]),
        "pp": pp, "bc": bc, "cmask": cmask, "cid": cid, "ctm": ctm, "crc": crc,
        "cid8": np.ascontiguousarray(np.tile(cid, (1, 8))),
    }
    in_maps = []
    for b in range(nb):
        m = dict(shared)
        m["xT"] = np.ascontiguousarray(x[b].T).reshape(KC, 128, S)
        in_maps.append(m)
    nc = build_program(depth)
    res = run_bass_kernel_spmd(nc, in_maps, core_ids=list(range(nb)), trace=trace)
    global LAST_DBG
    LAST_DBG = {k: np.asarray(res.results[0]["dbg_" + k]) for k in DEBUG_TAPS}
    outs = [np.asarray(r["out"]).reshape(D, S).T for r in res.results]
    return np.ascontiguousarray(np.stack(outs, axis=0).astype(np.float32)), res


def kernel(**inputs):
    out, _ = run(inputs, depth=DEPTH)
    return out
```

```python
import numpy as np
import concourse.bass as bass
import concourse.mybir as mybir
from concourse.bass_utils import run_bass_kernel_spmd
from contextlib import ExitStack

F32 = mybir.dt.float32
BF16 = mybir.dt.bfloat16
F32R = mybir.dt.float32r
AF = mybir.ActivationFunctionType
ALU = mybir.AluOpType

D = 1024
KC = 8
S = 2048
NT = 4
DEPTH = 4
NCH = 16
DFF = 2816
FK = 22
U0, Z0, XBC0, DT0, G0 = 0, 1024, 3072, 6144, 6208
ALPHA = (2 * DEPTH) ** 0.25
LN_EPS = 1e-5
RMS_EPS = 1e-5
POOL_WINDOWS = (2, 4, 8, 16)

PP_CW = 0
PP_CB = PP_CW + 120
PP_FW = PP_CB + 24
PP_FB = PP_FW + 132
PP_PS = PP_FB + 44
PP_L1G = PP_PS + 8
PP_L1B = PP_L1G + 8
PP_L2G = PP_L1B + 8
PP_L2B = PP_L2G + 8
NPP = PP_L2B + 8
BC_ALOG = 0
BC_DTB = 64
BC_DSK = 128
BC_NG = 160
NBC = BC_NG + 2048
NBC_SB = BC_NG + 512

SEM_GEN = 30000
PIPELINE_B = True
USE_REGIONS = True
SOFT_BARRIERS = False
SCHED_EPS = 3.0
LAT_TAIL = 0.7
WIN_REGION = 1500
NONCE = ""
DB_A = True
DB_F = True
JK_BF = True
ARENA_WORDS = 32256


class Buf:
    __slots__ = ("name", "w", "r", "dsem", "dcount", "excl")

    def __init__(self, name, excl=False):
        self.name = name
        self.w = None
        self.r = []
        self.dsem = None
        self.dcount = 0
        self.excl = excl


class Eng:
    def __init__(self, name):
        self.name = name
        self.count = 0
        self.pending = False
        self.ops = []
        self.waited = {}

    def semkey(self, cnt):
        return ("E", self.name, (cnt - 1) // SEM_GEN)


class _Rec:
    def __init__(self):
        self.call = None

    def __getattr__(self, name):
        def f(*a, **k):
            self.call = (name, a, k)
            return None
        return f


class Sched:
    def __init__(self, nc, dry=False):
        self.nc = nc
        self.dry = dry
        self.eng = {n: Eng(n) for n in ("tensor", "vector", "scalar", "gpsimd", "sync")}
        self.semkeys = {}
        self.nbuf = 0
        self.dma_out = []
        self.fence_toks = []
        self.fence_dma = {}
        self.dsem_count = {}
        self.region = None
        self._pend = {}

    def buf(self, name=None, excl=False):
        self.nbuf += 1
        b = Buf(name or f"b{self.nbuf}", excl)
        b.r = list(self.fence_toks)
        return b

    def bufs(self, name, n, excl=False):
        return [self.buf(f"{name}{i}", excl) for i in range(n)]

    @staticmethod
    def _tok_local(tok):
        key, val = tok
        if key[0] == "E":
            return key, val - key[2] * SEM_GEN
        return key, val

    def _need(self, e, toks):
        best = {}
        for t in toks:
            if t is None:
                continue
            if t[0][0] == "E" and t[0][1] == e.name and t[1] > e.count:
                continue
            key, val = self._tok_local(t)
            if e.waited.get(key, 0) >= val:
                continue
            if best.get(key, 0) < val:
                best[key] = val
        for k, v in best.items():
            e.waited[k] = v
            self.semkeys[k] = None
        return list(best.items())

    def op(self, engname, fn, reads=(), writes=(), inc=True):
        if self.dry:
            return None
        rec = _Rec()
        fn(rec)
        if self.region is not None:
            pend = self._pend.setdefault(engname, [])
            pend.append((rec.call, list(reads), list(writes)))
            if inc:
                self.region.append(("op", engname, pend))
                self._pend[engname] = []
            return None
        return self._op_core(engname, rec.call, reads, writes, inc)

    def _op_core(self, engname, call, reads, writes, inc):
        e = self.eng[engname]
        xr = [b for b in reads if b.excl]
        if xr:
            reads = [b for b in reads if not b.excl]
            writes = list(writes) + [b for b in xr if b not in writes]
        deps = []
        for b in reads:
            deps.append(b.w)
        for b in writes:
            deps.append(b.w)
            deps.extend(b.r)
        waits = self._need(e, deps)
        if inc:
            e.count += 1
            e.pending = False
            tokval = e.count
        else:
            e.pending = True
            tokval = e.count + 1
        key = e.semkey(tokval)
        tok = (key, tokval)
        self.semkeys[key] = None
        for b in reads:
            b.r.append(tok)
        for b in writes:
            b.w = tok
            b.r = []
        e.ops.append((waits, call, key if inc else None, None))
        return tok

    def dma(self, engname, out_ap, in_ap, reads=(), writes=(), sembuf=None, track=True):
        if self.dry:
            return None
        if self.region is not None:
            self.region.append(("dma", engname, (out_ap, in_ap, list(reads), list(writes), sembuf, track)))
            return None
        return self._dma_core(engname, out_ap, in_ap, reads, writes, sembuf, track)

    def _dma_core(self, engname, out_ap, in_ap, reads, writes, sembuf, track):
        e = self.eng[engname]
        sb = sembuf or writes[0]
        if sb.dsem is None:
            sb.dsem = ("D", sb.name)
        deps = []
        for b in reads:
            deps.append(b.w)
        for b in writes:
            deps.append(b.w)
            deps.extend(b.r)
        waits = self._need(e, deps)
        self.dsem_count[sb.dsem] = self.dsem_count.get(sb.dsem, 0) + 1
        tok = (sb.dsem, 16 * self.dsem_count[sb.dsem])
        self.semkeys[sb.dsem] = None
        for b in reads:
            b.r.append(tok)
        for b in writes:
            b.w = tok
            b.r = []

        e.ops.append((waits, ("dma_start", (), {"out": out_ap, "in_": in_ap}), None, sb.dsem))
        if track:
            self.dma_out.append(tok)
        return tok

    def begin_region(self):
        if self.dry or not USE_REGIONS:
            return
        assert self.region is None
        self.region = []
        self._pend = {}

    @staticmethod
    def _free(ap):
        n = 1
        for d in ap.shape[1:]:
            n *= d
        return n

    def _est(self, rec):
        kind, engname, body = rec
        if kind == "dma":
            return 0.15
        t = 0.0
        for call, _, _ in body:
            name, a, k = call
            out = a[0] if a else k.get("out")
            try:
                n = self._free(out)
            except Exception:
                n = 512
            if engname == "tensor":
                if name == "matmul":
                    d = max(n, 64) / 2400.0 + 0.03
                    if a[1].dtype == F32:
                        d *= 4.0
                    t += d
                else:
                    t += 0.1
            elif engname == "vector":
                t += (n + 150) / 960.0
            elif engname == "scalar":
                t += (n + 224) / 1200.0
            elif engname == "gpsimd":
                t += (2 * n + 150) / 960.0
            else:
                t += 0.1
        return t

    def end_region(self):
        if self.dry or not USE_REGIONS:
            return
        recs = self.region
        self.region = None
        for en, p in self._pend.items():
            assert not p, f"pending un-inc'd ops on {en} at region end"
        n = len(recs)
        last_w = {}
        readers = {}
        deps = [set() for _ in range(n)]
        for i, (kind, engname, body) in enumerate(recs):
            if kind == "dma":
                rr, ww = body[2], body[3]
            else:
                rr = [b for c in body for b in c[1]]
                ww = [b for c in body for b in c[2]]
            R = [b for b in rr if not b.excl]
            Wr = list(ww) + [b for b in rr if b.excl]
            for b in R:
                if id(b) in last_w:
                    deps[i].add(last_w[id(b)])
            for b in Wr:
                if id(b) in last_w:
                    deps[i].add(last_w[id(b)])
                deps[i].update(readers.get(id(b), ()))
            for b in R:
                readers.setdefault(id(b), []).append(i)
            for b in Wr:
                last_w[id(b)] = i
                readers[id(b)] = []
            deps[i].discard(i)
        dur = [self._est(r) for r in recs]
        succ = [[] for _ in range(n)]
        indeg = [0] * n
        for i in range(n):
            for d in deps[i]:
                succ[d].append(i)
            indeg[i] = len(deps[i])
        tail = [0.0] * n
        for i in range(n - 1, -1, -1):
            m = 0.0
            for j in succ[i]:
                if tail[j] > m:
                    m = tail[j]
            tail[i] = dur[i] + (m + LAT_TAIL if succ[i] else 0.0)
        ready_t = [0.0] * n
        ready = [i for i in range(n) if indeg[i] == 0]
        eng_free = {}
        order = []
        LAT = 1.2
        WIN = 48
        done = [False] * n
        lo = 0
        while ready:
            while lo < n and done[lo]:
                lo += 1
            cands = []
            emin = None
            for i in ready:
                if i > lo + WIN_REGION:
                    continue
                est = max(eng_free.get(recs[i][1], 0.0), ready_t[i])
                cands.append((est, i))
                if emin is None or est < emin:
                    emin = est
            if not cands:
                i = min(ready)
                est = max(eng_free.get(recs[i][1], 0.0), ready_t[i])
            else:
                best = None
                for est_i, i_ in cands:
                    if est_i <= emin + SCHED_EPS:
                        key = (-tail[i_], est_i, i_)
                        if best is None or key < best:
                            best = key
                i = best[2]
                est = best[1]
            ready.remove(i)
            done[i] = True
            fin = est + dur[i]
            eng_free[recs[i][1]] = fin
            if recs[i][0] == "dma":
                fin += 2.0
            order.append(i)
            for j in succ[i]:
                indeg[j] -= 1
                ready_t[j] = max(ready_t[j], fin + LAT)
                if indeg[j] == 0:
                    ready.append(j)
        assert len(order) == n
        for i in order:
            kind, engname, body = recs[i]
            if kind == "dma":
                self._dma_core(engname, *body)
            else:
                for k, (call, rr, ww) in enumerate(body):
                    self._op_core(engname, call, rr, ww, k == len(body) - 1)

    def soft_barrier(self):
        if self.dry:
            return
        if not SOFT_BARRIERS:
            return self.barrier()
        assert self.region is None
        for t in self.dma_out:
            if self.fence_dma.get(t[0], 0) < t[1]:
                self.fence_dma[t[0]] = t[1]
        self.dma_out = []
        toks = list(self.fence_dma.items())
        for e in self.eng.values():
            if e.pending:
                raise RuntimeError(f"engine {e.name} pending at soft barrier")
            if e.count > 0:
                toks.append((e.semkey(e.count), e.count))
        self.fence_toks = toks

    def barrier(self):
        if self.dry:
            return
        assert self.region is None
        toks = list(self.dma_out)
        self.dma_out = []
        for e in self.eng.values():
            if e.pending:
                raise RuntimeError(f"engine {e.name} pending at barrier")
            if e.count > 0:
                toks.append((e.semkey(e.count), e.count))
        for e in self.eng.values():
            waits = self._need(e, toks)
            if waits:
                e.ops.append((waits, None, None, None))

    def emit(self, stack):
        nc = self.nc
        sems = {}
        print(f"[sched] {len(self.semkeys)} semaphores", flush=True)
        for i, k in enumerate(self.semkeys):
            sems[k] = stack.enter_context(nc.semaphore(f"s{i}"))
        for e in self.eng.values():
            if e.pending:
                raise RuntimeError(f"engine {e.name} ends pending")
        block = stack.enter_context(nc.Block())

        def runner(e):
            def body(eng):
                for waits, fn, inckey, dsem in e.ops:
                    for k, v in waits:
                        eng.wait_ge(sems[k], v)
                    if fn is None:
                        continue
                    ins = getattr(eng, fn[0])(*fn[1], **fn[2])
                    if inckey is not None:
                        ins.then_inc(sems[inckey], 1)
                    elif dsem is not None:
                        ins.then_inc(sems[dsem], 16)
            return body

        block.tensor(runner(self.eng["tensor"]))
        block.vector(runner(self.eng["vector"]))
        block.scalar(runner(self.eng["scalar"]))
        block.gpsimd(runner(self.eng["gpsimd"]))
        block.sync(runner(self.eng["sync"]))


class T:
    __slots__ = ("ap", "b")

    def __init__(self, ap, b):
        self.ap = ap
        self.b = b


class Arena:
    def __init__(self, S_, arena_ap):
        self.S = S_
        self.arena = arena_ap
        self.off = 0

    def reset(self, off=0):
        self.off = off

    def f32(self, name, n):
        a = self.arena[:, self.off:self.off + n]
        self.off += n
        assert self.off <= ARENA_WORDS, (name, self.off)
        return T(a, self.S.buf(name))

    def bf(self, name, n):
        w = (n + 1) // 2
        a = self.arena[:, self.off:self.off + w].bitcast(BF16)
        self.off += w
        assert self.off <= ARENA_WORDS, (name, self.off)
        return T(a, self.S.buf(name))


DEBUG_TAPS = {}
DEBUG_ON = set()


def build_program(depth=DEPTH):
    nc = bass.Bass("TRN2", target_bir_lowering=False)
    dr = {}
    dr["xT"] = nc.dram_tensor("xT", [KC, 128, S], F32, kind="ExternalInput").ap()
    dr["w_in"] = nc.dram_tensor("w_in", [depth, D, 8256], F32, kind="ExternalInput").ap()
    dr["pool_w"] = nc.dram_tensor("pool_w", [depth, 4, 256, 256], F32, kind="ExternalInput").ap()
    dr["w_ssd_proj"] = nc.dram_tensor("w_ssd_proj", [depth, 2048, D], F32, kind="ExternalInput").ap()
    dr["w_out"] = nc.dram_tensor("w_out", [depth, D, D], F32, kind="ExternalInput").ap()
    dr["w_up"] = nc.dram_tensor("w_up", [depth, D, 2 * DFF], F32, kind="ExternalInput").ap()
    dr["w_down"] = nc.dram_tensor("w_down", [depth, DFF, D], F32, kind="ExternalInput").ap()
    dr["pp"] = nc.dram_tensor("pp", [128, depth * NPP], F32, kind="ExternalInput").ap()
    dr["bc"] = nc.dram_tensor("bc", [depth, NBC], F32, kind="ExternalInput").ap()
    dr["cmask"] = nc.dram_tensor("cmask", [128, 5 * 128], F32, kind="ExternalInput").ap()
    dr["cid"] = nc.dram_tensor("cid", [128, 128], F32, kind="ExternalInput").ap()
    dr["ctm"] = nc.dram_tensor("ctm", [128, 16 * 128], F32, kind="ExternalInput").ap()
    dr["crc"] = nc.dram_tensor("crc", [1, 64], F32, kind="ExternalInput").ap()
    dr["cid8"] = nc.dram_tensor("cid8", [128, 1024], F32, kind="ExternalInput").ap()
    dr["out"] = nc.dram_tensor("out", [KC, 128, S], F32, kind="ExternalOutput").ap()
    dr["xres"] = nc.dram_tensor("xres", [KC, 128, S], F32, kind="Internal").ap()
    dr["yn"] = nc.dram_tensor("yn", [16, 128, S], BF16, kind="Internal").ap()
    dr["wob"] = nc.dram_tensor("wob", [D, D], BF16, kind="Internal").ap()
    dr["wdb"] = nc.dram_tensor("wdb", [DFF, D], BF16, kind="Internal").ap()

    with ExitStack() as st:
        plan = []
        _emit_all(nc, st, dr, depth, dry=True, plan=plan, alloc=None)
        alloc = {}
        S_ = _emit_all(nc, st, dr, depth, dry=False, plan=plan, alloc=alloc)
        S_.emit(st)
    return nc


class WMgr:
    def __init__(self, S_, slots, plan, dry):
        self.S = S_
        self.slots = slots
        self.plan = plan
        self.dry = dry
        self.i = 0
        self.issued = 0

    def _issue(self, idx):
        spec = self.plan[idx]
        slot = self.slots[idx % len(self.slots)]
        for item in spec:
            o0, shape3, src = item[0], item[1], item[2]
            eng = item[3] if len(item) > 3 else "gpsimd"
            n = shape3[0] * shape3[1]
            dst = slot.ap[:, o0:o0 + n].rearrange("p (a b) -> p a b", a=shape3[0])
            self.S.dma(eng, dst, src, writes=[slot.b], track=False)

    def load(self, spec):
        if self.dry:
            self.plan.append(spec)
            self.i += 1
            return self.slots[(self.i - 1) % len(self.slots)]
        idx = self.i
        self.i += 1
        while self.issued < min(idx + 2, len(self.plan)):
            self._issue(self.issued)
            self.issued += 1
        return self.slots[idx % len(self.slots)]


def _emit_all(nc, st, dr, depth, dry, plan, alloc):
    S_ = Sched(nc, dry=dry)
    if dry:
        class _Fake:
            def __getitem__(self, k):
                return self

            def rearrange(self, *a, **k):
                return self

            def bitcast(self, *a):
                return self

            def unsqueeze(self, *a):
                return self

            def to_broadcast(self, *a):
                return self
        fake = _Fake()

        def sbt(name, shape, dt):
            return fake

        def pst(name, shape, dt):
            return fake
    else:
        def sbt(name, shape, dt):
            return st.enter_context(nc.sbuf_tensor(name, shape, dt))

        def pst(name, shape, dt):
            return st.enter_context(nc.psum_tensor(name, shape, dt))

    op = S_.op
    dma = S_.dma

    def tap(name, ap, bufs, shape, dt):
        if name not in DEBUG_ON or dry:
            return
        d_ = nc.dram_tensor("dbg_" + name, list(shape), dt, kind="ExternalOutput").ap()
        DEBUG_TAPS[name] = d_
        dma("sync", d_, ap, reads=list(bufs), writes=[S_.buf("dbg_" + name)])

    CM = sbt("CM" + NONCE, [128, 5 * 128], F32)
    bCM = S_.buf("CM")
    LE, GE, GT_, LT_, ONES = (CM[:, i * 128:(i + 1) * 128] for i in range(5))
    IDF = sbt("IDF", [128, 128], F32)
    bIDF = S_.buf("IDF")
    IDB = sbt("IDB", [128, 128], BF16)
    bIDB = S_.buf("IDB")
    RB = sbt("RB", [128, 2048], F32R)
    bRB = S_.buf("RB")
    RB3 = RB[:, :].rearrange("p (h l) -> p h l", h=16)
    GLR = sbt("GLR", [128, 256], F32R)
    bGLR = S_.buf("GLR")
    RCN = sbt("RCN", [128, 64], F32)
    bRCN = S_.buf("RCN")
    PP = sbt("PP", [128, depth * NPP], F32)
    bPP = S_.buf("PP")
    BC = sbt("BC", [128, NBC_SB], F32)
    bNG = S_.buf("BCng")
    bBC = S_.buf("BC")
    XB = sbt("XB", [128, KC * S], BF16)
    XB3 = XB[:, :].rearrange("p (k t) -> p k t", k=KC)
    bXB = S_.bufs("XB", NT)
    slots = [T(sbt(f"WS{i}", [128, 4096], BF16)[:, :], S_.buf(f"WS{i}")) for i in range(3)]
    ARENA = sbt("ARENA", [128, ARENA_WORDS], F32)
    AR = Arena(S_, ARENA)
    PS = pst("PS", [128, 8 * 512], F32)
    bPS = S_.bufs("PSB", 8, excl=True)
    W = WMgr(S_, slots, plan, dry)

    def bank(i, n=1):
        return PS[:, i * 512:(i + n) * 512]

    dma("sync", CM[:, :], dr["cmask"], writes=[bCM])
    dma("sync", IDF[:, :], dr["cid"], writes=[bIDF])
    dma("gpsimd", IDB[:, :], dr["cid"], writes=[bIDB])
    dma("sync", RCN[:, :], dr["crc"].to_broadcast([128, 64]), writes=[bRCN])
    dma("sync", PP[:, :], dr["pp"], writes=[bPP])
    dma("gpsimd", GLR[:, :], dr["cmask"][:, 256:512], writes=[bGLR])
    for nt in range(NT):
        dma("gpsimd", XB3[:, :, nt * 512:(nt + 1) * 512],
            dr["xT"].rearrange("k p t -> p k t")[:, :, nt * 512:(nt + 1) * 512], writes=[bXB[nt]])

    def pcol(l, off):
        return PP[:, l * NPP + off: l * NPP + off + 1]

    def win_src(l, c0, ncols):
        return dr["w_in"][l].rearrange("(k p) n -> p k n", p=128)[:, :, c0:c0 + ncols]

    def proj_fm(slot, so, ncols_in_slot, cidx, psbanks, pbufs):
        w3 = slot.ap[:, so:so + KC * ncols_in_slot].rearrange("p (k n) -> p k n", k=KC)
        for nt in range(NT):
            for kc in range(KC):
                last = (nt == NT - 1 and kc == KC - 1)
                op("tensor",
                   lambda e, nt=nt, kc=kc: e.matmul(bank(psbanks + nt), w3[:, kc, cidx * 128:(cidx + 1) * 128],
                                                    XB3[:, kc, nt * 512:(nt + 1) * 512],
                                                    start=(kc == 0), stop=(kc == KC - 1)),
                   reads=[slot.b] + bXB, writes=pbufs, inc=last)

    xres_src = dr["xT"]
    xres_bufs = [S_.bufs(f"xT{i}_", 2) for i in range(NT)]
    bXRES = [S_.bufs(f"xres{i}_", 2) for i in range(NT)]
    bYN = S_.bufs("yn", 2)
    bWOB = S_.buf("wob")
    bWDB = S_.buf("wdb")
    bOUT = S_.bufs("out", 2)

    for l in range(depth):
        S_.soft_barrier()
        AR.reset()
        SCR1 = AR.f32("SCR1", 2052)
        PAD2 = AR.f32("PAD2", 2052)
        SCR2 = AR.f32("SCR2", 2048)
        SCR3 = AR.bf("SCR3", 2048)
        MB2 = AR.bf("MB2", 2048)
        XTOK = AR.bf("XTOK", 16 * 512)
        bXTOK = S_.bufs("XTOKc", 16)
        XTOK3 = XTOK.ap.rearrange("p (c n) -> p c n", c=16)
        BTOK = AR.bf("BTOK", 16 * 128)
        bBTOK = S_.bufs("BTOKc", 16)
        BTOK3 = BTOK.ap.rearrange("p (c n) -> p c n", c=16)
        BT = AR.bf("BT", 2048)
        CT = AR.bf("CT", 2048)
        HPREV = AR.bf("HPREV", 16 * 512)
        bHPREV = S_.bufs("HPREVc", 16)
        HPREV3 = HPREV.ap.rearrange("p (c n) -> p c n", c=16)
        XDTFs = [AR.bf(f"XDTF{i}", 512) for i in range(2)]
        XDTBs = [AR.bf(f"XDTB{i}", 512) for i in range(2)]
        XDD = AR.bf("XDD", 512)
        HS = AR.f32("HS", 512)
        HBb = AR.bf("HBb", 512)
        DTt = AR.f32("DT", 1024)
        ADT = AR.f32("ADT", 1024)
        ACUM = T(SCR2.ap[:, 0:1024], SCR2.b)
        EXPA = AR.f32("EXPA", 1024)
        DTDE = AR.f32("DTDE", 1024)
        DECC = AR.f32("DECC", 1024)
        EA = AR.f32("EA", 64)
        SFB = AR.f32("SFB", 256)
        acc2_off = AR.off
        T1 = AR.f32("T1", 512)
        T2 = AR.f32("T2", 512)
        YT = AR.f32("YT", 512)
        SZ = AR.f32("SZ", 512)
        ACC2ap = ARENA[:, acc2_off:acc2_off + 2048]
        ACC2b = [T1.b, T2.b, YT.b, SZ.b]
        VV = AR.f32("V", 512)
        V2 = AR.f32("V2", 512)
        JK = AR.bf("JK", 512) if JK_BF else AR.f32("JK", 512)
        SS = AR.f32("SS", 4)
        VN = AR.bf("VN", 512)
        YNT = [AR.bf(f"YNT{i}", 512) for i in range(2)]
        ID8 = AR.bf("ID8", 1024)
        DIg = AR.bf("DIg", 1024)
        DI3 = DIg.ap.rearrange("p (h l) -> p h l", h=8)

        def v3(ap, c):
            return ap[:, c * 64:(c + 1) * 64]

        def hd_bc(t, c, h0):
            return t.ap[:, c * 64 + h0: c * 64 + h0 + 8].unsqueeze(2).to_broadcast([128, 8, 64])

        dma("sync", BC[:, 0:BC_NG], dr["bc"][l:l + 1, 0:BC_NG].to_broadcast([128, BC_NG]), writes=[bBC])
        dma("gpsimd", ID8.ap, dr["cid8"], writes=[ID8.b])
        wdt = W.load([(0, (KC, 64), win_src(l, DT0, 64))])
        wdt3 = wdt.ap[:, 0:KC * 64].rearrange("p (k n) -> p k n", k=KC)
        DTP = PS[:, 4 * 512: 4 * 512 + 1024]
        ACP = PS[:, 6 * 512: 6 * 512 + 1024]
        ATP = PS[:, 2 * 512: 2 * 512 + 1024]
        for i in range(NCH):
            for kc in range(KC):
                op("tensor", lambda e, i=i, kc=kc: e.matmul(DTP[:, i * 64:(i + 1) * 64],
                                                          XB3[:, kc, i * 128:(i + 1) * 128], wdt3[:, kc, :],
                                                          start=(kc == 0), stop=(kc == KC - 1)),
                   reads=[wdt.b] + bXB, writes=[bPS[4], bPS[5]], inc=(kc == KC - 1 and i == NCH - 1))
        op("vector", lambda e: e.tensor_tensor(DTt.ap.rearrange("p (c h) -> p c h", c=16),
                                               DTP.rearrange("p (c h) -> p c h", c=16),
                                               BC[:, BC_DTB:BC_DTB + 64].unsqueeze(1).to_broadcast([128, 16, 64]),
                                               ALU.add),
           reads=[bPS[4], bPS[5], bBC], writes=[DTt.b])
        op("scalar", lambda e: e.activation(DTt.ap, DTt.ap, AF.Exp), reads=[DTt.b], writes=[DTt.b])
        op("scalar", lambda e: e.activation(DTt.ap, DTt.ap, AF.Ln, bias=1.0), reads=[DTt.b], writes=[DTt.b])
        op("scalar", lambda e: e.activation(EA.ap, BC[:, BC_ALOG:BC_ALOG + 64], AF.Exp), reads=[bBC], writes=[EA.b])
        op("vector", lambda e: e.scalar_tensor_tensor(ADT.ap.rearrange("p (c h) -> p c h", c=16),
                                                      DTt.ap.rearrange("p (c h) -> p c h", c=16), -1.0,
                                                      EA.ap.unsqueeze(1).to_broadcast([128, 16, 64]),
                                                      ALU.mult, ALU.mult),
           reads=[DTt.b, EA.b], writes=[ADT.b])
        for c in range(NCH):
            op("tensor", lambda e, c=c: e.matmul(ACP[:, c * 64: c * 64 + 32], LE, ADT.ap[:, c * 64: c * 64 + 32],
                                                 start=True, stop=True),
               reads=[ADT.b, bCM], writes=[bPS[6], bPS[7]], inc=False)
            op("tensor", lambda e, c=c: e.matmul(ACP[:, c * 64 + 32: c * 64 + 64], GE,
                                                 ADT.ap[:, c * 64 + 32: c * 64 + 64], start=True, stop=True),
               reads=[ADT.b, bCM], writes=[bPS[6], bPS[7]], inc=False)
            op("tensor", lambda e, c=c: e.matmul(ATP[:, c * 64: c * 64 + 64], ONES, ADT.ap[:, c * 64: c * 64 + 64],
                                                 start=True, stop=True),
               reads=[ADT.b, bCM], writes=[bPS[2], bPS[3]], inc=(c == NCH - 1))
        op("scalar", lambda e: e.copy(ACUM.ap, ACP), reads=[bPS[6], bPS[7]], writes=[ACUM.b])
        op("scalar", lambda e: e.activation(EXPA.ap, ACUM.ap, AF.Exp), reads=[ACUM.b], writes=[EXPA.b])
        op("scalar", lambda e: e.activation(DECC.ap, ATP, AF.Exp), reads=[bPS[2], bPS[3]], writes=[DECC.b])
        op("vector", lambda e: e.tensor_tensor(DTDE.ap, ATP, ACUM.ap, ALU.subtract),
           reads=[bPS[2], bPS[3], ACUM.b], writes=[DTDE.b])
        op("scalar", lambda e: e.activation(DTDE.ap, DTDE.ap, AF.Exp), reads=[DTDE.b], writes=[DTDE.b])
        op("vector", lambda e: e.tensor_tensor(DTDE.ap, DTDE.ap, DTt.ap, ALU.mult),
           reads=[DTDE.b, DTt.b], writes=[DTDE.b])
        op("gpsimd", lambda e: e.memset(SCR1.ap[:, 0:2], 0.0), writes=[SCR1.b])
        op("gpsimd", lambda e: e.memset(SCR1.ap[:, 2050:2052], 0.0), writes=[SCR1.b])
        op("gpsimd", lambda e: e.memset(PAD2.ap[:, 0:2], 0.0), writes=[PAD2.b])
        op("gpsimd", lambda e: e.memset(PAD2.ap[:, 2050:2052], 0.0), writes=[PAD2.b])

        for g in range(4):
            S_.begin_region()
            wx = W.load([(0, (KC, 512), win_src(l, XBC0 + 512 * g, 512))])
            wbc = W.load([(0, (KC, 128), win_src(l, XBC0 + 2048 + 128 * g, 128)),
                          (KC * 128, (KC, 128), win_src(l, XBC0 + 2560 + 128 * g, 128))])
            if l == 0 and g == 0:
                tap("wx", wx.ap, [wx.b], [128, 4096], BF16)
            if g > 0:
                op("gpsimd", lambda e: e.memset(PAD2.ap[:, 0:2], 0.0), writes=[PAD2.b])
            tpi = 0

            def projA(cc):
                if cc < 4:
                    proj_fm(wx, 0, 512, cc, 0, bPS[0:4])
                elif cc == 4:
                    proj_fm(wbc, 0, 128, 0, 0, bPS[0:4])
                else:
                    proj_fm(wbc, KC * 128, 128, 0, 0, bPS[0:4])

            projA(0)
            for cc in range(6):
                cch = (4 * g + cc) if cc < 4 else ((16 + g) if cc == 4 else (20 + g))
                pad = SCR1 if (cc % 2 == 0 or not DB_A) else PAD2
                acc_ap = SCR2.ap if (cc % 2 == 0 or not DB_A) else ACC2ap
                acc_b = [SCR2.b] if (cc % 2 == 0 or not DB_A) else ACC2b
                op("scalar", lambda e, pad=pad: e.copy(pad.ap[:, 2:2050], bank(0, 4)), reads=bPS[0:4], writes=[pad.b])
                if cc < 5:
                    projA(cc + 1)
                op("scalar", lambda e, cch=cch, pad=pad, acc_ap=acc_ap: e.activation(
                    acc_ap, pad.ap[:, 0:2048], AF.Identity, bias=pcol(l, PP_CB + cch), scale=pcol(l, PP_CW + cch * 5)),
                   reads=[pad.b, bPP], writes=acc_b)
                for k in range(1, 5):
                    op("vector", lambda e, k=k, cch=cch, pad=pad, acc_ap=acc_ap: e.scalar_tensor_tensor(
                        acc_ap, pad.ap[:, k:k + 2048], pcol(l, PP_CW + cch * 5 + k), acc_ap, ALU.mult, ALU.add),
                       reads=[pad.b, bPP] + acc_b, writes=acc_b)
                dst = (SCR3 if cc % 2 == 0 else MB2) if cc < 4 else (BT if cc == 4 else CT)
                op("scalar", lambda e, dst=dst, acc_ap=acc_ap: e.activation(dst.ap, acc_ap, AF.Silu),
                   reads=acc_b, writes=[dst.b])
                if l == 0 and g == 0 and cc == 0:
                    tap("xs0", SCR3.ap, [SCR3.b], [128, 2048], BF16)
                    tap("pad0", SCR1.ap, [SCR1.b], [128, 2052], F32)
                if cc < 5:
                    for q4 in range(4):
                        pb = 4 + (tpi % 2)
                        tpi += 1
                        TPv = bank(pb)[:, 0:256].bitcast(BF16)
                        for q in range(4):
                            i = q4 * 4 + q
                            op("tensor", lambda e, dst=dst, i=i, q=q, TPv=TPv: e.transpose(
                                TPv[:, q * 128:(q + 1) * 128], dst.ap[:, i * 128:(i + 1) * 128], IDB[:, :]),
                               reads=[dst.b, bIDB], writes=[bPS[pb]], inc=(q == 3))
                        if cc < 4:
                            op("scalar", lambda e, q4=q4, cc=cc, TPv=TPv: e.copy(
                                XTOK3[:, q4 * 4:q4 * 4 + 4, cc * 128:(cc + 1) * 128],
                                TPv.rearrange("p (a b) -> p a b", a=4)),
                               reads=[bPS[pb]], writes=bXTOK[q4 * 4:q4 * 4 + 4])
                        else:
                            op("scalar", lambda e, q4=q4, TPv=TPv: e.copy(
                                BTOK3[:, q4 * 4:q4 * 4 + 4, :], TPv.rearrange("p (a b) -> p a b", a=4)),
                               reads=[bPS[pb]], writes=bBTOK[q4 * 4:q4 * 4 + 4])

            wz = W.load([(0, (KC, 512), win_src(l, Z0 + 512 * g, 512))])
            wz3 = wz.ap[:, 0:KC * 512].rearrange("p (k n) -> p k n", k=KC)
            ko0, ko1 = 2 * g, 2 * g + 2
            dma("gpsimd", dr["wob"].rearrange("(k p) n -> p k n", p=128)[:, ko0:ko1, :],
                dr["w_out"][l].rearrange("(k p) n -> p k n", p=128)[:, ko0:ko1, :], writes=[bWOB])
            kd0, kd1 = 6 * g, min(6 * g + 6, FK)
            dma("gpsimd", dr["wdb"].rearrange("(k p) n -> p k n", p=128)[:, kd0:kd1, :],
                dr["w_down"][l].rearrange("(k p) n -> p k n", p=128)[:, kd0:kd1, :], writes=[bWDB])
            hf, hb = 8 * g, 32 + 8 * g
            R3 = SCR1.ap[:, 0:2048].rearrange("p (h l) -> p h l", h=16)
            Es = [SCR2, T(PAD2.ap[:, 0:2048], PAD2.b)]

            def state_update(c, h0):
                op("gpsimd", lambda e: e.tensor_tensor(XDD.ap.rearrange("p (h d) -> p h d", h=8),
                                                       XTOK3[:, c, :].rearrange("p (h d) -> p h d", h=8),
                                                       hd_bc(DTDE, c, h0), ALU.mult),
                   reads=[bXTOK[c], DTDE.b], writes=[XDD.b])
                op("tensor", lambda e: e.matmul(bank(5), BTOK3[:, c, :], XDD.ap, start=True, stop=True),
                   reads=[bBTOK[c], XDD.b], writes=[bPS[5]])
                op("vector", lambda e: e.tensor_tensor(HS.ap.rearrange("p (h d) -> p h d", h=8),
                                                       HS.ap.rearrange("p (h d) -> p h d", h=8),
                                                       hd_bc(DECC, c, h0), ALU.mult),
                   reads=[HS.b, DECC.b], writes=[HS.b])
                op("vector", lambda e: e.tensor_tensor(HS.ap, HS.ap, bank(5), ALU.add),
                   reads=[HS.b, bPS[5]], writes=[HS.b])

            dma("sync", BC[:, BC_NG:BC_NG + 512],
                dr["bc"][l:l + 1, BC_NG + 512 * g:BC_NG + 512 * (g + 1)].to_broadcast([128, 512]), writes=[bNG])
            op("gpsimd", lambda e: e.tensor_tensor(
                DI3, ID8.ap.rearrange("p (h l) -> p h l", h=8),
                BC[:, BC_DSK + hf: BC_DSK + hf + 8].unsqueeze(2).to_broadcast([128, 8, 128]), ALU.mult),
               reads=[ID8.b, bBC], writes=[DIg.b])
            op("gpsimd", lambda e: e.memset(HS.ap, 0.0), writes=[HS.b])
            for c in range(NCH):
                op("scalar", lambda e, c=c: e.copy(HPREV3[:, c, :], HS.ap), reads=[HS.b], writes=[bHPREV[c]])
                if c < NCH - 1:
                    state_update(c, hf)
            MBs = [SCR3, MB2]
            bSC = bPS[7]
            bTPY = bPS[7]

            def stage1(c, par):
                Mt = MBs[par]
                M3 = Mt.ap.rearrange("p (h l) -> p h l", h=16)
                xf, xb_ = XDTFs[par], XDTBs[par]
                E = Es[par]
                SC = bank(7)[:, 0:128]
                op("tensor", lambda e: e.matmul(SC, BT.ap[:, c * 128:(c + 1) * 128], CT.ap[:, c * 128:(c + 1) * 128],
                                                start=True, stop=True),
                   reads=[BT.b, CT.b], writes=[bSC])
                op("vector", lambda e: e.tensor_tensor(SFB.ap.rearrange("p (a l) -> p a l", a=2),
                                                       SC.unsqueeze(1).to_broadcast([128, 2, 128]),
                                                       CM[:, 0:256].rearrange("p (a l) -> p a l", a=2), ALU.mult),
                   reads=[bSC, bCM], writes=[SFB.b])
                op("gpsimd", lambda e: e.tensor_tensor(
                    RB3[:, 0:8, :], LE.unsqueeze(1).to_broadcast([128, 8, 128]),
                    ADT.ap[:, c * 64 + hf: c * 64 + hf + 8].unsqueeze(2).to_broadcast([128, 8, 128]), ALU.mult),
                   reads=[bCM, ADT.b], writes=[bRB])
                op("gpsimd", lambda e: e.tensor_tensor(
                    RB3[:, 8:16, :], GE.unsqueeze(1).to_broadcast([128, 8, 128]),
                    ADT.ap[:, c * 64 + hb: c * 64 + hb + 8].unsqueeze(2).to_broadcast([128, 8, 128]), ALU.mult),
                   reads=[bCM, ADT.b], writes=[bRB])
                op("gpsimd", lambda e: e.tensor_tensor(xf.ap.rearrange("p (h d) -> p h d", h=8),
                                                       XTOK3[:, c, :].rearrange("p (h d) -> p h d", h=8),
                                                       hd_bc(DTt, c, hf), ALU.mult),
                   reads=[bXTOK[c], DTt.b], writes=[xf.b])
                op("gpsimd", lambda e: e.tensor_tensor(xb_.ap.rearrange("p (h d) -> p h d", h=8),
                                                       XTOK3[:, c, :].rearrange("p (h d) -> p h d", h=8),
                                                       hd_bc(DTt, c, hb), ALU.mult),
                   reads=[bXTOK[c], DTt.b], writes=[xb_.b])
                for d_ in range(2):
                    lhs = GLR[:, 0:128] if d_ == 0 else GLR[:, 128:256]
                    for q in range(2):
                        op("tensor", lambda e, d_=d_, q=q, lhs=lhs: e.matmul(
                            bank(q), lhs, RB[:, d_ * 1024 + q * 512: d_ * 1024 + (q + 1) * 512],
                            start=True, stop=True),
                           reads=[bRB, bGLR], writes=[bPS[q]], inc=(q == 1))
                    op("scalar", lambda e, d_=d_: e.activation(E.ap[:, d_ * 1024:(d_ + 1) * 1024], bank(0, 2), AF.Exp),
                       reads=bPS[0:2], writes=[E.b])
                    op("vector", lambda e, d_=d_: e.tensor_tensor(
                        M3[:, d_ * 8:(d_ + 1) * 8, :],
                        E.ap[:, d_ * 1024:(d_ + 1) * 1024].rearrange("p (h l) -> p h l", h=8),
                        SFB.ap[:, d_ * 128:(d_ + 1) * 128].unsqueeze(1).to_broadcast([128, 8, 128]), ALU.mult),
                       reads=[E.b, SFB.b], writes=[Mt.b])

            def stage2(c, par, ci):
                Mt = MBs[par]
                M3 = Mt.ap.rearrange("p (h l) -> p h l", h=16)
                xf, xb_ = XDTFs[par], XDTBs[par]
                op("scalar", lambda e: e.copy(HBb.ap, HS.ap), reads=[HS.b], writes=[HBb.b])
                if c > 0:
                    state_update(c, hb)
                for h in range(8):
                    op("tensor", lambda e, h=h: e.matmul(bank(2)[:, h * 64:(h + 1) * 64], M3[:, h, :],
                                                         xf.ap[:, h * 64:(h + 1) * 64], start=True, stop=False),
                       reads=[Mt.b, xf.b], writes=[bPS[2]], inc=False)
                    op("tensor", lambda e, h=h: e.matmul(bank(2)[:, h * 64:(h + 1) * 64], M3[:, 8 + h, :],
                                                         xb_.ap[:, h * 64:(h + 1) * 64], start=False, stop=False),
                       reads=[Mt.b, xb_.b], writes=[bPS[2]], inc=False)
                    op("tensor", lambda e, h=h: e.matmul(bank(2)[:, h * 64:(h + 1) * 64], DI3[:, h, :],
                                                         XTOK3[:, c, h * 64:(h + 1) * 64], start=False, stop=True),
                       reads=[DIg.b, bXTOK[c]], writes=[bPS[2]], inc=(h == 7))
                op("tensor", lambda e: e.matmul(bank(3), CT.ap[:, c * 128:(c + 1) * 128], HPREV3[:, c, :],
                                                start=True, stop=True),
                   reads=[CT.b, bHPREV[c]], writes=[bPS[3]])
                op("tensor", lambda e: e.matmul(bank(4), CT.ap[:, c * 128:(c + 1) * 128], HBb.ap,
                                                start=True, stop=True),
                   reads=[CT.b, HBb.b], writes=[bPS[4]])
                for kc in range(KC):
                    op("tensor", lambda e, kc=kc: e.matmul(bank(6), XB3[:, kc, c * 128:(c + 1) * 128], wz3[:, kc, :],
                                                           start=(kc == 0), stop=(kc == KC - 1)),
                       reads=[wz.b] + bXB, writes=[bPS[6]], inc=(kc == KC - 1))
                op("vector", lambda e: e.tensor_tensor(T1.ap.rearrange("p (h d) -> p h d", h=8),
                                                       bank(3).rearrange("p (h d) -> p h d", h=8),
                                                       hd_bc(EXPA, c, hf), ALU.mult),
                   reads=[bPS[3], EXPA.b], writes=[T1.b])
                op("vector", lambda e: e.tensor_tensor(T2.ap.rearrange("p (h d) -> p h d", h=8),
                                                       bank(4).rearrange("p (h d) -> p h d", h=8),
                                                       hd_bc(EXPA, c, hb), ALU.mult),
                   reads=[bPS[4], EXPA.b], writes=[T2.b])
                op("gpsimd", lambda e: e.tensor_tensor(T1.ap, T1.ap, T2.ap, ALU.add), reads=[T1.b, T2.b], writes=[T1.b])
                op("vector", lambda e: e.tensor_tensor(YT.ap, T1.ap, bank(2), ALU.add), reads=[T1.b, bPS[2]], writes=[YT.b])
                op("scalar", lambda e: e.activation(SZ.ap, bank(6), AF.Silu), reads=[bPS[6]], writes=[SZ.b])
                op("vector", lambda e: e.tensor_tensor(VV.ap, YT.ap, SZ.ap, ALU.mult), reads=[YT.b, SZ.b], writes=[VV.b])
                op("gpsimd", lambda e: e.memset(SS.ap[:, 0:1], 0.0), writes=[SS.b])
                op("scalar", lambda e: e.activation(JK.ap, VV.ap, AF.Square, accum_out=SS.ap[:, 0:1]),
                   reads=[VV.b], writes=[JK.b, SS.b])
                op("scalar", lambda e: e.activation(SS.ap[:, 1:2], SS.ap[:, 0:1], AF.Ln, bias=RMS_EPS, scale=1.0 / 512),
                   reads=[SS.b], writes=[SS.b])
                op("scalar", lambda e: e.activation(SS.ap[:, 2:3], SS.ap[:, 1:2], AF.Exp, scale=-0.5),
                   reads=[SS.b], writes=[SS.b])
                op("scalar", lambda e: e.activation(V2.ap, VV.ap, AF.Copy, scale=SS.ap[:, 2:3]),
                   reads=[VV.b, SS.b], writes=[V2.b])
                op("vector", lambda e: e.tensor_tensor(VN.ap, V2.ap, BC[:, BC_NG: BC_NG + 512], ALU.mult),
                   reads=[V2.b, bNG], writes=[VN.b])
                TPY = bank(7)[:, 256:512].bitcast(BF16)
                for q in range(4):
                    op("tensor", lambda e, q=q: e.transpose(TPY[:, q * 128:(q + 1) * 128], VN.ap[:, q * 128:(q + 1) * 128],
                                                            IDB[:, :]),
                       reads=[VN.b, bIDB], writes=[bTPY], inc=(q == 3))
                ynt = YNT[ci % 2]
                op("scalar", lambda e: e.copy(ynt.ap, TPY), reads=[bTPY], writes=[ynt.b])
                dma("sync", dr["yn"][4 * g:4 * g + 4, :, c * 128:(c + 1) * 128].rearrange("q p t -> p q t"),
                    ynt.ap.rearrange("p (q t) -> p q t", q=4), reads=[ynt.b], writes=[bYN[ci % 2]], sembuf=ynt.b)

            op("gpsimd", lambda e: e.memset(HS.ap, 0.0), writes=[HS.b])
            if PIPELINE_B:
                stage1(NCH - 1, 0)
            for ci, c in enumerate(range(NCH - 1, -1, -1)):
                if PIPELINE_B:
                    if c > 0:
                        stage1(c - 1, (ci + 1) % 2)
                else:
                    stage1(c, ci % 2)
                stage2(c, ci % 2, ci)
            S_.end_region()

        S_.soft_barrier()
        AR.reset()
        MRG = AR.bf("MRG", KC * S)
        MRG3 = MRG.ap.rearrange("p (k t) -> p k t", k=KC)
        bMRG = S_.bufs("MRGj", KC)
        mrg_end = AR.off
        YN = AR.bf("YN", 16 * S)
        YN3 = YN.ap.rearrange("p (k t) -> p k t", k=16)
        GCs = [AR.f32(f"GC{i}", 2048) for i in range(2)]
        S_.begin_region()
        bYNl = S_.bufs("YNl", 4)
        for k4 in range(4):
            dma("sync", YN3[:, 4 * k4:4 * k4 + 4, :], dr["yn"][4 * k4:4 * k4 + 4].rearrange("k p t -> p k t"),
                reads=bYN, writes=[bYNl[k4]])
        for j in range(KC):
            wp = W.load([(0, (16, 128), dr["w_ssd_proj"][l].rearrange("(k p) n -> p k n", p=128)[:, :, j * 128:(j + 1) * 128])])
            wp3 = wp.ap[:, 0:2048].rearrange("p (k n) -> p k n", k=16)
            wg = W.load([(0, (KC, 128), win_src(l, G0 + 1024 + j * 128, 128))])
            for nt in range(NT):
                for kc in range(16):
                    op("tensor", lambda e, nt=nt, kc=kc: e.matmul(bank(nt), wp3[:, kc, :], YN3[:, kc, nt * 512:(nt + 1) * 512],
                                                                  start=(kc == 0), stop=(kc == 15)),
                       reads=[wp.b, bYNl[kc // 4]], writes=bPS[0:4], inc=(nt == NT - 1 and kc == 15))
            G = GCs[j % 2]
            proj_fm(wg, 0, 128, 0, 4, bPS[4:8])
            op("scalar", lambda e: e.activation(G.ap, bank(4, 4), AF.Sigmoid), reads=bPS[4:8], writes=[G.b])
            op("vector", lambda e, j=j: e.tensor_tensor(MRG3[:, j, :], G.ap, bank(0, 4), ALU.mult),
               reads=[G.b] + bPS[0:4], writes=[bMRG[j]])
        S_.end_region()

        if l == 0:
            tap("mrgC", MRG.ap, bMRG, [128, KC * S], BF16)
            tap("yn", YN.ap, bYNl, [128, 16 * S], BF16)
        S_.soft_barrier()
        AR.reset(mrg_end)
        P0s = [AR.f32(f"P0{i}", 2064) for i in range(2)]
        Q1s = [AR.f32(f"Q1{i}", 2064) for i in range(2)]
        Q2s = [AR.f32(f"Q2{i}", 2064) for i in range(2)]
        PLD = [AR.bf(f"PLD{i}", 2048) for i in range(2)]
        Gs = [AR.f32(f"G{i}", 2048) for i in range(2)]
        TMPs = [AR.f32(f"TMP{i}", 2048) for i in range(2)]
        TEs = [AR.f32(f"TE{i}", 16) for i in range(2)]
        S_.begin_region()
        for P0 in P0s:
            op("gpsimd", lambda e: e.memset(P0.ap[:, 0:8], 0.0), writes=[P0.b])
            op("gpsimd", lambda e: e.memset(P0.ap[:, 2056:2064], 0.0), writes=[P0.b])
        for gi, w_ in enumerate(POOL_WINDOWS):
            half = w_ // 2
            wu = W.load([(0, (KC, 256), win_src(l, U0 + 256 * gi, 256))])
            for k2 in range(2):
                pb0 = 4 * (k2 % 2)
                P0, Q1, Q2, TE = P0s[k2], Q1s[k2], Q2s[k2], TEs[k2]
                proj_fm(wu, 0, 256, k2, pb0, bPS[pb0:pb0 + 4])
                op("scalar", lambda e, pb0=pb0: e.copy(P0.ap[:, 8:2056], bank(pb0, 4)), reads=bPS[pb0:pb0 + 4], writes=[P0.b])
                src, dst = P0, Q1
                sh = 1
                while sh < w_:
                    op("vector", lambda e, src=src, dst=dst, sh=sh: e.tensor_tensor(
                        dst.ap[:, sh:2064], src.ap[:, sh:2064], src.ap[:, 0:2064 - sh], ALU.add),
                       reads=[src.b], writes=[dst.b])
                    src = dst
                    dst = Q2 if dst is Q1 else Q1
                    sh *= 2
                o = 8 + half - 1
                pl = PLD[k2]
                op("vector", lambda e, src=src, o=o, pl=pl, w_=w_: e.scalar_tensor_tensor(
                    pl.ap, src.ap[:, o:o + 2048], 1.0 / w_, P0.ap[:, 8:2056], ALU.mult, ALU.subtract),
                   reads=[src.b, P0.b], writes=[pl.b])
                nl, nr = half, half - 1
                op("vector", lambda e, src=src, o=o, gi=gi, nl=nl: e.tensor_tensor(
                    TE.ap[:, 0:nl], src.ap[:, o:o + nl], RCN[:, gi * 16: gi * 16 + nl], ALU.mult),
                   reads=[src.b, bRCN], writes=[TE.b])
                op("vector", lambda e, pl=pl, nl=nl: e.tensor_tensor(pl.ap[:, 0:nl], TE.ap[:, 0:nl], P0.ap[:, 8:8 + nl],
                                                                      ALU.subtract),
                   reads=[TE.b, P0.b], writes=[pl.b])
                if nr > 0:
                    op("vector", lambda e, src=src, o=o, gi=gi, nr=nr: e.tensor_tensor(
                        TE.ap[:, 8:8 + nr], src.ap[:, o + 2048 - nr:o + 2048],
                        RCN[:, gi * 16 + 8: gi * 16 + 8 + nr], ALU.mult),
                       reads=[src.b, bRCN], writes=[TE.b])
                    op("vector", lambda e, pl=pl, nr=nr: e.tensor_tensor(
                        pl.ap[:, 2048 - nr:2048], TE.ap[:, 8:8 + nr], P0.ap[:, 8 + 2048 - nr:8 + 2048], ALU.subtract),
                       reads=[TE.b, P0.b], writes=[pl.b])
            wm = W.load([(0, (2, 256), dr["pool_w"][l, gi].rearrange("(k p) n -> p k n", p=128))])
            wm3 = wm.ap[:, 0:512].rearrange("p (k n) -> p k n", k=2)
            for jj in range(2):
                j = 2 * gi + jj
                for nt in range(NT):
                    for k2 in range(2):
                        op("tensor", lambda e, nt=nt, k2=k2, jj=jj: e.matmul(
                            bank(nt), wm3[:, k2, jj * 128:(jj + 1) * 128], PLD[k2].ap[:, nt * 512:(nt + 1) * 512],
                            start=(k2 == 0), stop=(k2 == 1)),
                           reads=[wm.b, PLD[0].b, PLD[1].b], writes=bPS[0:4], inc=(nt == NT - 1 and k2 == 1))
                wg = W.load([(0, (KC, 128), win_src(l, G0 + j * 128, 128))])
                G, TMP = Gs[jj], TMPs[jj]
                proj_fm(wg, 0, 128, 0, 4, bPS[4:8])
                op("scalar", lambda e: e.activation(G.ap, bank(4, 4), AF.Sigmoid), reads=bPS[4:8], writes=[G.b])
                op("vector", lambda e, j=j: e.scalar_tensor_tensor(TMP.ap, bank(0, 4), pcol(l, PP_PS + j), G.ap,
                                                                   ALU.mult, ALU.mult),
                   reads=bPS[0:4] + [G.b, bPP], writes=[TMP.b])
                op("gpsimd", lambda e, j=j: e.tensor_tensor(MRG3[:, j, :], TMP.ap, MRG3[:, j, :], ALU.add),
                   reads=[TMP.b, bMRG[j]], writes=[bMRG[j]])
        S_.end_region()

        def outproj_ln(rhs3, rhs_bufs, nK, wsrc, goff, boff, final):
            nonlocal xres_src, xres_bufs
            SUMt = AR.f32("SUMt", KC * 512)
            SUM3 = SUMt.ap.rearrange("p (k t) -> p k t", k=KC)
            XR = [AR.f32(f"XR{i}", 512) for i in range(2)]
            SQ = [AR.f32(f"SQ{i}", 512) for i in range(2)]
            MEAN = AR.f32("MEAN", 512)
            M2 = AR.f32("M2", 512)
            RSTD = AR.f32("RSTD", 512)
            TA = [AR.f32(f"TA{i}", 512) for i in range(2)]
            XN = [AR.f32(f"XN{i}", 512) for i in range(2)]
            for lst, nm in ((TA, "TA"), (XN, "XN"), (SQ, "SQ")):
                while len(lst) < 4 and ARENA_WORDS - AR.off >= 512 * 8:
                    lst.append(AR.f32(f"{nm}{len(lst)}", 512))
            while len(XR) < 8 and ARENA_WORDS - AR.off >= 512:
                XR.append(AR.f32(f"XR{len(XR)}", 512))
            nxr = len(XR)
            kgroups = [(k0, min(4, nK - k0)) for k0 in range(0, nK, 4)]
            out_dst = dr["out"] if final else dr["xres"]
            S_.begin_region()
            for nt in range(NT):
                for (k0, nk) in kgroups:
                    ws = W.load([(0, (nk, 1024), wsrc.rearrange("(k p) n -> p k n", p=128)[:, k0:k0 + nk, :], "sync")])
                    ws3 = ws.ap[:, 0:nk * 1024].rearrange("p (k n) -> p k n", k=nk)
                    for j in range(KC):
                        for kk in range(nk):
                            kc = k0 + kk
                            op("tensor", lambda e, j=j, kk=kk, kc=kc, ws3=ws3: e.matmul(
                                bank(j), ws3[:, kk, j * 128:(j + 1) * 128], rhs3[:, kc, nt * 512:(nt + 1) * 512],
                                start=(kc == 0), stop=(kc == nK - 1)),
                               reads=[ws.b] + rhs_bufs, writes=[bPS[j]], inc=(kk == nk - 1))
                for j in range(KC):
                    xr = XR[(nt * KC + j) % nxr]
                    dma("sync", xr.ap, xres_src[j, :, nt * 512:(nt + 1) * 512], reads=xres_bufs[nt], writes=[xr.b])
                    op("vector", lambda e, j=j, xr=xr: e.scalar_tensor_tensor(SUM3[:, j, :], xr.ap, float(ALPHA), bank(j),
                                                                              ALU.mult, ALU.add),
                       reads=[xr.b, bPS[j]], writes=[SUMt.b])
                for j in range(KC):
                    op("tensor", lambda e, j=j: e.matmul(bank(0), ONES, SUM3[:, j, :], start=(j == 0), stop=(j == KC - 1)),
                       reads=[SUMt.b, bCM], writes=[bPS[0]], inc=(j == KC - 1))
                for j in range(KC):
                    sq = SQ[j % len(SQ)]
                    op("scalar", lambda e, j=j, sq=sq: e.activation(sq.ap, SUM3[:, j, :], AF.Square),
                       reads=[SUMt.b], writes=[sq.b])
                    op("tensor", lambda e, j=j, sq=sq: e.matmul(bank(1), ONES, sq.ap, start=(j == 0), stop=(j == KC - 1)),
                       reads=[sq.b, bCM], writes=[bPS[1]])
                op("vector", lambda e: e.tensor_scalar(MEAN.ap, bank(0), 1.0 / D, None, ALU.mult),
                   reads=[bPS[0]], writes=[MEAN.b])
                op("vector", lambda e: e.tensor_tensor(M2.ap, MEAN.ap, MEAN.ap, ALU.mult), reads=[MEAN.b], writes=[M2.b])
                op("vector", lambda e: e.scalar_tensor_tensor(RSTD.ap, bank(1), 1.0 / D, M2.ap, ALU.mult, ALU.subtract),
                   reads=[bPS[1], M2.b], writes=[RSTD.b])
                op("scalar", lambda e: e.activation(RSTD.ap, RSTD.ap, AF.Ln, bias=LN_EPS), reads=[RSTD.b], writes=[RSTD.b])
                op("scalar", lambda e: e.activation(RSTD.ap, RSTD.ap, AF.Exp, scale=-0.5), reads=[RSTD.b], writes=[RSTD.b])
                for j in range(KC):
                    ta = TA[j % len(TA)]
                    xn = XN[j % len(XN)]
                    op("vector", lambda e, j=j, ta=ta: e.tensor_tensor(ta.ap, SUM3[:, j, :], MEAN.ap, ALU.subtract),
                       reads=[SUMt.b, MEAN.b], writes=[ta.b])
                    op("vector", lambda e, ta=ta: e.tensor_tensor(ta.ap, ta.ap, RSTD.ap, ALU.mult),
                       reads=[ta.b, RSTD.b], writes=[ta.b])
                    op("scalar", lambda e, j=j, ta=ta, xn=xn: e.activation(xn.ap, ta.ap, AF.Identity,
                                                                           bias=pcol(l, boff + j), scale=pcol(l, goff + j)),
                       reads=[ta.b, bPP], writes=[xn.b])
                    if not final:
                        op("scalar", lambda e, j=j, ta=ta: e.activation(XB3[:, j, nt * 512:(nt + 1) * 512], ta.ap, AF.Identity,
                                                                        bias=pcol(l, boff + j), scale=pcol(l, goff + j)),
                           reads=[ta.b, bPP], writes=[bXB[nt]])
                    dma("sync", out_dst[j, :, nt * 512:(nt + 1) * 512], xn.ap, reads=[xn.b],
                        writes=[(bOUT if final else bXRES[nt])[j % 2]], sembuf=xn.b)
            S_.end_region()
            if not final:
                xres_src = dr["xres"]
                xres_bufs = bXRES

        if l == 0:
            tap("mrgD", MRG.ap, bMRG, [128, KC * S], BF16)
        S_.soft_barrier()
        AR.reset(mrg_end)
        outproj_ln(MRG3, bMRG, KC, dr["wob"], PP_L1G, PP_L1B, final=False)
        if l == 0:
            tap("xb1", XB[:, :], bXB, [128, KC * S], BF16)

        S_.soft_barrier()
        AR.reset()
        HB = AR.bf("HB", FK * S)
        HB3 = HB.ap.rearrange("p (k t) -> p k t", k=FK)
        bHB = S_.bufs("HBk", FK)
        hb_end = AR.off
        PADFs = [AR.f32(f"PADF{i}", 2050) for i in range(2)]
        ACCF = AR.f32("ACCF", 2048)
        GTt = AR.f32("GT", 2048)
        for PADF in PADFs:
            op("gpsimd", lambda e: e.memset(PADF.ap[:, 0:1], 0.0), writes=[PADF.b])
            op("gpsimd", lambda e: e.memset(PADF.ap[:, 2049:2050], 0.0), writes=[PADF.b])
        S_.begin_region()
        for q in range(6):
            ncol = 512 if q < 5 else 256
            wgs = W.load([(0, (KC, ncol), dr["w_up"][l].rearrange("(k p) n -> p k n", p=128)[:, :, 512 * q:512 * q + ncol])])
            wvs = W.load([(0, (KC, ncol), dr["w_up"][l].rearrange("(k p) n -> p k n", p=128)[:, :, DFF + 512 * q:DFF + 512 * q + ncol])])
            for kk in range(ncol // 128):
                k = 4 * q + kk
                for half_, wsl in enumerate((wgs, wvs)):
                    pb0 = 4 * half_
                    PADF = PADFs[half_ if DB_F else 0]
                    cch = k + FK * half_
                    proj_fm(wsl, 0, ncol, kk, pb0, bPS[pb0:pb0 + 4])
                    op("scalar", lambda e, pb0=pb0: e.copy(PADF.ap[:, 1:2049], bank(pb0, 4)),
                       reads=bPS[pb0:pb0 + 4], writes=[PADF.b])
                    AC = GTt if half_ == 0 else ACCF
                    op("scalar", lambda e, cch=cch, AC=AC: e.activation(AC.ap, PADF.ap[:, 0:2048], AF.Identity,
                                                                        bias=pcol(l, PP_FB + cch), scale=pcol(l, PP_FW + cch * 3)),
                       reads=[PADF.b, bPP], writes=[AC.b])
                    for t_ in range(1, 3):
                        op("vector", lambda e, t_=t_, cch=cch, AC=AC: e.scalar_tensor_tensor(
                            AC.ap, PADF.ap[:, t_:t_ + 2048], pcol(l, PP_FW + cch * 3 + t_), AC.ap, ALU.mult, ALU.add),
                           reads=[PADF.b, AC.b, bPP], writes=[AC.b])
                    if half_ == 0:
                        op("scalar", lambda e: e.activation(GTt.ap, GTt.ap, AF.Gelu), reads=[GTt.b], writes=[GTt.b])
                    else:
                        op("vector", lambda e, k=k: e.tensor_tensor(HB3[:, k, :], GTt.ap, ACCF.ap, ALU.mult),
                           reads=[GTt.b, ACCF.b], writes=[bHB[k]])
        S_.end_region()

        if l == 0:
            tap("hb", HB.ap, bHB, [128, FK * S], BF16)
        S_.soft_barrier()
        AR.reset(hb_end)
        outproj_ln(HB3, bHB, FK, dr["wdb"], PP_L2G, PP_L2B, final=(l == depth - 1))

    S_.barrier()
    return S_


def _consts():
    r = np.arange(128)[:, None]
    c = np.arange(128)[None, :]
    le = (r <= c).astype(np.float32)
    ge = (r >= c).astype(np.float32)
    gt = (r > c).astype(np.float32)
    lt = (r < c).astype(np.float32)
    ones = np.ones((128, 128), np.float32)
    cmask = np.concatenate([le, ge, gt, lt, ones], axis=1)
    cid = np.eye(128, dtype=np.float32)
    ctm = np.concatenate([le] * 8 + [ge] * 8, axis=1)
    crc = np.zeros((1, 64), np.float32)
    t = np.arange(S)
    for gi, w in enumerate(POOL_WINDOWS):
        half = w // 2
        cnt = np.minimum(t + half - 1, S - 1) - np.maximum(t - half, 0) + 1
        rc = (1.0 / cnt).astype(np.float32)
        crc[0, gi * 16: gi * 16 + half] = rc[:half]
        if half > 1:
            crc[0, gi * 16 + 8: gi * 16 + 8 + half - 1] = rc[S - (half - 1):]
    return cmask, cid, ctm, crc


def _pack_params(inp, depth):
    pp = np.zeros((128, depth, NPP), np.float32)
    bc = np.zeros((depth, NBC), np.float32)
    for l in range(depth):
        pp[:, l, PP_CW:PP_CW + 120] = inp["ssd_conv_w"][l].T.reshape(24, 128, 5).transpose(1, 0, 2).reshape(128, 120)
        pp[:, l, PP_CB:PP_CB + 24] = inp["ssd_conv_b"][l].reshape(24, 128).T
        pp[:, l, PP_FW:PP_FW + 132] = inp["ffn_conv_w"][l].T.reshape(44, 128, 3).transpose(1, 0, 2).reshape(128, 132)
        pp[:, l, PP_FB:PP_FB + 44] = inp["ffn_conv_b"][l].reshape(44, 128).T
        pp[:, l, PP_PS:PP_PS + 8] = inp["pool_scale"][l].reshape(8, 128).T
        pp[:, l, PP_L1G:PP_L1G + 8] = inp["ln1_g"][l].reshape(8, 128).T
        pp[:, l, PP_L1B:PP_L1B + 8] = inp["ln1_b"][l].reshape(8, 128).T
        pp[:, l, PP_L2G:PP_L2G + 8] = inp["ln2_g"][l].reshape(8, 128).T
        pp[:, l, PP_L2B:PP_L2B + 8] = inp["ln2_b"][l].reshape(8, 128).T
        bc[l, BC_ALOG:BC_ALOG + 64] = inp["a_log"][l].reshape(64)
        bc[l, BC_DTB:BC_DTB + 64] = inp["dt_bias"][l].reshape(64)
        bc[l, BC_DSK:BC_DSK + 32] = inp["d_skip"][l]
        bc[l, BC_NG:BC_NG + 2048] = inp["ssd_norm_g"][l]
    return np.ascontiguousarray(pp.reshape(128, depth * NPP)), bc


def run(inputs, depth=DEPTH, n_cores=8, trace=False):
    inp = {k: np.asarray(v, dtype=np.float32) for k, v in inputs.items()}
    x = inp["x"]
    nb = x.shape[0]
    cmask, cid, ctm, crc = _consts()
    pp, bc = _pack_params(inp, depth)
    shared = {
        "w_in": np.ascontiguousarray(inp["w_in"][:depth]),
        "pool_w": np.ascontiguousarray(inp["pool_w"][:depth]),
        "w_ssd_proj": np.ascontiguousarray(inp["w_ssd_proj"][:depth]),
        "w_out": np.ascontiguousarray(inp["w_out"][:depth]),
        "w_up": np.ascontiguousarray(inp["w_up"][:depth]),
        "w_down": np.ascontiguousarray(inp["w_down"][:depth]),
        "pp": pp, "bc": bc, "cmask": cmask, "cid": cid, "ctm": ctm, "crc": crc,
        "cid8": np.ascontiguousarray(np.tile(cid, (1, 8))),
    }
    in_maps = []
    for b in range(nb):
        m = dict(shared)
        m["xT"] = np.ascontiguousarray(x[b].T).reshape(KC, 128, S)
        in_maps.append(m)
    nc = build_program(depth)
    res = run_bass_kernel_spmd(nc, in_maps, core_ids=list(range(nb)), trace=trace)
    global LAST_DBG
    LAST_DBG = {k: np.asarray(res.results[0]["dbg_" + k]) for k in DEBUG_TAPS}
    outs = [np.asarray(r["out"]).reshape(D, S).T for r in res.results]
    return np.ascontiguousarray(np.stack(outs, axis=0).astype(np.float32)), res


def kernel(**inputs):
    out, _ = run(inputs, depth=DEPTH)
    return out
```
